# Optimizing a Trainium2 kernel written in Bass

```python
import math
import jax, jax.numpy as jnp
from jax import lax
import numpy as np

D_MODEL = 1024
BATCH = 8
SEQ = 2048
DEPTH = 4

GRID_W = 64
CTX_LEN = 256
N_MIXERS = 2
BRANCH = D_MODEL
GLA_HEADS = 4
GLA_DK = D_MODEL // 2 // GLA_HEADS
GLA_DV = BRANCH // GLA_HEADS
GLA_KEY_W = GLA_HEADS * GLA_DK
GLA_GATE_RANK = 16
GLA_TAU = 16.0
GLA_CHUNK = 64
NA_HEADS = 16
NA_DH = BRANCH // NA_HEADS
NA_KH = 8
NA_KW = 16
ROPE_BASE = 10000.0
LN_EPS = 1e-5
NORM_EPS = 1e-6
ALPHA = (2 * DEPTH) ** 0.25
BETA = (8 * DEPTH) ** -0.25
N_GLA_LAYERS = (DEPTH + 1) // N_MIXERS
N_NA_LAYERS = DEPTH // N_MIXERS

kernel_name = 'hybrid_gla_natten_prefix_dit'


def layer_norm(x, g, b):
    xf = x.astype(jnp.float32)
    mu = jnp.mean(xf, -1, keepdims=True)
    var = jnp.mean(jnp.square(xf - mu), -1, keepdims=True)
    y = (xf - mu) * lax.rsqrt(var + LN_EPS)
    return (y * g + b).astype(x.dtype)


def adaln(cond, w, b):
    m = jax.nn.silu(cond) @ w + b
    return jnp.split(m, 3, axis=-1)


def to_heads(t, n_heads):
    B, L, _ = t.shape
    return t.reshape(B, L, n_heads, -1).transpose(0, 2, 1, 3)


def from_heads(t):
    B, H, L, d = t.shape
    return t.transpose(0, 2, 1, 3).reshape(B, L, H * d)


def axial_rope_tables(n_tokens, dim):
    pos = jnp.arange(n_tokens, dtype=jnp.int32)
    row = (pos // GRID_W).astype(jnp.float32)
    col = (pos % GRID_W).astype(jnp.float32)
    quarter = dim // 4
    inv = ROPE_BASE ** (-jnp.arange(quarter, dtype=jnp.float32) / quarter)
    ang = jnp.concatenate([row[:, None] * inv, col[:, None] * inv], -1)
    return jnp.cos(ang), jnp.sin(ang)


def apply_rope(x, cos, sin):
    x1, x2 = jnp.split(x, 2, axis=-1)
    cos = cos.astype(x.dtype)
    sin = sin.astype(x.dtype)
    return jnp.concatenate([x1 * cos - x2 * sin, x1 * sin + x2 * cos], -1)


def gla_chunked(q, k, v, log_a, s0, strict):
    B, H, L, dk = q.shape
    dv = v.shape[-1]
    C = GLA_CHUNK
    n = L // C
    qc = q.reshape(B, H, n, C, dk)
    kc = k.reshape(B, H, n, C, dk)
    vc = v.reshape(B, H, n, C, dv)
    la = log_a.astype(jnp.float32).reshape(B, H, n, C, dk)
    b = jnp.cumsum(la, axis=3)
    b_last = b[:, :, :, -1:, :]
    qb = b - la if strict else b
    q_t = qc * jnp.exp(qb).astype(q.dtype)
    k_t = kc * jnp.exp(-b).astype(q.dtype)
    idx = jnp.arange(C)
    mask = (idx[None, :] < idx[:, None]) if strict else (idx[None, :] <= idx[:, None])
    att = jnp.where(mask, jnp.einsum('bhnqd,bhnsd->bhnqs', q_t, k_t), 0)
    o_intra = jnp.einsum('bhnqs,bhnse->bhnqe', att, vc)
    k_end = kc * jnp.exp(b_last - b).astype(q.dtype)
    kv = jnp.einsum('bhnsd,bhnse->bhnde', k_end, vc)
    dec = jnp.exp(b_last[:, :, :, 0, :])

    def step(s, inp):
        d_n, kv_n = inp
        return d_n[..., None] * s + kv_n, s

    s_fin, s_prev = lax.scan(step, s0, (jnp.moveaxis(dec, 2, 0), jnp.moveaxis(kv, 2, 0)))
    s_prev = jnp.moveaxis(s_prev, 0, 2)
    o_inter = jnp.einsum('bhnqd,bhnde->bhnqe', q_t, s_prev)
    o = (o_intra + o_inter).reshape(B, H, L, dv).astype(v.dtype)
    return o, s_fin


def gla_bidir(q, k, v, la_f, la_b, s0_f, s0_b):
    o_f, s_f = gla_chunked(q, k, v, la_f, s0_f, strict=False)
    flip = lambda t: jnp.flip(t, axis=2)
    o_b, s_b = gla_chunked(flip(q), flip(k), flip(v), flip(la_b), s0_b, strict=True)
    return o_f + flip(o_b), s_f, s_b


def gla_project(h, w_in, dec_w1, dec_w2, dec_b, rope):
    p = h @ w_in
    q, k, v, g = jnp.split(p, [GLA_KEY_W, 2 * GLA_KEY_W, 2 * GLA_KEY_W + BRANCH], axis=-1)
    q = to_heads(q, GLA_HEADS) * GLA_DK ** -0.5
    k = to_heads(k, GLA_HEADS)
    if rope is not None:
        cos, sin = rope
        q = apply_rope(q, cos, sin)
        k = apply_rope(k, cos, sin)
    v = to_heads(v, GLA_HEADS)
    low = jnp.einsum('bld,zdr->zblr', h, dec_w1)
    logits = jnp.einsum('zblr,zrk->zblk', low, dec_w2) + dec_b[:, None, None, :]
    log_a = jax.nn.log_sigmoid(logits.astype(jnp.float32)) / GLA_TAU
    return q, k, v, to_heads(log_a[0], GLA_HEADS), to_heads(log_a[1], GLA_HEADS), g


def gla_merge(o, g, norm_g, w_out):
    of = o.astype(jnp.float32)
    of = of * lax.rsqrt(jnp.mean(of * of, -1, keepdims=True) + NORM_EPS) * norm_g
    of = from_heads(of).astype(g.dtype)
    return (of * jax.nn.silu(g)) @ w_out


def gla_layer(h_lat, h_ctx, w_in, dec_w1, dec_w2, dec_b, norm_g, w_out, rope, need_ctx):
    qc, kc, vc, lfc, lbc, gc = gla_project(h_ctx, w_in, dec_w1, dec_w2, dec_b, None)
    ql, kl, vl, lfl, lbl, gl = gla_project(h_lat, w_in, dec_w1, dec_w2, dec_b, rope)
    s0 = jnp.zeros((h_lat.shape[0], GLA_HEADS, GLA_DK, GLA_DV), jnp.float32)
    o_c, s_f, s_b = gla_bidir(qc, kc, vc, lfc, lbc, s0, s0)
    o_l, _, _ = gla_bidir(ql, kl, vl, lfl, lbl, s_f, s_b)
    y_lat = gla_merge(o_l, gl, norm_g, w_out)
    y_ctx = gla_merge(o_c, gc, norm_g, w_out) if need_ctx else None
    return y_lat, y_ctx


def na_latent(q, k, v, kc, vc, rpb):
    B, H, S, dh = q.shape
    rows = S // GRID_W
    kh = min(NA_KH, rows)
    kw = NA_KW
    col = jnp.arange(GRID_W)
    col_start = jnp.clip(col - kw // 2, 0, GRID_W - kw)
    key_col = col_start[:, None] + jnp.arange(kw)[None, :]
    band_idx = (jnp.arange(kh)[None, :, None] * GRID_W + key_col[:, None, :]).reshape(GRID_W, kh * kw)
    col_off = key_col - col[:, None] + (NA_KW - 1)
    q_rows = jnp.moveaxis(q.reshape(B, H, rows, GRID_W, dh), 2, 0)

    def one_row(inp):
        r, q_r = inp
        r_start = jnp.clip(r - kh // 2, 0, rows - kh)
        k_band = lax.dynamic_slice_in_dim(k, r_start * GRID_W, kh * GRID_W, axis=2)
        v_band = lax.dynamic_slice_in_dim(v, r_start * GRID_W, kh * GRID_W, axis=2)
        k_g = k_band[:, :, band_idx]
        v_g = v_band[:, :, band_idx]
        row_off = r_start + jnp.arange(kh) - r + (NA_KH - 1)
        bias = rpb[:, row_off[None, :, None], col_off[:, None, :]].reshape(H, GRID_W, kh * kw)
        s_loc = jnp.einsum('bhqd,bhqkd->bhqk', q_r, k_g) + bias
        s_ctx = jnp.einsum('bhqd,bhcd->bhqc', q_r, kc)
        p = jax.nn.softmax(jnp.concatenate([s_loc, s_ctx], -1).astype(jnp.float32), axis=-1).astype(v.dtype)
        return (jnp.einsum('bhqk,bhqkd->bhqd', p[..., :kh * kw], v_g)
                + jnp.einsum('bhqc,bhcd->bhqd', p[..., kh * kw:], vc))

    o = lax.map(one_row, (jnp.arange(rows), q_rows))
    return jnp.moveaxis(o, 0, 2).reshape(B, H, S, dh)


def na_merge(o, g, w_out):
    return (from_heads(o) * jax.nn.silu(g)) @ w_out


def na_layer(h_lat, h_ctx, w_in, rpb, w_out, need_ctx):
    q, k, v, g = jnp.split(h_lat @ w_in, 4, axis=-1)
    q = to_heads(q, NA_HEADS) * NA_DH ** -0.5
    k = to_heads(k, NA_HEADS)
    v = to_heads(v, NA_HEADS)
    if need_ctx:
        qc, kc, vc, gc = jnp.split(h_ctx @ w_in, 4, axis=-1)
    else:
        kc, vc = jnp.split(h_ctx @ w_in[:, BRANCH:3 * BRANCH], 2, axis=-1)
    kc = to_heads(kc, NA_HEADS)
    vc = to_heads(vc, NA_HEADS)
    y_lat = na_merge(na_latent(q, k, v, kc, vc, rpb), g, w_out)
    y_ctx = None
    if need_ctx:
        qc = to_heads(qc, NA_HEADS) * NA_DH ** -0.5
        p = jax.nn.softmax(jnp.einsum('bhqd,bhkd->bhqk', qc, kc).astype(jnp.float32), axis=-1).astype(vc.dtype)
        y_ctx = na_merge(jnp.einsum('bhqk,bhkd->bhqd', p, vc), gc, w_out)
    return y_lat, y_ctx


def setup_inputs(seed: int = 0) -> dict:
    key = jax.random.key(seed)
    ks = jax.random.split(key, 17)
    nrm = lambda k, shape, s: jax.random.normal(k, shape, jnp.float32) * s
    D = D_MODEL
    return {
        'x': nrm(ks[0], (BATCH, SEQ, D), 1.0),
        'c': nrm(ks[1], (BATCH, D), 1.0),
        'ctx': nrm(ks[2], (BATCH, CTX_LEN, D), 1.0),
        'c_ctx': nrm(ks[3], (D,), 1.0),
        'ada_w': nrm(ks[4], (DEPTH, D, 3 * D), 0.5 * D ** -0.5),
        'ada_b': nrm(ks[5], (DEPTH, 3 * D), 0.02),
        'ln_g': 1.0 + nrm(ks[6], (DEPTH, D), 0.02),
        'ln_b': nrm(ks[7], (DEPTH, D), 0.02),
        'w_out': nrm(ks[8], (DEPTH, BRANCH, D), BETA * BRANCH ** -0.5),
        'gla_w_in': nrm(ks[9], (N_GLA_LAYERS, D, 2 * GLA_KEY_W + 2 * BRANCH), D ** -0.5),
        'gla_dec_w1': nrm(ks[10], (N_GLA_LAYERS, 2, D, GLA_GATE_RANK), D ** -0.5),
        'gla_dec_w2': nrm(ks[11], (N_GLA_LAYERS, 2, GLA_GATE_RANK, GLA_KEY_W), GLA_GATE_RANK ** -0.5),
        'gla_dec_b': nrm(ks[12], (N_GLA_LAYERS, 2, GLA_KEY_W), 0.1),
        'gla_norm_g': 1.0 + nrm(ks[13], (N_GLA_LAYERS, GLA_DV), 0.02),
        'na_w_in': nrm(ks[14], (N_NA_LAYERS, D, 4 * BRANCH), D ** -0.5),
        'na_rpb': nrm(ks[15], (N_NA_LAYERS, NA_HEADS, 2 * NA_KH - 1, 2 * NA_KW - 1), 0.1),
    }


def reference(x, c, ctx, c_ctx, ada_w, ada_b, ln_g, ln_b, w_out, gla_w_in, gla_dec_w1,
              gla_dec_w2, gla_dec_b, gla_norm_g, na_w_in, na_rpb):
    S = x.shape[1]
    rope = axial_rope_tables(S, GLA_DK)
    cond_lat = c[:, None, :]
    cond_ctx = c_ctx[None, None, :]
    for i in range(DEPTH):
        need_ctx = i < DEPTH - 1
        j = i // N_MIXERS
        sh_l, sc_l, gt_l = adaln(cond_lat, ada_w[i], ada_b[i])
        sh_c, sc_c, gt_c = adaln(cond_ctx, ada_w[i], ada_b[i])
        h_lat = x * (1 + sc_l) + sh_l
        h_ctx = ctx * (1 + sc_c) + sh_c
        if i % N_MIXERS == 0:
            y_lat, y_ctx = gla_layer(h_lat, h_ctx, gla_w_in[j], gla_dec_w1[j], gla_dec_w2[j],
                                     gla_dec_b[j], gla_norm_g[j], w_out[i], rope, need_ctx)
        else:
            y_lat, y_ctx = na_layer(h_lat, h_ctx, na_w_in[j], na_rpb[j], w_out[i], need_ctx)
        x = layer_norm(ALPHA * x + gt_l * y_lat, ln_g[i], ln_b[i])
        if need_ctx:
            ctx = layer_norm(ALPHA * ctx + gt_c * y_ctx, ln_g[i], ln_b[i])
    return x
```

```python
from contextlib import ExitStack
import os

import numpy as np
import concourse.bass as bass
import concourse.mybir as mybir
from concourse.bass_utils import run_bass_kernel_spmd

F32 = mybir.dt.float32
BF16 = mybir.dt.bfloat16
AF = mybir.ActivationFunctionType
ALU = mybir.AluOpType


class _Op:
    __slots__ = ("eng", "fn", "deps", "is_dma", "semkey", "sig", "val", "n")

    def __init__(self, eng, fn, is_dma, semkey):
        self.eng, self.fn, self.is_dma, self.semkey = eng, fn, is_dma, semkey
        self.deps, self.sig, self.val, self.n = (), False, 0, 0


class _Ent:
    __slots__ = ("w", "r")

    def __init__(self):
        self.w, self.r = None, {}


class Prog:
    ENG = ("pe", "act", "dve", "pool", "sp")

    def __init__(self, nc):
        self.nc = nc
        self.stack = ExitStack()
        self.streams = {e: [] for e in self.ENG}
        self.res = {}
        self.nops = 0
        self.dma_last = {}

    def sb(self, name, shape, dtype):
        return self.stack.enter_context(self.nc.sbuf_tensor(name, list(shape), dtype))

    def ps(self, name):
        return self.stack.enter_context(self.nc.psum_tensor(name, [128, 512], F32))

    @staticmethod
    def _ck(o):
        return o.semkey if o.is_dma else o.eng

    def _conf(self, key):
        b = self.res.get(key[0])
        if not b:
            return
        n = len(key)
        for k, ent in b.items():
            m = len(k)
            if (k[:n] == key) if m >= n else (key[:m] == k):
                yield k, ent

    def _ent(self, key):
        b = self.res.setdefault(key[0], {})
        e = b.get(key)
        if e is None:
            e = b[key] = _Ent()
        return e

    def _record(self, o, reads, writes):
        deps = {}

        def add(d):
            if d is None or d is o:
                return
            if o.eng == "pe" and d.eng == "pe" and not d.is_dma:
                return
            if d.is_dma:
                d = self.dma_last[d.semkey]
            ck = self._ck(d)
            p = deps.get(ck)
            if p is None or d.n > p.n:
                deps[ck] = d

        for k in reads:
            for _, ent in self._conf(k):
                add(ent.w)
        for k in writes:
            for _, ent in self._conf(k):
                add(ent.w)
                for r in ent.r.values():
                    add(r)
        for k in reads:
            self._ent(k).r[self._ck(o)] = o
        for k in writes:
            n = len(k)
            b = self.res.setdefault(k[0], {})
            for kk in [kk for kk in b if len(kk) > n and kk[:n] == k]:
                del b[kk]
            e = self._ent(k)
            e.w, e.r = o, {}
        o.deps = tuple(deps.values())
        self.nops += 1
        o.n = self.nops
        self.streams[o.eng].append(o)
        return o

    def op(self, eng, fn, reads=(), writes=()):
        return self._record(_Op(eng, fn, False, None), reads, writes)

    def dma(self, q, out, in_, reads=(), writes=(), semkey=None):
        if semkey is None:
            semkey = writes[0]
        fn = lambda e: e.dma_start(out=out, in_=in_)
        o = self._record(_Op(q, fn, True, ("dma",) + tuple(semkey)), reads, writes)
        self.dma_last[o.semkey] = o
        return o

    def finish(self):
        nc = self.nc
        for s in self.streams.values():
            for o in s:
                for d in o.deps:
                    d.sig = True
        semnames = {}
        for e in self.ENG:
            cnt = 0
            for o in self.streams[e]:
                if not o.is_dma and o.sig:
                    cnt += 1
                    o.val = cnt
            semnames[e] = None
        dcnt = {}
        for e in self.ENG:
            for o in self.streams[e]:
                if o.is_dma:
                    dcnt[o.semkey] = dcnt.get(o.semkey, 0) + 16
                    o.val = dcnt[o.semkey]
                    semnames[o.semkey] = None
        sems = {}
        for i, k in enumerate(semnames):
            sems[k] = self.stack.enter_context(nc.semaphore("s%d" % i))
        self.n_sems = len(sems)
        ck = self._ck

        def emit(name, eng):
            seen = {}
            for o in self.streams[name]:
                for d in o.deps:
                    k = ck(d)
                    if seen.get(k, 0) < d.val:
                        eng.wait_ge(sems[k], d.val)
                        seen[k] = d.val
                if o.fn is None:
                    continue
                ins = o.fn(eng)
                if o.is_dma:
                    ins.then_inc(sems[o.semkey], 16)
                elif o.sig:
                    ins.then_inc(sems[name], 1)
            if name == "sp":
                for k, v in dcnt.items():
                    if seen.get(k, 0) < v:
                        eng.wait_ge(sems[k], v)

        with nc.Block() as block:
            @block.tensor
            def _(e):
                emit("pe", e)

            @block.scalar
            def _(e):
                emit("act", e)

            @block.vector
            def _(e):
                emit("dve", e)

            @block.gpsimd
            def _(e):
                emit("pool", e)

            @block.sync
            def _(e):
                emit("sp", e)
        self.stack.close()


D = 1024
KC = 8
CTXN = 256
GRID_W = 64
NA_KH, NA_KW = 8, 16
GLA_DK, GLA_DV, GLA_H = 128, 256, 4
NA_DH, NA_H = 64, 16
DEPTH = 4
ALPHA = (2 * DEPTH) ** 0.25
LN_EPS = 1e-5
NORM_EPS = 1e-6
NEG = -30000.0


def na_plan(H):
    rs = lambda r: min(max(r - NA_KH // 2, 0), H - NA_KH)
    tiles = []
    for t in range(H // 2):
        rows = [r for r in range(H) if any(rs(r) <= 2 * t + a < rs(r) + NA_KH for a in (0, 1))]
        assert rows == list(range(rows[0], rows[-1] + 1))
        sig = [(r - 2 * t, rs(r) <= 2 * t < rs(r) + NA_KH, rs(r) <= 2 * t + 1 < rs(r) + NA_KH) for r in rows]
        tiles.append((rows[0], sig))
    table, offs = [], {}
    for t in sorted(range(len(tiles)), key=lambda t: -len(tiles[t][1])):
        sig = tiles[t][1]
        off = None
        for o in range(0, len(table) - len(sig) + 1):
            if table[o:o + len(sig)] == sig:
                off = o
                break
        if off is None:
            off = len(table)
            table.extend(sig)
        offs[t] = off
    return tiles, table, offs


def make_consts(H):
    LAT = H * GRID_W
    idx = np.arange(128)
    s, t = idx[:, None], idx[None, :]
    same = (s // 64) == (t // 64)
    cst = np.zeros((128, 1152), np.float32)
    cst[:, 0:128] = np.eye(128, dtype=np.float32)
    sc = np.float32(-1.0 / 16.0)
    cst[:, 128:256] = (same & (s <= t)) * sc
    cst[:, 256:384] = (same & (s > t)) * sc
    cst[:, 384:512] = (same & (s >= t)) * sc
    cst[:, 512:640] = (same & (s < t)) * sc
    cst[:, 640:768] = (same & (s <= t))
    cst[:, 768:896] = (same & (s > t))
    rot = np.zeros((128, 128), np.float32)
    for m in range(64):
        rot[m + 64, m] = -1.0
        rot[m, m + 64] = 1.0
    cst[:, 896:1024] = rot
    cst[:, 1024:1152] = 1.0
    pos = np.arange(LAT)
    row = (pos // GRID_W).astype(np.float32)
    col = (pos % GRID_W).astype(np.float32)
    inv = np.power(np.float32(10000.0), -np.arange(32, dtype=np.float32) / np.float32(32)).astype(np.float32)
    ang = np.concatenate([row[:, None] * inv, col[:, None] * inv], -1).astype(np.float32)
    cos, sin = np.cos(ang).astype(np.float32), np.sin(ang).astype(np.float32)
    rope = np.zeros((128, 2, LAT), np.float32)
    rope[:, 0, :] = np.concatenate([cos.T, cos.T], 0)
    rope[:, 1, :] = np.concatenate([sin.T, sin.T], 0)
    return cst, rope


def make_natab(rpb, H):
    _, table, _ = na_plan(H)
    NB = len(table)
    nl, nh = rpb.shape[0], rpb.shape[1]
    a = np.arange(2)[:, None, None]
    kc = np.arange(64)[None, :, None]
    c = np.arange(64)[None, None, :]
    cs = np.clip(c - NA_KW // 2, 0, GRID_W - NA_KW)
    colok = (kc >= cs) & (kc < cs + NA_KW)
    dc = np.clip(kc - c + NA_KW - 1, 0, 2 * NA_KW - 2)
    out = np.full((nl, nh, 128, NB * 64), NEG, np.float32)
    for b, (u, v0, v1) in enumerate(table):
        d = a - u + NA_KH - 1
        vrow = np.array([v0, v1])[:, None, None]
        ok = np.broadcast_to(vrow & colok & (d >= 0) & (d <= 2 * NA_KH - 2), (2, 64, 64))
        dd = np.broadcast_to(np.clip(d, 0, 2 * NA_KH - 2), (2, 64, 64))
        dcc = np.broadcast_to(dc, (2, 64, 64))
        vals = rpb[:, :, dd, dcc]
        blk = np.where(ok[None, None], vals, np.float32(NEG)).reshape(nl, nh, 128, 64)
        out[:, :, :, b * 64:(b + 1) * 64] = blk
    return out


class _Stop(Exception):
    pass


def build(H=32, n_layers=DEPTH, stop=None):
    LAT = H * GRID_W
    TOK = CTXN + LAT
    NT = TOK // 128
    blocks = [(0, CTXN)] + [(CTXN + 512 * i, 512) for i in range(LAT // 512)]
    tiles_na, table_na, offs_na = na_plan(H)
    NBC = len(table_na) * 64

    nc = bass.Bass("TRN2", target_bir_lowering=False)
    P = Prog(nc)
    din = lambda name, shape: nc.dram_tensor(name, list(shape), F32, kind="ExternalInput").ap()
    xin = din("xin", [TOK, D])
    cc_d = din("cc", [128, KC, 2])
    ada_w = din("ada_w", [DEPTH, D, 3 * D])
    adab_d = din("adab", [128, DEPTH * 24])
    ln_g = din("ln_g", [DEPTH, D])
    ln_b = din("ln_b", [DEPTH, D])
    w_out = din("w_out", [DEPTH, D, D])
    gla_w_in = din("gla_w_in", [2, D, 3 * D])
    gla_w1 = din("gla_dec_w1", [2, 2, D, 16])
    gla_w2 = din("gla_dec_w2", [2, 2, 16, 512])
    gla_b = din("gla_dec_b", [2, 2, 512])
    gla_ng = din("gla_norm_g", [2, GLA_DV])
    na_w_in = din("na_w_in", [2, D, 4 * D])
    natab = din("natab", [2, NA_H, 128, NBC])
    cst_d = din("cst", [128, 1152])
    rope_d = din("rope", [128, 2, LAT])
    out = nc.dram_tensor("out", [LAT, D], F32, kind="ExternalOutput").ap()
    xs = [nc.dram_tensor("xs%d" % i, [TOK, D], F32).ap() for i in range(2)]

    hT = P.sb("hT", [128, KC, TOK], BF16)
    UT = P.sb("UT", [128, KC, TOK], BF16)
    cst = P.sb("cstb", [128, 1152], F32)
    ident = cst[:, 0:128]
    A_le, A_gt, A_ge, A_lt = (cst[:, 128 + 128 * i:256 + 128 * i] for i in range(4))
    A_gtge = cst[:, 256:512]
    M_f, M_b = cst[:, 640:768], cst[:, 768:896]
    rot = cst[:, 896:1024]
    onesf = cst[:, 1024:1152]
    identb = P.sb("identb", [128, 128], BF16)
    onesb = P.sb("onesb", [128, 64], BF16)
    adaW = P.sb("adaW", [128, 2, KC, 128], F32)
    ccs = P.sb("ccs", [128, KC, 2], F32)
    cct = P.sb("cct", [128, KC, 2], F32)
    adab = P.sb("adabs", [128, DEPTH * 24], F32)
    mcol = P.sb("mcol", [128, 2, 24, 2], F32)
    sc1 = P.sb("sc1", [128, 2, KC, 2], F32)
    gdiag = P.sb("gdiag", [128, 2, 128], F32)
    gtb = P.sb("gtb", [128, 2, D], F32)
    lnb = P.sb("lnb", [128, 2, D], F32)
    wout = P.sb("wout", [128, KC, D], BF16)
    xt = P.sb("xt", [128, 2, D], F32)
    t1 = P.sb("t1", [128, D], F32)
    bst = P.sb("bst", [128, 2, 6], F32)
    mv = P.sb("mv", [128, 2], F32)
    sm = P.sb("sm", [128, 8], F32)
    ARENA = 75 * 1024
    arena = P.sb("arena", [128, ARENA // 2], BF16)
    ps = [P.ps("ps%d" % i) for i in range(8)]
    psk = lambda i: ("ps", i)

    class Carver:
        def __init__(self):
            self.off = 0

        def __call__(self, dtype, *free):
            n = int(np.prod(free))
            nb = n * (4 if dtype == F32 else 2)
            nb = (nb + 63) // 64 * 64
            assert self.off + nb <= ARENA, (self.off, nb)
            a = arena[:, self.off // 2:(self.off + nb) // 2]
            self.off += nb
            if dtype == F32:
                a = a.bitcast(F32)
            a = a[:, 0:n]
            if len(free) == 2:
                a = a.rearrange("p (a b) -> p a b", a=free[0])
            elif len(free) == 3:
                a = a.rearrange("p (a b c) -> p a b c", a=free[0], b=free[1])
            return a

    def mm(out_, lhsT, rhs, start, stop, reads, writes):
        P.op("pe", lambda e: e.matmul(out_, lhsT=lhsT, rhs=rhs, start=start, stop=stop), reads, writes)

    def tr(out_, in_, idn, reads, writes):
        P.op("pe", lambda e: e.transpose(out_, in_, idn), reads, writes)

    def act(out_, in_, func, reads, writes, scale=1.0, bias=0.0, accum=None):
        if accum is None:
            P.op("act", lambda e: e.activation(out=out_, in_=in_, func=func, bias=bias, scale=scale), reads, writes)
        else:
            P.op("act", lambda e: e.activation(out=out_, in_=in_, func=func, bias=bias, scale=scale, accum_out=accum),
                 reads, writes)

    def tt(eng, out_, in0, in1, op, reads, writes):
        P.op(eng, lambda e: e.tensor_tensor(out=out_, in0=in0, in1=in1, op=op), reads, writes)

    def tsc(eng, out_, in0, s1, s2, op0, op1, reads, writes):
        if s2 is None:
            P.op(eng, lambda e: e.tensor_scalar(out=out_, in0=in0, scalar1=s1, scalar2=None, op0=op0), reads, writes)
        else:
            P.op(eng, lambda e: e.tensor_scalar(out=out_, in0=in0, scalar1=s1, scalar2=s2, op0=op0, op1=op1),
                 reads, writes)

    def stt(out_, in0, scalar, in1, op0, op1, reads, writes):
        P.op("dve", lambda e: e.scalar_tensor_tensor(out=out_, in0=in0, scalar=scalar, in1=in1, op0=op0, op1=op1),
             reads, writes)

    def cp(eng, out_, in_, reads, writes):
        if eng == "act":
            act(out_, in_, AF.Copy, reads, writes)
        else:
            P.op(eng, lambda e: e.tensor_copy(out=out_, in_=in_), reads, writes)

    def barrier():
        last = {}
        for e in P.ENG:
            for o in P.streams[e]:
                if o.fn is not None:
                    last[P._ck(o)] = o
        for e in P.ENG:
            b = _Op(e, None, False, None)
            b.deps = tuple(d for d in last.values() if not (d.eng == e and not d.is_dma))
            P.nops += 1
            b.n = P.nops
            P.streams[e].append(b)
        P.res = {}

    def stage(name):
        if stop == name:
            raise _Stop()

    tiles_of = lambda a, n: [("hT", t) for t in range(a // 128, (a + n) // 128)]

    P.dma("sp", cst[:, :], cst_d, writes=[("cst",)])
    P.dma("sp", ccs[:, :, :], cc_d, writes=[("ccs",)])
    P.dma("sp", adab[:, :], adab_d, writes=[("adab",)])
    cp("dve", identb[:, :], ident, [("cst",)], [("identb",)])
    P.op("pool", lambda e: e.memset(onesb[:, :], 1.0), writes=[("onesb",)])
    act(cct[:, :, :], ccs[:, :, :], AF.Exp, [("ccs",)], [("cct",)], scale=-1.0)
    tsc("dve", cct[:, :, :], cct[:, :, :], 1.0, None, ALU.add, None, [("cct",)], [("cct",)])
    P.op("dve", lambda e: e.reciprocal(out=cct[:, :, :], in_=cct[:, :, :]), [("cct",)], [("cct",)])
    tt("dve", ccs[:, :, :], ccs[:, :, :], cct[:, :, :], ALU.mult, [("ccs",), ("cct",)], [("ccs",)])

    ada_cnt = [0]

    def ada_cols(l, js, par):
        for j in js:
            b = ada_cnt[0] % 2
            ada_cnt[0] += 1
            P.dma("sp", adaW[:, b, :, :], ada_w[l, :, j * 128:(j + 1) * 128].rearrange("(k p) c -> p k c", p=128),
                  writes=[("adaW", b)])
            for k in range(KC):
                mm(ps[7][:, 2 * j:2 * j + 2], adaW[:, b, k, :], ccs[:, k, :], k == 0, k == KC - 1,
                   [("adaW", b), ("ccs",)], [psk(7)])
            tsc("dve", mcol[:, par, j, :], ps[7][:, 2 * j:2 * j + 2], adab[:, l * 24 + j:l * 24 + j + 1], None,
                ALU.add, None, [psk(7), ("adab",)], [("mcol", par, j)])

    def ada_mod(l, par):
        ada_cols(l, range(16), par)
        tsc("dve", sc1[:, par, :, :], mcol[:, par, 8:16, :], 1.0, None, ALU.add, None,
            [("mcol", par)], [("sc1", par)])

    def ada_gate(l, par):
        ada_cols(l, range(16, 24), par)
        for cond in range(2):
            for j in range(8):
                gb = j % 2
                tsc("dve", gdiag[:, gb, :], ident, mcol[:, par, 16 + j, cond:cond + 1], None, ALU.mult, None,
                    [("cst",), ("mcol", par, 16 + j)], [("gdiag", gb)])
                mm(ps[7][:, (j % 4) * 128:(j % 4 + 1) * 128], onesf, gdiag[:, gb, :], True, True,
                   [("cst",), ("gdiag", gb)], [psk(7)])
                if j % 4 == 3:
                    cp("act", gtb[:, cond, (j // 4) * 512:(j // 4 + 1) * 512], ps[7][:, :], [psk(7)],
                       [("gtb", cond, j // 4)])

    def emit_hT(t, src, srckey, par):
        cond = 1 if t < 2 else 0
        for half in range(2):
            pb = 5 + half
            for kk in range(4):
                k = half * 4 + kk
                tr(ps[pb][:, kk * 128:(kk + 1) * 128], src[:, k * 128:(k + 1) * 128], ident,
                   [srckey, ("cst",)], [psk(pb)])
            for kk in range(4):
                k = half * 4 + kk
                o_ = hT[:, k, t * 128:(t + 1) * 128]
                i_ = ps[pb][:, kk * 128:(kk + 1) * 128]
                rd = [psk(pb), ("sc1", par), ("mcol", par)]
                tsc("dve", o_, i_, sc1[:, par, k, cond:cond + 1], mcol[:, par, k, cond:cond + 1],
                    ALU.mult, ALU.add, rd, [("hT", t, k)])

    def gla_layer(l, j):
        C = Carver()
        Wqk = C(BF16, KC, 256)
        Wv = C(BF16, KC, 256)
        qT = C(BF16, TOK)
        kT = C(BF16, TOK)
        ktok = C(BF16, NT, 128)
        vtok = C(BF16, NT, 256)
        ob = C(F32, NT, 256)
        lowT = C(BF16, TOK)
        w1 = C(BF16, KC, 32)
        w2a = C(BF16, 512)
        normg = C(F32, 256)
        ropeb = C(F32, 2, 512)
        qraw = C(F32, 512)
        tb_ = C(F32, 512)
        e1 = C(F32, 128)
        sp_ = C(F32, 128)
        eq = C(F32, 128)
        ek = C(F32, 128)
        edec = C(F32, 2)
        et = C(F32, 128)
        qd = C(BF16, 128)
        kd = C(BF16, 128)
        kend = C(BF16, 128)
        attT = C(BF16, 128)
        Sa = C(F32, 2, 256)
        Sb = C(BF16, 2, 256)
        eg = C(F32, 256)
        g2 = C(F32, 256)
        of = C(F32, 256)
        ub = C(BF16, 256)

        wsrc = gla_w_in[j]
        for z in range(2):
            P.dma("pool", w1[:, :, z * 16:(z + 1) * 16], gla_w1[j, z].rearrange("(k p) r -> p k r", p=128),
                  writes=[("w1", z)], semkey=("w1",))
            P.dma("pool", w2a[z * 32:z * 32 + 16, :], gla_w2[j, z], writes=[("w2a", z, 0)], semkey=("w2a",))
            P.dma("pool", w2a[z * 32 + 16:z * 32 + 17, :], gla_b[j, z:z + 1, :], writes=[("w2a", z, 1)],
                  semkey=("w2a",))
        P.dma("sp", normg[:, :], gla_ng[j].partition_broadcast(128), writes=[("normg",)])
        P.op("pool", lambda e: e.memset(lowT[0:64, :], 1.0), writes=[("lowT",)])
        for (a, n) in blocks:
            for z in range(2):
                for k in range(KC):
                    mm(ps[0][z * 32:z * 32 + 16, 0:n], w1[:, k, z * 16:(z + 1) * 16], hT[:, k, a:a + n], k == 0,
                       k == KC - 1, [("w1", z)] + tiles_of(a, n), [psk(0)])
                cp("act", lowT[z * 32:z * 32 + 16, a:a + n], ps[0][z * 32:z * 32 + 16, 0:n], [psk(0)],
                   [("lowT", z, a)])

        stage("S4")
        for h in range(GLA_H):
            P.dma("pool", Wqk[:, :, 0:128], wsrc[:, h * 128:(h + 1) * 128].rearrange("(k p) c -> p k c", p=128),
                  writes=[("Wqk", 0)])
            P.dma("pool", Wqk[:, :, 128:256],
                  wsrc[:, 512 + h * 128:512 + (h + 1) * 128].rearrange("(k p) c -> p k c", p=128),
                  writes=[("Wqk", 1)])
            P.dma("pool", Wv[:, :, :],
                  wsrc[:, 1024 + h * 256:1024 + (h + 1) * 256].rearrange("(k p) c -> p k c", p=128),
                  writes=[("Wv",)])
            for (a, n) in blocks:
                if a >= CTXN:
                    P.dma("sp", ropeb[:, :, :], rope_d[:, :, a - CTXN:a - CTXN + 512], writes=[("ropeb",)])
                for w in range(2):
                    dst, dk_ = (qT, "qT") if w == 0 else (kT, "kT")
                    pb = w
                    for k in range(KC):
                        mm(ps[pb][:, 0:n], Wqk[:, k, w * 128:(w + 1) * 128], hT[:, k, a:a + n], k == 0, k == KC - 1,
                           [("Wqk", w)] + tiles_of(a, n), [psk(pb)])
                    scl = GLA_DK ** -0.5 if w == 0 else 1.0
                    if a < CTXN:
                        act(dst[:, a:a + n], ps[pb][:, 0:n], AF.Copy, [psk(pb)], [(dk_, a)], scale=scl)
                    else:
                        act(qraw[:, :], ps[pb][:, 0:n], AF.Copy, [psk(pb)], [("qraw",)], scale=scl)
                        mm(ps[2][:, :], rot, qraw[:, :], True, True, [("cst",), ("qraw",)], [psk(2)])
                        tt("pool", qraw[:, :], qraw[:, :], ropeb[:, 0, :], ALU.mult, [("qraw",), ("ropeb",)],
                           [("qraw",)])
                        tt("dve", tb_[:, :], ps[2][:, :], ropeb[:, 1, :], ALU.mult, [psk(2), ("ropeb",)], [("tb",)])
                        tt("dve", dst[:, a:a + n], qraw[:, :], tb_[:, :], ALU.add, [("qraw",), ("tb",)], [(dk_, a)])
            for t0 in range(0, NT, 4):
                nt = min(4, NT - t0)
                pv = ps[2][:, 0:256].bitcast(BF16)
                for i in range(nt):
                    tr(pv[:, i * 128:(i + 1) * 128], kT[:, (t0 + i) * 128:(t0 + i + 1) * 128], identb[:, :],
                       [("kT",), ("identb",)], [psk(2)])
                cp("act", ktok[:, t0:t0 + nt, :], pv[:, 0:nt * 128].rearrange("p (a b) -> p a b", a=nt), [psk(2)],
                   [("ktok", t0)])
            for t0 in range(0, NT, 2):
                pb = 3
                for i in range(2):
                    t = t0 + i
                    for k in range(KC):
                        mm(ps[pb][:, i * 256:(i + 1) * 256], hT[:, k, t * 128:(t + 1) * 128], Wv[:, k, :], k == 0,
                           k == KC - 1, [("Wv",), ("hT", t)], [psk(pb)])
                cp("act", vtok[:, t0:t0 + 2, :], ps[pb][:, :].rearrange("p (a b) -> p a b", a=2), [psk(pb)],
                   [("vtok", t0)])

            P.dma("pool", Wv[:, :, :],
                  wsrc[:, 2048 + h * 256:2048 + (h + 1) * 256].rearrange("(k p) c -> p k c", p=128),
                  writes=[("Wv",)])

            def tile_common(t, z, fwd):
                tk = slice(t * 128, (t + 1) * 128)
                zr = slice(z * 32, z * 32 + 17)
                mm(ps[4][:, 0:128], lowT[zr, tk], w2a[zr, h * 128:(h + 1) * 128], True, True,
                   [("lowT",), ("w2a", z)], [psk(4)])
                act(e1[:, :], ps[4][:, 0:128], AF.Exp, [psk(4)], [("e1",)], scale=-1.0)
                act(sp_[:, :], e1[:, :], AF.Ln, [("e1",)], [("sp",)], bias=1.0)
                if fwd:
                    mm(ps[4][:, 0:128], sp_[:, :], A_le, True, True, [("sp",), ("cst",)], [psk(4)])
                    act(eq[:, :], ps[4][:, 0:128], AF.Exp, [psk(4)], [("eq",)])
                    act(ek[:, :], ps[4][:, 0:128], AF.Exp, [psk(4)], [("ek",)], scale=-1.0)
                    mm(ps[5][:, 0:128], A_gt, sp_[:, :], True, True, [("sp",), ("cst",)], [psk(5)])
                else:
                    mm(ps[4][:, 0:256], sp_[:, :], A_gtge, True, True, [("sp",), ("cst",)], [psk(4)])
                    act(eq[:, :], ps[4][:, 0:128], AF.Exp, [psk(4)], [("eq",)])
                    act(ek[:, :], ps[4][:, 128:256], AF.Exp, [psk(4)], [("ek",)], scale=-1.0)
                    act(edec[:, :], ps[4][:, 128:256:64], AF.Exp, [psk(4)], [("edec",)])
                    mm(ps[5][:, 0:128], A_lt, sp_[:, :], True, True, [("sp",), ("cst",)], [psk(5)])
                act(et[:, :], ps[5][:, 0:128], AF.Exp, [psk(5)], [("et",)])
                tt("pool", qd[:, :], qT[:, tk], eq[:, :], ALU.mult, [("qT",), ("eq",)], [("qd",)])
                tt("pool", kd[:, :], kT[:, tk], ek[:, :], ALU.mult, [("kT",), ("ek",)], [("kd",)])
                tt("dve", kend[:, :], ktok[:, t, :], et[:, :], ALU.mult, [("ktok",), ("et",)], [("kend",)])
                mm(ps[6][:, 0:128], kd[:, :], qd[:, :], True, True, [("kd",), ("qd",)], [psk(6)])
                tt("dve", attT[:, :], ps[6][:, 0:128], M_f if fwd else M_b, ALU.mult, [psk(6), ("cst",)], [("attT",)])

            def sweep(t, fwd, cur):
                mm(ps[3][:, 0:256], attT[:, :], vtok[:, t, :], True, False, [("attT",), ("vtok",)], [psk(3)])
                for ci, c in enumerate((0, 1) if fwd else (1, 0)):
                    cs = slice(c * 64, (c + 1) * 64)
                    mm(ps[3][cs, 0:256], qd[:, cs], Sb[:, cur, :], False, True, [("qd",), ("Sb", cur)], [psk(3)])
                    pk = 0 + ci
                    mm(ps[pk][:, 0:256], kend[cs, :], vtok[cs, t, :], True, True, [("kend",), ("vtok",)], [psk(pk)])
                    dsc = eq[:, c * 64 + 63:c * 64 + 64] if fwd else edec[:, c:c + 1]
                    nx = 1 - cur
                    stt(Sa[:, nx, :], Sa[:, cur, :], dsc, ps[pk][:, 0:256], ALU.mult, ALU.add,
                        [("Sa", cur), ("eq",), ("edec",), psk(pk)], [("Sa", nx)])
                    cp("act", Sb[:, nx, :], Sa[:, nx, :], [("Sa", nx)], [("Sb", nx)])
                    cur = nx
                return cur

            stage("S5")
            P.op("pool", lambda e: e.memset(Sa[:, 0, :], 0.0), writes=[("Sa", 0)])
            P.op("pool", lambda e: e.memset(Sb[:, 0, :], 0.0), writes=[("Sb", 0)])
            cur = 0
            for t in [1, 0] + list(range(NT - 1, 1, -1)):
                tile_common(t, 1, False)
                cur = sweep(t, False, cur)
                cp("act", ob[:, t, :], ps[3][:, 0:256], [psk(3)], [("ob", t)])
            stage("S6")
            P.op("pool", lambda e: e.memset(Sa[:, 0, :], 0.0), writes=[("Sa", 0)])
            P.op("pool", lambda e: e.memset(Sb[:, 0, :], 0.0), writes=[("Sb", 0)])
            cur = 0
            for t in range(NT):
                tk = slice(t * 128, (t + 1) * 128)
                tile_common(t, 0, True)
                cur = sweep(t, True, cur)
                for k in range(KC):
                    mm(ps[2][:, 0:256], hT[:, k, tk], Wv[:, k, :], k == 0, k == KC - 1, [("Wv",), ("hT", t)], [psk(2)])
                act(eg[:, :], ps[2][:, 0:256], AF.Exp, [psk(2)], [("eg",)], scale=-1.0)
                tsc("pool", eg[:, :], eg[:, :], 1.0, None, ALU.add, None, [("eg",)], [("eg",)])
                P.op("dve", lambda e: e.reciprocal(out=eg[:, :], in_=eg[:, :]), [("eg",)], [("eg",)])
                tt("dve", g2[:, :], ps[2][:, 0:256], eg[:, :], ALU.mult, [psk(2), ("eg",)], [("g2",)])
                tt("pool", g2[:, :], g2[:, :], normg[:, :], ALU.mult, [("g2",), ("normg",)], [("g2",)])
                tt("dve", of[:, :], ps[3][:, 0:256], ob[:, t, :], ALU.add, [psk(3), ("ob", t)], [("of",)])
                act(eg[:, :], of[:, :], AF.Square, [("of",)], [("eg",), ("sm", 0)], accum=sm[:, 0:1])
                act(sm[:, 1:2], sm[:, 0:1], AF.Ln, [("sm", 0)], [("sm", 1)], scale=1.0 / GLA_DV, bias=NORM_EPS)
                act(sm[:, 2:3], sm[:, 1:2], AF.Exp, [("sm", 1)], [("sm", 2)], scale=-0.5)
                stt(ub[:, :], of[:, :], sm[:, 2:3], g2[:, :], ALU.mult, ALU.mult, [("of",), ("sm", 2), ("g2",)],
                    [("ub",)])
                pu = ps[7][:, 0:128].bitcast(BF16)
                for i in range(2):
                    tr(pu[:, i * 128:(i + 1) * 128], ub[:, i * 128:(i + 1) * 128], identb[:, :],
                       [("ub",), ("identb",)], [psk(7)])
                cp("act", UT[:, 2 * h:2 * h + 2, tk], pu.rearrange("p (a b) -> p a b", a=2), [psk(7)],
                   [("UT", t, 2 * h), ("UT", t, 2 * h + 1)])
            stage("S7")

    def na_layer(l, j, need_ctx):
        C = Carver()
        Wp = C(BF16, KC, 512)
        QT = C(BF16, TOK)
        KT = C(BF16, TOK)
        VT = C(BF16, TOK)
        SG = C(F32, TOK)
        Vtok = C(BF16, NT, 128)
        tabf = C(F32, NBC)
        Tb = C(BF16, 2, NBC)
        E = C(BF16, 4, 512)
        rec = C(F32, 512)
        uu = C(F32, 512)
        egb = C(F32, 512)
        wsrc = na_w_in[j]
        ecnt = [0]
        scnt = [0]

        for p in range(NA_H // 2):
            for i in range(4):
                P.dma("pool", Wp[:, :, i * 128:(i + 1) * 128],
                      wsrc[:, i * D + p * 128:i * D + (p + 1) * 128].rearrange("(k p) c -> p k c", p=128),
                      writes=[("Wp", i)])
            for (a, n) in blocks:
                for i in range(4):
                    pb = i % 2
                    for k in range(KC):
                        mm(ps[pb][:, 0:n], Wp[:, k, i * 128:(i + 1) * 128], hT[:, k, a:a + n], k == 0, k == KC - 1,
                           [("Wp", i)] + tiles_of(a, n), [psk(pb)])
                    if i == 0:
                        act(QT[:, a:a + n], ps[pb][:, 0:n], AF.Copy, [psk(pb)], [("QT", a)], scale=NA_DH ** -0.5)
                    elif i == 1:
                        cp("dve", KT[:, a:a + n], ps[pb][:, 0:n], [psk(pb)], [("KT", a)])
                    elif i == 2:
                        cp("act", VT[:, a:a + n], ps[pb][:, 0:n], [psk(pb)], [("VT", a)])
                    else:
                        act(egb[:, 0:n], ps[pb][:, 0:n], AF.Exp, [psk(pb)], [("egb",)], scale=-1.0)
                        tsc("pool", egb[:, 0:n], egb[:, 0:n], 1.0, None, ALU.add, None, [("egb",)], [("egb",)])
                        P.op("dve", lambda e, n=n: e.reciprocal(out=egb[:, 0:n], in_=egb[:, 0:n]), [("egb",)],
                             [("egb",)])
                        tt("dve", SG[:, a:a + n], ps[pb][:, 0:n], egb[:, 0:n], ALU.mult, [psk(pb), ("egb",)],
                           [("SG", a)])
            for t0 in range(0, NT, 4):
                nt = min(4, NT - t0)
                pv = ps[2][:, 0:256].bitcast(BF16)
                for i in range(nt):
                    tr(pv[:, i * 128:(i + 1) * 128], VT[:, (t0 + i) * 128:(t0 + i + 1) * 128], identb[:, :],
                       [("VT",), ("identb",)], [psk(2)])
                cp("dve", Vtok[:, t0:t0 + nt, :], pv[:, 0:nt * 128].rearrange("p (a b) -> p a b", a=nt), [psk(2)],
                   [("Vtok", t0)])
            for hp in range(2):
                P.dma("sp", tabf[:, :], natab[j, 2 * p + hp], writes=[("tabf",)])
                act(Tb[:, hp, :], tabf[:, :], AF.Exp, [("tabf",)], [("Tb", hp)])
            qblocks = [(CTXN + 512 * i, 512, 8 * i) for i in range(LAT // 512)]
            if need_ctx:
                qblocks.append((0, CTXN, None))
            for bi, (qa, qn, r0) in enumerate(qblocks):
                po, pd = (4, 5) if bi % 2 == 0 else (6, 7)
                for hp in range(2):
                    hs = slice(hp * 64, hp * 64 + 64)
                    klist = [(0, 0, qn, None), (1, 0, qn, None)]
                    if r0 is not None:
                        for t in range(H // 2):
                            lo, sig = tiles_na[t]
                            hi = lo + len(sig) - 1
                            ra, rb = max(lo, r0), min(hi, r0 + 7)
                            if ra > rb:
                                continue
                            klist.append((2 + t, (ra - r0) * 64, (rb - ra + 1) * 64, (offs_na[t] + ra - lo) * 64))
                    for ki, (kt, qo, n, tc) in enumerate(klist):
                        sb_ = 2 + scnt[0] % 2
                        scnt[0] += 1
                        eb = ecnt[0] % 4
                        ecnt[0] += 1
                        mm(ps[sb_][:, 0:n], KT[hs, kt * 128:(kt + 1) * 128], QT[hs, qa + qo:qa + qo + n], True, True,
                           [("KT",), ("QT",)], [psk(sb_)])
                        act(E[:, eb, 0:n], ps[sb_][:, 0:n], AF.Exp, [psk(sb_)], [("E", eb)])
                        if tc is not None:
                            tt("dve" if ki % 3 else "pool", E[:, eb, 0:n], E[:, eb, 0:n], Tb[:, hp, tc:tc + n],
                               ALU.mult, [("E", eb), ("Tb", hp)], [("E", eb)])
                        first, last = ki == 0, ki == len(klist) - 1
                        mm(ps[po][hs, qo:qo + n], Vtok[:, kt, hs], E[:, eb, 0:n], first, last,
                           [("Vtok",), ("E", eb)], [psk(po)])
                        mm(ps[pd][hs, qo:qo + n], onesb[:, :], E[:, eb, 0:n], first, last,
                           [("onesb",), ("E", eb)], [psk(pd)])
                P.op("dve", lambda e, pd=pd, qn=qn: e.reciprocal(out=rec[:, 0:qn], in_=ps[pd][:, 0:qn]), [psk(pd)],
                     [("rec",)])
                tt("dve", uu[:, 0:qn], ps[po][:, 0:qn], rec[:, 0:qn], ALU.mult, [psk(po), ("rec",)], [("uu",)])
                for tq in range(qa // 128, (qa + qn) // 128):
                    o0 = tq * 128 - qa
                    tt("pool", UT[:, p, tq * 128:(tq + 1) * 128], uu[:, o0:o0 + 128], SG[:, tq * 128:(tq + 1) * 128],
                       ALU.mult, [("uu",), ("SG",)], [("UT", tq, p)])

    def outproj(l, need_ctx, last):
        xsrc = xin if l == 0 else xs[(l - 1) % 2]
        P.dma("pool", wout[:, :, :], w_out[l].rearrange("(k p) c -> p k c", p=128), writes=[("wout",)])
        P.dma("sp", lnb[:, 0, :], ln_g[l].partition_broadcast(128), writes=[("lnb", 0)])
        P.dma("sp", lnb[:, 1, :], ln_b[l].partition_broadcast(128), writes=[("lnb", 1)])
        tl = list(range(NT)) if need_ctx else list(range(2, NT))
        for ti, t in enumerate(tl):
            xb = ti % 2
            cond = 1 if t < 2 else 0
            tk = slice(t * 128, (t + 1) * 128)
            x_ = xt[:, xb, :]
            P.dma("sp", x_, xsrc[tk, :], reads=[("xd", l - 1, t)], writes=[("xt", xb)], semkey=("xt", xb))
            for nh in range(2):
                for k in range(KC):
                    mm(ps[nh][:, :], UT[:, k, tk], wout[:, k, nh * 512:(nh + 1) * 512], k == 0, k == KC - 1,
                       [("UT", t), ("wout",)], [psk(nh)])
                tt("dve", t1[:, nh * 512:(nh + 1) * 512], ps[nh][:, :], gtb[:, cond, nh * 512:(nh + 1) * 512],
                   ALU.mult, [psk(nh), ("gtb", cond)], [("t1", nh)])
            stt(x_, x_, float(ALPHA), t1[:, :], ALU.mult, ALU.add, [("xt", xb), ("t1",)], [("xt", xb)])
            for nh in range(2):
                P.op("dve", lambda e, nh=nh, x_=x_: e.bn_stats(out=bst[:, nh, :], in_=x_[:, nh * 512:(nh + 1) * 512]),
                     [("xt", xb)], [("bst", nh)])
            P.op("dve", lambda e: e.bn_aggr(out=mv[:, :], in_=bst[:, :, :].rearrange("p a b -> p (a b)")),
                 [("bst",)], [("mv",)])
            act(sm[:, 4:5], mv[:, 1:2], AF.Ln, [("mv",)], [("sm", 4)], bias=LN_EPS)
            act(sm[:, 5:6], sm[:, 4:5], AF.Exp, [("sm", 4)], [("sm", 5)], scale=-0.5)
            tsc("dve", x_, x_, mv[:, 0:1], sm[:, 5:6], ALU.subtract, ALU.mult, [("xt", xb), ("mv",), ("sm", 5)],
                [("xt", xb)])
            tt("pool", x_, x_, lnb[:, 0, :], ALU.mult, [("xt", xb), ("lnb", 0)], [("xt", xb)])
            tt("pool", x_, x_, lnb[:, 1, :], ALU.add, [("xt", xb), ("lnb", 1)], [("xt", xb)])
            if last:
                if t >= 2:
                    P.dma("sp", out[(t - 2) * 128:(t - 1) * 128, :], x_, reads=[("xt", xb)], writes=[("outd", t)],
                          semkey=("xst", xb))
            else:
                P.dma("sp", xs[l % 2][tk, :], x_, reads=[("xt", xb)], writes=[("xd", l, t)], semkey=("xst", xb))
                emit_hT(t, x_, ("xt", xb), (l + 1) % 2)

    def main():
        stage("S0")
        ada_mod(0, 0)
        stage("S1")
        for t in range(NT):
            xb = t % 2
            P.dma("sp", xt[:, xb, :], xin[t * 128:(t + 1) * 128, :], writes=[("xt", xb)], semkey=("xt", xb))
            emit_hT(t, xt[:, xb, :], ("xt", xb), 0)
        stage("S2")
        for l in range(n_layers):
            need_ctx = l < DEPTH - 1
            last = l == n_layers - 1
            ada_gate(l, l % 2)
            stage("S3")
            if not last:
                ada_mod(l + 1, (l + 1) % 2)
            if l % 2 == 0:
                gla_layer(l, l // 2)
            else:
                na_layer(l, l // 2, need_ctx)
            stage("S8")
            outproj(l, need_ctx, last)
            if not last:
                barrier()

    try:
        main()
    except _Stop:
        P.dma("sp", out[0:128, :], xt[:, 0, :], reads=[("xt", 0)], writes=[("outd", 0)], semkey=("xst", 0))
    P.finish()
    return nc, P


_CACHE = {}


def prep_inputs(H, x, c, ctx, c_ctx, ada_w, ada_b, ln_g, ln_b, w_out, gla_w_in, gla_dec_w1, gla_dec_w2,
                gla_dec_b, gla_norm_g, na_w_in, na_rpb):
    f = lambda a: np.ascontiguousarray(np.asarray(a, dtype=np.float32))
    B = x.shape[0]
    cst, rope = make_consts(H)
    natab = make_natab(f(na_rpb), H)
    adab = f(np.asarray(ada_b).reshape(DEPTH, 24, 128).transpose(2, 0, 1).reshape(128, DEPTH * 24))
    shared = dict(ada_w=f(ada_w), adab=adab, ln_g=f(ln_g), ln_b=f(ln_b), w_out=f(w_out), gla_w_in=f(gla_w_in),
                  gla_dec_w1=f(gla_dec_w1), gla_dec_w2=f(gla_dec_w2), gla_dec_b=f(gla_dec_b),
                  gla_norm_g=f(gla_norm_g), na_w_in=f(na_w_in), natab=natab, cst=cst, rope=rope)
    maps = []
    for b in range(B):
        m = dict(shared)
        m["xin"] = f(np.concatenate([np.asarray(ctx[b]), np.asarray(x[b])], 0))
        cc = np.stack([np.asarray(c[b]).reshape(KC, 128).T, np.asarray(c_ctx).reshape(KC, 128).T], -1)
        m["cc"] = f(cc)
        maps.append(m)
    return maps


def kernel(**inputs):
    H = 32
    if "nc" not in _CACHE:
        _CACHE["nc"] = build(H, DEPTH)[0]
    nc = _CACHE["nc"]
    maps = prep_inputs(H, **inputs)
    res = run_bass_kernel_spmd(nc, maps, core_ids=list(range(len(maps))))
    return np.stack([np.asarray(r["out"], dtype=np.float32) for r in res.results], 0)
```

```python
from contextlib import ExitStack
import os

import numpy as np
import concourse.bass as bass
import concourse.mybir as mybir
from concourse.bass_utils import run_bass_kernel_spmd

F32 = mybir.dt.float32
BF16 = mybir.dt.bfloat16
AF = mybir.ActivationFunctionType
ALU = mybir.AluOpType


class _Op:
    __slots__ = ("eng", "fn", "deps", "is_dma", "semkey", "sig", "val", "n")

    def __init__(self, eng, fn, is_dma, semkey):
        self.eng, self.fn, self.is_dma, self.semkey = eng, fn, is_dma, semkey
        self.deps, self.sig, self.val, self.n = (), False, 0, 0


class _Ent:
    __slots__ = ("w", "r")

    def __init__(self):
        self.w, self.r = None, {}


class Prog:
    ENG = ("pe", "act", "dve", "pool", "sp")

    def __init__(self, nc):
        self.nc = nc
        self.stack = ExitStack()
        self.streams = {e: [] for e in self.ENG}
        self.res = {}
        self.nops = 0
        self.dma_last = {}

    def sb(self, name, shape, dtype):
        return self.stack.enter_context(self.nc.sbuf_tensor(name, list(shape), dtype))

    def ps(self, name):
        return self.stack.enter_context(self.nc.psum_tensor(name, [128, 512], F32))

    @staticmethod
    def _ck(o):
        return o.semkey if o.is_dma else o.eng

    def _conf(self, key):
        b = self.res.get(key[0])
        if not b:
            return
        n = len(key)
        for k, ent in b.items():
            m = len(k)
            if (k[:n] == key) if m >= n else (key[:m] == k):
                yield k, ent

    def _ent(self, key):
        b = self.res.setdefault(key[0], {})
        e = b.get(key)
        if e is None:
            e = b[key] = _Ent()
        return e

    def _record(self, o, reads, writes):
        reads = [k[:2] if k[0] == "ps" else k for k in reads]
        writes = [k[:2] if k[0] == "ps" else k for k in writes]
        deps = {}

        def add(d):
            if d is None or d is o:
                return
            if o.eng == "pe" and d.eng == "pe" and not d.is_dma:
                return
            if d.is_dma:
                d = self.dma_last[d.semkey]
            ck = self._ck(d)
            p = deps.get(ck)
            if p is None or d.n > p.n:
                deps[ck] = d

        for k in reads:
            for _, ent in self._conf(k):
                add(ent.w)
        for k in writes:
            for _, ent in self._conf(k):
                add(ent.w)
                for r in ent.r.values():
                    add(r)
        for k in reads:
            self._ent(k).r[self._ck(o)] = o
        for k in writes:
            n = len(k)
            b = self.res.setdefault(k[0], {})
            for kk in [kk for kk in b if len(kk) > n and kk[:n] == k]:
                del b[kk]
            e = self._ent(k)
            e.w, e.r = o, {}
        o.deps = tuple(deps.values())
        self.nops += 1
        o.n = self.nops
        self.streams[o.eng].append(o)
        return o

    def op(self, eng, fn, reads=(), writes=()):
        return self._record(_Op(eng, fn, False, None), reads, writes)

    def dma(self, q, out, in_, reads=(), writes=(), semkey=None):
        if semkey is None:
            semkey = writes[0]
        fn = lambda e: e.dma_start(out=out, in_=in_)
        o = self._record(_Op(q, fn, True, ("dma",) + tuple(semkey)), reads, writes)
        self.dma_last[o.semkey] = o
        return o

    def finish(self):
        nc = self.nc
        for s in self.streams.values():
            for o in s:
                for d in o.deps:
                    d.sig = True
        semnames = {}
        for e in self.ENG:
            cnt = 0
            for o in self.streams[e]:
                if not o.is_dma and o.sig:
                    cnt += 1
                    o.val = cnt
            semnames[e] = None
        dcnt = {}
        for e in self.ENG:
            for o in self.streams[e]:
                if o.is_dma:
                    dcnt[o.semkey] = dcnt.get(o.semkey, 0) + 16
                    o.val = dcnt[o.semkey]
                    semnames[o.semkey] = None
        sems = {}
        for i, k in enumerate(semnames):
            sems[k] = self.stack.enter_context(nc.semaphore("s%d" % i))
        self.n_sems = len(sems)
        ck = self._ck

        def emit(name, eng):
            seen = {}
            for o in self.streams[name]:
                for d in o.deps:
                    k = ck(d)
                    if seen.get(k, 0) < d.val:
                        eng.wait_ge(sems[k], d.val)
                        seen[k] = d.val
                if o.fn is None:
                    continue
                ins = o.fn(eng)
                if o.is_dma:
                    ins.then_inc(sems[o.semkey], 16)
                elif o.sig:
                    ins.then_inc(sems[name], 1)
            if name == "sp":
                for k, v in dcnt.items():
                    if seen.get(k, 0) < v:
                        eng.wait_ge(sems[k], v)

        with nc.Block() as block:
            @block.tensor
            def _(e):
                emit("pe", e)

            @block.scalar
            def _(e):
                emit("act", e)

            @block.vector
            def _(e):
                emit("dve", e)

            @block.gpsimd
            def _(e):
                emit("pool", e)

            @block.sync
            def _(e):
                emit("sp", e)
        self.stack.close()


D = 1024
KC = 8
CTXN = 256
GRID_W = 64
NA_KH, NA_KW = 8, 16
GLA_DK, GLA_DV, GLA_H = 128, 256, 4
NA_DH, NA_H = 64, 16
DEPTH = 4
ALPHA = (2 * DEPTH) ** 0.25
LN_EPS = 1e-5
NORM_EPS = 1e-6
NEG = -30000.0


def na_plan(H):
    rs = lambda r: min(max(r - NA_KH // 2, 0), H - NA_KH)
    tiles = []
    for t in range(H // 2):
        rows = [r for r in range(H) if any(rs(r) <= 2 * t + a < rs(r) + NA_KH for a in (0, 1))]
        assert rows == list(range(rows[0], rows[-1] + 1))
        sig = [(r - 2 * t, rs(r) <= 2 * t < rs(r) + NA_KH, rs(r) <= 2 * t + 1 < rs(r) + NA_KH) for r in rows]
        tiles.append((rows[0], sig))
    table, offs = [], {}
    for t in sorted(range(len(tiles)), key=lambda t: -len(tiles[t][1])):
        sig = tiles[t][1]
        off = None
        for o in range(0, len(table) - len(sig) + 1):
            if table[o:o + len(sig)] == sig:
                off = o
                break
        if off is None:
            off = len(table)
            table.extend(sig)
        offs[t] = off
    return tiles, table, offs


def make_consts(H):
    LAT = H * GRID_W
    idx = np.arange(128)
    s, t = idx[:, None], idx[None, :]
    same = (s // 64) == (t // 64)
    cst = np.zeros((128, 1152), np.float32)
    cst[:, 0:128] = np.eye(128, dtype=np.float32)
    sc = np.float32(-1.0 / 16.0)
    cst[:, 128:256] = (same & (s <= t)) * sc
    cst[:, 256:384] = (same & (s > t)) * sc
    cst[:, 384:512] = (same & (s >= t)) * sc
    cst[:, 512:640] = (same & (s < t)) * sc
    cst[:, 640:768] = (same & (s <= t))
    cst[:, 768:896] = (same & (s > t))
    rot = np.zeros((128, 128), np.float32)
    for m in range(64):
        rot[m + 64, m] = -1.0
        rot[m, m + 64] = 1.0
    cst[:, 896:1024] = rot
    cst[:, 1024:1152] = 1.0
    pos = np.arange(LAT)
    row = (pos // GRID_W).astype(np.float32)
    col = (pos % GRID_W).astype(np.float32)
    inv = np.power(np.float32(10000.0), -np.arange(32, dtype=np.float32) / np.float32(32)).astype(np.float32)
    ang = np.concatenate([row[:, None] * inv, col[:, None] * inv], -1).astype(np.float32)
    cos, sin = np.cos(ang).astype(np.float32), np.sin(ang).astype(np.float32)
    rope = np.zeros((128, 2, LAT), np.float32)
    rope[:, 0, :] = np.concatenate([cos.T, cos.T], 0)
    rope[:, 1, :] = np.concatenate([sin.T, sin.T], 0)
    return cst, rope


def make_natab(rpb, H):
    _, table, _ = na_plan(H)
    NB = len(table)
    nl, nh = rpb.shape[0], rpb.shape[1]
    a = np.arange(2)[:, None, None]
    kc = np.arange(64)[None, :, None]
    c = np.arange(64)[None, None, :]
    cs = np.clip(c - NA_KW // 2, 0, GRID_W - NA_KW)
    colok = (kc >= cs) & (kc < cs + NA_KW)
    dc = np.clip(kc - c + NA_KW - 1, 0, 2 * NA_KW - 2)
    out = np.full((nl, nh, 128, NB * 64), NEG, np.float32)
    for b, (u, v0, v1) in enumerate(table):
        d = a - u + NA_KH - 1
        vrow = np.array([v0, v1])[:, None, None]
        ok = np.broadcast_to(vrow & colok & (d >= 0) & (d <= 2 * NA_KH - 2), (2, 64, 64))
        dd = np.broadcast_to(np.clip(d, 0, 2 * NA_KH - 2), (2, 64, 64))
        dcc = np.broadcast_to(dc, (2, 64, 64))
        vals = rpb[:, :, dd, dcc]
        blk = np.where(ok[None, None], vals, np.float32(NEG)).reshape(nl, nh, 128, 64)
        out[:, :, :, b * 64:(b + 1) * 64] = blk
    return out


class _Stop(Exception):
    pass


def build(H=32, n_layers=DEPTH, stop=None):
    LAT = H * GRID_W
    TOK = CTXN + LAT
    NT = TOK // 128
    blocks = [(0, CTXN)] + [(CTXN + 512 * i, 512) for i in range(LAT // 512)]
    tiles_na, table_na, offs_na = na_plan(H)
    NBC = len(table_na) * 64

    nc = bass.Bass("TRN2", target_bir_lowering=False)
    P = Prog(nc)
    din = lambda name, shape: nc.dram_tensor(name, list(shape), F32, kind="ExternalInput").ap()
    xin = din("xin", [TOK, D])
    cc_d = din("cc", [128, KC, 2])
    ada_w = din("ada_w", [DEPTH, D, 3 * D])
    adab_d = din("adab", [128, DEPTH * 24])
    ln_g = din("ln_g", [DEPTH, D])
    ln_b = din("ln_b", [DEPTH, D])
    w_out = din("w_out", [DEPTH, D, D])
    gla_w_in = din("gla_w_in", [2, D, 3 * D])
    gla_w1 = din("gla_dec_w1", [2, 2, D, 16])
    gla_w2 = din("gla_dec_w2", [2, 2, 16, 512])
    gla_b = din("gla_dec_b", [2, 2, 512])
    gla_ng = din("gla_norm_g", [2, GLA_DV])
    na_w_in = din("na_w_in", [2, D, 4 * D])
    natab = din("natab", [2, NA_H, 128, NBC])
    cst_d = din("cst", [128, 1152])
    rope_d = din("rope", [128, 2, LAT])
    out = nc.dram_tensor("out", [LAT, D], F32, kind="ExternalOutput").ap()
    xs = [nc.dram_tensor("xs%d" % i, [TOK, D], F32).ap() for i in range(2)]

    hT = P.sb("hT", [128, KC, TOK], BF16)
    UT = P.sb("UT", [128, KC, TOK], BF16)
    cst = P.sb("cstb", [128, 1152], F32)
    ident = cst[:, 0:128]
    A_le, A_gt, A_ge, A_lt = (cst[:, 128 + 128 * i:256 + 128 * i] for i in range(4))
    A_gtge = cst[:, 256:512]
    M_f, M_b = cst[:, 640:768], cst[:, 768:896]
    rot = cst[:, 896:1024]
    onesf = cst[:, 1024:1152]
    identb = P.sb("identb", [128, 128], BF16)
    onesb = P.sb("onesb", [128, 64], BF16)
    adaW = P.sb("adaW", [128, 2, KC, 128], F32)
    ccs = P.sb("ccs", [128, KC, 2], F32)
    cct = P.sb("cct", [128, KC, 2], F32)
    adab = P.sb("adabs", [128, DEPTH * 24], F32)
    mcol = P.sb("mcol", [128, 2, 24, 2], F32)
    sc1 = P.sb("sc1", [128, 2, KC, 2], F32)
    gdiag = P.sb("gdiag", [128, 2, 128], F32)
    gtb = P.sb("gtb", [128, 2, D], F32)
    lnb = P.sb("lnb", [128, 2, D], F32)
    xt = P.sb("xt", [128, 2, D], F32)
    t1 = P.sb("t1", [128, 2, D], F32)
    bst = P.sb("bst", [128, 2, 2, 6], F32)
    mv = P.sb("mv", [128, 2, 2], F32)
    sm = P.sb("sm", [128, 8], F32)
    smo = P.sb("smo", [128, 2, 4], F32)
    ARENA = 82 * 1024
    arena = P.sb("arena", [128, ARENA // 2], BF16)
    ps = [P.ps("ps%d" % i) for i in range(8)]
    psk = lambda i: ("ps", i)

    class Carver:
        def __init__(self):
            self.off = 0

        def __call__(self, dtype, *free):
            n = int(np.prod(free))
            nb = n * (4 if dtype == F32 else 2)
            nb = (nb + 63) // 64 * 64
            assert self.off + nb <= ARENA, (self.off, nb)
            a = arena[:, self.off // 2:(self.off + nb) // 2]
            self.off += nb
            if dtype == F32:
                a = a.bitcast(F32)
            a = a[:, 0:n]
            if len(free) == 2:
                a = a.rearrange("p (a b) -> p a b", a=free[0])
            elif len(free) == 3:
                a = a.rearrange("p (a b c) -> p a b c", a=free[0], b=free[1])
            return a

    def mm(out_, lhsT, rhs, start, stop, reads, writes):
        P.op("pe", lambda e: e.matmul(out_, lhsT=lhsT, rhs=rhs, start=start, stop=stop), reads, writes)

    def tr(out_, in_, idn, reads, writes):
        P.op("pe", lambda e: e.transpose(out_, in_, idn), reads, writes)

    def act(out_, in_, func, reads, writes, scale=1.0, bias=0.0, accum=None):
        if accum is None:
            P.op("act", lambda e: e.activation(out=out_, in_=in_, func=func, bias=bias, scale=scale), reads, writes)
        else:
            P.op("act", lambda e: e.activation(out=out_, in_=in_, func=func, bias=bias, scale=scale, accum_out=accum),
                 reads, writes)

    def tt(eng, out_, in0, in1, op, reads, writes):
        P.op(eng, lambda e: e.tensor_tensor(out=out_, in0=in0, in1=in1, op=op), reads, writes)

    def tsc(eng, out_, in0, s1, s2, op0, op1, reads, writes):
        if s2 is None:
            P.op(eng, lambda e: e.tensor_scalar(out=out_, in0=in0, scalar1=s1, scalar2=None, op0=op0), reads, writes)
        else:
            P.op(eng, lambda e: e.tensor_scalar(out=out_, in0=in0, scalar1=s1, scalar2=s2, op0=op0, op1=op1),
                 reads, writes)

    def stt(out_, in0, scalar, in1, op0, op1, reads, writes):
        P.op("dve", lambda e: e.scalar_tensor_tensor(out=out_, in0=in0, scalar=scalar, in1=in1, op0=op0, op1=op1),
             reads, writes)

    def cp(eng, out_, in_, reads, writes):
        if eng == "act":
            act(out_, in_, AF.Copy, reads, writes)
        else:
            P.op(eng, lambda e: e.tensor_copy(out=out_, in_=in_), reads, writes)

    def barrier():
        last = {}
        for e in P.ENG:
            for o in P.streams[e]:
                if o.fn is not None:
                    last[P._ck(o)] = o
        for e in P.ENG:
            b = _Op(e, None, False, None)
            b.deps = tuple(d for d in last.values() if not (d.eng == e and not d.is_dma))
            P.nops += 1
            b.n = P.nops
            P.streams[e].append(b)
        P.res = {}

    def stage(name):
        if stop == name:
            raise _Stop()

    tiles_of = lambda a, n: [("hT", t) for t in range(a // 128, (a + n) // 128)]

    P.dma("sp", cst[:, :], cst_d, writes=[("cst",)])
    P.dma("sp", ccs[:, :, :], cc_d, writes=[("ccs",)])
    P.dma("sp", adab[:, :], adab_d, writes=[("adab",)])
    cp("dve", identb[:, :], ident, [("cst",)], [("identb",)])
    P.op("pool", lambda e: e.memset(onesb[:, :], 1.0), writes=[("onesb",)])
    act(cct[:, :, :], ccs[:, :, :], AF.Exp, [("ccs",)], [("cct",)], scale=-1.0)
    tsc("dve", cct[:, :, :], cct[:, :, :], 1.0, None, ALU.add, None, [("cct",)], [("cct",)])
    P.op("dve", lambda e: e.reciprocal(out=cct[:, :, :], in_=cct[:, :, :]), [("cct",)], [("cct",)])
    tt("dve", ccs[:, :, :], ccs[:, :, :], cct[:, :, :], ALU.mult, [("ccs",), ("cct",)], [("ccs",)])

    ada_cnt = [0]

    def ada_cols(l, js, par):
        for j in js:
            b = ada_cnt[0] % 2
            ada_cnt[0] += 1
            P.dma("sp", adaW[:, b, :, :], ada_w[l, :, j * 128:(j + 1) * 128].rearrange("(k p) c -> p k c", p=128),
                  writes=[("adaW", b)])
            for k in range(KC):
                mm(ps[7][:, 2 * j:2 * j + 2], adaW[:, b, k, :], ccs[:, k, :], k == 0, k == KC - 1,
                   [("adaW", b), ("ccs",)], [psk(7)])
            tsc("dve", mcol[:, par, j, :], ps[7][:, 2 * j:2 * j + 2], adab[:, l * 24 + j:l * 24 + j + 1], None,
                ALU.add, None, [psk(7), ("adab",)], [("mcol", par, j)])

    def ada_mod(l, par):
        ada_cols(l, range(16), par)
        tsc("dve", sc1[:, par, :, :], mcol[:, par, 8:16, :], 1.0, None, ALU.add, None,
            [("mcol", par)], [("sc1", par)])

    def ada_gate(l, par):
        ada_cols(l, range(16, 24), par)
        for cond in range(2):
            for j in range(8):
                gb = j % 2
                tsc("dve", gdiag[:, gb, :], ident, mcol[:, par, 16 + j, cond:cond + 1], None, ALU.mult, None,
                    [("cst",), ("mcol", par, 16 + j)], [("gdiag", gb)])
                mm(ps[7][:, (j % 4) * 128:(j % 4 + 1) * 128], onesf, gdiag[:, gb, :], True, True,
                   [("cst",), ("gdiag", gb)], [psk(7)])
                if j % 4 == 3:
                    cp("act", gtb[:, cond, (j // 4) * 512:(j // 4 + 1) * 512], ps[7][:, :], [psk(7)],
                       [("gtb", cond, j // 4)])

    def emit_hT(t, src, srckey, par):
        cond = 1 if t < 2 else 0
        for half in range(2):
            pb = 5 + half
            for kk in range(4):
                k = half * 4 + kk
                tr(ps[pb][:, kk * 128:(kk + 1) * 128], src[:, k * 128:(k + 1) * 128], ident,
                   [srckey, ("cst",)], [psk(pb)])
            for kk in range(4):
                k = half * 4 + kk
                o_ = hT[:, k, t * 128:(t + 1) * 128]
                i_ = ps[pb][:, kk * 128:(kk + 1) * 128]
                rd = [psk(pb), ("sc1", par), ("mcol", par)]
                tsc("dve", o_, i_, sc1[:, par, k, cond:cond + 1], mcol[:, par, k, cond:cond + 1],
                    ALU.mult, ALU.add, rd, [("hT", t, k)])

    def gla_layer(l, j):
        C = Carver()
        Wqk = C(BF16, KC, 256)
        Wv = C(BF16, KC, 256)
        qT = C(BF16, TOK)
        kT = C(BF16, TOK)
        ktok = C(BF16, NT, 128)
        vtok = C(BF16, NT, 256)
        obw = C(BF16, max(KC * D, NT * 512))
        ob = obw[:, 0:NT * 512].bitcast(F32).rearrange("p (a b) -> p a b", a=NT)
        wout_l = obw[:, 0:KC * D].rearrange("p (k c) -> p k c", k=KC)
        lowT = C(BF16, TOK)
        w1 = C(BF16, KC, 32)
        w2a = C(BF16, 512)
        normg = C(F32, 256)
        ropeb = C(F32, 2, 512)
        qraw = C(F32, 512)
        tb_ = C(F32, 512)
        e1 = C(F32, 2, 128)
        sp_ = C(F32, 2, 128)
        eq = C(F32, 3, 128)
        ek = C(F32, 2, 128)
        edec = C(F32, 3, 2)
        et = C(F32, 2, 128)
        qd = C(BF16, 3, 128)
        kd = C(BF16, 2, 128)
        kend = C(BF16, 2, 2, 128)
        attT = C(BF16, 2, 128)
        Sa = C(F32, 2, 256)
        Sb = C(BF16, 2, 256)
        eg = C(F32, 256)
        g2 = C(F32, 3, 256)
        of = C(F32, 256)
        ub = C(BF16, 2, 256)

        wsrc = gla_w_in[j]
        for z in range(2):
            P.dma("pool", w1[:, :, z * 16:(z + 1) * 16], gla_w1[j, z].rearrange("(k p) r -> p k r", p=128),
                  writes=[("w1", z)], semkey=("w1",))
            P.dma("pool", w2a[z * 32:z * 32 + 16, :], gla_w2[j, z], writes=[("w2a", z, 0)], semkey=("w2a",))
            P.dma("pool", w2a[z * 32 + 16:z * 32 + 17, :], gla_b[j, z:z + 1, :], writes=[("w2a", z, 1)],
                  semkey=("w2a",))
        P.dma("sp", normg[:, :], gla_ng[j].partition_broadcast(128), writes=[("normg",)])
        P.op("pool", lambda e: e.memset(lowT[0:64, :], 1.0), writes=[("lowT",)])
        P.op("pool", lambda e: e.memset(kend[:, :, :, :], 0.0), writes=[("kend",)])
        for (a, n) in blocks:
            for z in range(2):
                for k in range(KC):
                    mm(ps[0][z * 32:z * 32 + 16, 0:n], w1[:, k, z * 16:(z + 1) * 16], hT[:, k, a:a + n], k == 0,
                       k == KC - 1, [("w1", z)] + tiles_of(a, n), [psk(0)])
                cp("act", lowT[z * 32:z * 32 + 16, a:a + n], ps[0][z * 32:z * 32 + 16, 0:n], [psk(0)],
                   [("lowT", z, a)])

        stage("S4")
        for h in range(GLA_H):
            P.dma("pool", Wqk[:, :, 0:128], wsrc[:, h * 128:(h + 1) * 128].rearrange("(k p) c -> p k c", p=128),
                  writes=[("Wqk", 0)])
            P.dma("pool", Wqk[:, :, 128:256],
                  wsrc[:, 512 + h * 128:512 + (h + 1) * 128].rearrange("(k p) c -> p k c", p=128),
                  writes=[("Wqk", 1)])
            P.dma("pool", Wv[:, :, :],
                  wsrc[:, 1024 + h * 256:1024 + (h + 1) * 256].rearrange("(k p) c -> p k c", p=128),
                  writes=[("Wv",)])
            for (a, n) in blocks:
                if a >= CTXN:
                    P.dma("sp", ropeb[:, :, :], rope_d[:, :, a - CTXN:a - CTXN + 512], writes=[("ropeb",)])
                for w in range(2):
                    dst, dk_ = (qT, "qT") if w == 0 else (kT, "kT")
                    pb = w
                    for k in range(KC):
                        mm(ps[pb][:, 0:n], Wqk[:, k, w * 128:(w + 1) * 128], hT[:, k, a:a + n], k == 0, k == KC - 1,
                           [("Wqk", w)] + tiles_of(a, n), [psk(pb)])
                    scl = GLA_DK ** -0.5 if w == 0 else 1.0
                    if a < CTXN:
                        act(dst[:, a:a + n], ps[pb][:, 0:n], AF.Copy, [psk(pb)], [(dk_, a)], scale=scl)
                    else:
                        act(qraw[:, :], ps[pb][:, 0:n], AF.Copy, [psk(pb)], [("qraw",)], scale=scl)
                        mm(ps[2][:, :], rot, qraw[:, :], True, True, [("cst",), ("qraw",)], [psk(2)])
                        tt("pool", qraw[:, :], qraw[:, :], ropeb[:, 0, :], ALU.mult, [("qraw",), ("ropeb",)],
                           [("qraw",)])
                        tt("dve", tb_[:, :], ps[2][:, :], ropeb[:, 1, :], ALU.mult, [psk(2), ("ropeb",)], [("tb",)])
                        tt("dve", dst[:, a:a + n], qraw[:, :], tb_[:, :], ALU.add, [("qraw",), ("tb",)], [(dk_, a)])
            for t0 in range(0, NT, 4):
                nt = min(4, NT - t0)
                pv = ps[2][:, 0:256].bitcast(BF16)
                for i in range(nt):
                    tr(pv[:, i * 128:(i + 1) * 128], kT[:, (t0 + i) * 128:(t0 + i + 1) * 128], identb[:, :],
                       [("kT",), ("identb",)], [psk(2)])
                cp("act", ktok[:, t0:t0 + nt, :], pv[:, 0:nt * 128].rearrange("p (a b) -> p a b", a=nt), [psk(2)],
                   [("ktok", t0)])
            for t0 in range(0, NT, 2):
                pb = 3
                for i in range(2):
                    t = t0 + i
                    for k in range(KC):
                        mm(ps[pb][:, i * 256:(i + 1) * 256], hT[:, k, t * 128:(t + 1) * 128], Wv[:, k, :], k == 0,
                           k == KC - 1, [("Wv",), ("hT", t)], [psk(pb)])
                cp("act", vtok[:, t0:t0 + 2, :], ps[pb][:, :].rearrange("p (a b) -> p a b", a=2), [psk(pb)],
                   [("vtok", t0)])
            P.dma("pool", Wv[:, :, :],
                  wsrc[:, 2048 + h * 256:2048 + (h + 1) * 256].rearrange("(k p) c -> p k c", p=128),
                  writes=[("Wv",)])
            stage("S5")

            def run_pass(order, fwd):
                z = 0 if fwd else 1
                zr = slice(z * 32, z * 32 + 17)
                nO = len(order)
                corder = (0, 1) if fwd else (1, 0)
                P.op("pool", lambda e: e.memset(Sa[:, 0, :], 0.0), writes=[("Sa", 0)])
                P.op("pool", lambda e: e.memset(Sb[:, 0, :], 0.0), writes=[("Sb", 0)])
                tks = lambda p: slice(order[p] * 128, (order[p] + 1) * 128)
                rlg = ps[4][:, 0:128]
                rt = ps[5][:, 256:384]
                ra = ps[6][:, 0:128]
                rg = ps[7][:, 0:256]

                def P0pe(p):
                    b = p % 2
                    mm(rlg, lowT[zr, tks(p)], w2a[zr, h * 128:(h + 1) * 128], True, True, [("lowT",), ("w2a", z)],
                       [psk(4)])

                def P0act(p):
                    b = p % 2
                    act(e1[:, b, :], rlg, AF.Exp, [psk(4)], [("e1", b)], scale=-1.0)
                    act(sp_[:, b, :], e1[:, b, :], AF.Ln, [("e1", b)], [("sp", b)], bias=1.0)

                def P1pe(p):
                    b = p % 2
                    if fwd:
                        mm(ps[5][:, 0:128], sp_[:, b, :], A_le, True, True, [("sp", b), ("cst",)], [psk(5)])
                        mm(rt, A_gt, sp_[:, b, :], True, True, [("sp", b), ("cst",)], [psk(5)])
                    else:
                        mm(ps[5][:, 0:256], sp_[:, b, :], A_gtge, True, True, [("sp", b), ("cst",)], [psk(5)])
                        mm(rt, A_lt, sp_[:, b, :], True, True, [("sp", b), ("cst",)], [psk(5)])

                def P1act(p):
                    b, b3 = p % 2, p % 3
                    act(eq[:, b3, :], ps[5][:, 0:128], AF.Exp, [psk(5)], [("eq", b3)])
                    if fwd:
                        act(ek[:, b, :], ps[5][:, 0:128], AF.Exp, [psk(5)], [("ek", b)], scale=-1.0)
                    else:
                        act(ek[:, b, :], ps[5][:, 128:256], AF.Exp, [psk(5)], [("ek", b)], scale=-1.0)
                        act(edec[:, b3, :], ps[5][:, 128:256:64], AF.Exp, [psk(5)], [("edec", b3)])
                    act(et[:, b, :], rt, AF.Exp, [psk(5)], [("et", b)])

                def P1mul(p):
                    t, b, b3 = order[p], p % 2, p % 3
                    tt("pool", qd[:, b3, :], qT[:, tks(p)], eq[:, b3, :], ALU.mult, [("qT",), ("eq", b3)], [("qd", b3)])
                    for c in range(2):
                        cs = slice(c * 64, (c + 1) * 64)
                        tt("dve", kend[cs, b, c, :], ktok[cs, t, :], et[cs, b, :], ALU.mult, [("ktok",), ("et", b)],
                           [("kend", b, c)])
                    tt("pool", kd[:, b, :], kT[:, tks(p)], ek[:, b, :], ALU.mult, [("kT",), ("ek", b)], [("kd", b)])

                def P2pe(p):
                    t, b, b3 = order[p], p % 2, p % 3
                    mm(ra, kd[:, b, :], qd[:, b3, :], True, True, [("kd", b), ("qd", b3)], [psk(6)])
                    for c in range(2):
                        mm(ps[b][:, c * 256:(c + 1) * 256], kend[:, b, c, :], vtok[:, t, :], True, True,
                           [("kend", b, c), ("vtok",)], [psk(b)])

                def P2dve(p):
                    b = p % 2
                    tt("dve", attT[:, b, :], ra, M_f if fwd else M_b, ALU.mult, [psk(6), ("cst",)], [("attT", b)])

                def SWa(p):
                    t, b, b3 = order[p], p % 2, p % 3
                    pso = ps[2 + b]
                    mm(pso[:, 0:256], attT[:, b, :], vtok[:, t, :], True, False, [("attT", b), ("vtok",)],
                       [psk(2 + b)])
                    c = corder[0]
                    cs = slice(c * 64, (c + 1) * 64)
                    mm(pso[cs, 0:256], qd[:, b3, cs], Sb[:, 0, :], False, True, [("qd", b3), ("Sb", 0)], [psk(2 + b)])
                    cur = 0
                    for c in corder:
                        dsc = eq[:, b3, c * 64 + 63:c * 64 + 64] if fwd else edec[:, b3, c:c + 1]
                        nx = 1 - cur
                        stt(Sa[:, nx, :], Sa[:, cur, :], dsc, ps[b][:, c * 256:(c + 1) * 256], ALU.mult, ALU.add,
                            [("Sa", cur), ("eq", b3), ("edec", b3), psk(b)], [("Sa", nx)])
                        cp("act", Sb[:, nx, :], Sa[:, nx, :], [("Sa", nx)], [("Sb", nx)])
                        cur = nx

                def SWb(p):
                    b, b3 = p % 2, p % 3
                    c = corder[1]
                    cs = slice(c * 64, (c + 1) * 64)
                    mm(ps[2 + b][cs, 0:256], qd[:, b3, cs], Sb[:, 1, :], False, True, [("qd", b3), ("Sb", 1)],
                       [psk(2 + b)])

                def OBE(p):
                    t, b = order[p], p % 2
                    cp("act", ob[:, t, :], ps[2 + b][:, 0:256], [psk(2 + b)], [("ob", t)])

                def Gpe(p):
                    t = order[p]
                    for k in range(KC):
                        mm(rg, hT[:, k, tks(p)], Wv[:, k, :], k == 0, k == KC - 1, [("Wv",), ("hT", t)], [psk(7)])

                def Gact(p):
                    act(eg[:, :], rg, AF.Exp, [psk(7)], [("eg",)], scale=-1.0)

                def Gdve(p):
                    b3 = p % 3
                    tsc("dve", eg[:, :], eg[:, :], 1.0, None, ALU.add, None, [("eg",)], [("eg",)])
                    P.op("dve", lambda e: e.reciprocal(out=eg[:, :], in_=eg[:, :]), [("eg",)], [("eg",)])
                    tt("dve", g2[:, b3, :], rg, eg[:, :], ALU.mult, [psk(7), ("eg",)], [("g2", b3)])
                    tt("pool", g2[:, b3, :], g2[:, b3, :], normg[:, :], ALU.mult, [("g2", b3), ("normg",)],
                       [("g2", b3)])

                def FINa(p):
                    t, b = order[p], p % 2
                    tt("dve", of[:, :], ps[2 + b][:, 0:256], ob[:, t, :], ALU.add, [psk(2 + b), ("ob", t)], [("of",)])
                    act(eg[:, :], of[:, :], AF.Square, [("of",)], [("eg",), ("sm", 0)], accum=sm[:, 0:1])

                def FINb(p):
                    b, b3 = p % 2, p % 3
                    act(sm[:, 1:2], sm[:, 0:1], AF.Ln, [("sm", 0)], [("sm", 1)], scale=1.0 / GLA_DV, bias=NORM_EPS)
                    act(sm[:, 2:3], sm[:, 1:2], AF.Exp, [("sm", 1)], [("sm", 2)], scale=-0.5)
                    stt(ub[:, b, :], of[:, :], sm[:, 2:3], g2[:, b3, :], ALU.mult, ALU.mult,
                        [("of",), ("sm", 2), ("g2", b3)], [("ub", b)])

                def FIN2(p):
                    t, b = order[p], p % 2
                    pu = ps[7][:, 256:384].bitcast(BF16)
                    for i in range(2):
                        tr(pu[:, i * 128:(i + 1) * 128], ub[:, b, i * 128:(i + 1) * 128], identb[:, :],
                           [("ub", b), ("identb",)], [psk(7)])
                    cp("act", UT[:, 2 * h:2 * h + 2, tks(p)], pu.rearrange("p (a b) -> p a b", a=2), [psk(7)],
                       [("UT", t, 2 * h), ("UT", t, 2 * h + 1)])

                if fwd:
                    sched = [(SWa, 0), (P1pe, 2), (P2pe, 1), (P0pe, 3), (Gpe, 1), (P1act, 2), (P2dve, 1), (Gact, 1),
                             (P0act, 3), (P1mul, 2), (Gdve, 1), (SWb, 0), (FINa, -1), (FINb, -1), (FIN2, -2)]
                else:
                    sched = [(SWa, 0), (P1pe, 2), (P2pe, 1), (P0pe, 3), (P1act, 2), (P2dve, 1), (P0act, 3),
                             (P1mul, 2), (SWb, 0), (OBE, -1)]
                for i in range(-3, nO + 2):
                    for fn, off in sched:
                        if 0 <= i + off < nO:
                            fn(i + off)

            run_pass([1, 0] + list(range(NT - 1, 1, -1)), False)
            stage("S6")
            run_pass(list(range(NT)), True)
            stage("S7")
        P.dma("pool", wout_l, w_out[l].rearrange("(k p) c -> p k c", p=128), writes=[("ob",), ("wout",)],
              semkey=("wout",))
        return wout_l

    def na_layer(l, j, need_ctx):
        C = Carver()
        Wp = C(BF16, KC, 512)
        QT = C(BF16, TOK)
        KT = C(BF16, TOK)
        VT = C(BF16, TOK)
        SG = C(F32, TOK)
        Vtok = C(BF16, NT, 128)
        tabf = C(F32, NBC)
        Tb = C(BF16, 2, NBC)
        E = C(BF16, 6, 512)
        rec = C(F32, 512)
        uu = C(F32, 512)
        egb = C(F32, 512)
        wout_n = C(BF16, KC, D)
        wsrc = na_w_in[j]
        P.dma("pool", wout_n, w_out[l].rearrange("(k p) c -> p k c", p=128), writes=[("wout",)],
              semkey=("wout",))

        for p in range(NA_H // 2):
            for i in range(4):
                P.dma("pool", Wp[:, :, i * 128:(i + 1) * 128],
                      wsrc[:, i * D + p * 128:i * D + (p + 1) * 128].rearrange("(k p) c -> p k c", p=128),
                      writes=[("Wp", i)])
            for (a, n) in blocks:
                for i in range(4):
                    pb = i % 2
                    for k in range(KC):
                        mm(ps[pb][:, 0:n], Wp[:, k, i * 128:(i + 1) * 128], hT[:, k, a:a + n], k == 0, k == KC - 1,
                           [("Wp", i)] + tiles_of(a, n), [psk(pb)])
                    if i == 0:
                        act(QT[:, a:a + n], ps[pb][:, 0:n], AF.Copy, [psk(pb)], [("QT", a)], scale=NA_DH ** -0.5)
                    elif i == 1:
                        cp("dve", KT[:, a:a + n], ps[pb][:, 0:n], [psk(pb)], [("KT", a)])
                    elif i == 2:
                        cp("act", VT[:, a:a + n], ps[pb][:, 0:n], [psk(pb)], [("VT", a)])
                    else:
                        act(egb[:, 0:n], ps[pb][:, 0:n], AF.Exp, [psk(pb)], [("egb",)], scale=-1.0)
                        tsc("pool", egb[:, 0:n], egb[:, 0:n], 1.0, None, ALU.add, None, [("egb",)], [("egb",)])
                        P.op("dve", lambda e, n=n: e.reciprocal(out=egb[:, 0:n], in_=egb[:, 0:n]), [("egb",)],
                             [("egb",)])
                        tt("dve", SG[:, a:a + n], ps[pb][:, 0:n], egb[:, 0:n], ALU.mult, [psk(pb), ("egb",)],
                           [("SG", a)])
            for t0 in range(0, NT, 4):
                nt = min(4, NT - t0)
                pv = ps[2][:, 0:256].bitcast(BF16)
                for i in range(nt):
                    tr(pv[:, i * 128:(i + 1) * 128], VT[:, (t0 + i) * 128:(t0 + i + 1) * 128], identb[:, :],
                       [("VT",), ("identb",)], [psk(2)])
                cp("dve", Vtok[:, t0:t0 + nt, :], pv[:, 0:nt * 128].rearrange("p (a b) -> p a b", a=nt), [psk(2)],
                   [("Vtok", t0)])
            for hp in range(2):
                P.dma("sp", tabf[:, :], natab[j, 2 * p + hp], writes=[("tabf",)])
                act(Tb[:, hp, :], tabf[:, :], AF.Exp, [("tabf",)], [("Tb", hp)])
            qblocks = [(CTXN + 512 * i, 512, 8 * i) for i in range(LAT // 512)]
            if need_ctx:
                qblocks.append((0, CTXN, None))
            steps = []
            for bi, (qa, qn, r0) in enumerate(qblocks):
                po, pd = (4, 5) if bi % 2 == 0 else (6, 7)
                for hp in range(2):
                    klist = [(0, 0, qn, None), (1, 0, qn, None)]
                    if r0 is not None:
                        for t in range(H // 2):
                            lo, sig = tiles_na[t]
                            hi = lo + len(sig) - 1
                            ra, rb = max(lo, r0), min(hi, r0 + 7)
                            if ra > rb:
                                continue
                            klist.append((2 + t, (ra - r0) * 64, (rb - ra + 1) * 64, (offs_na[t] + ra - lo) * 64))
                    for ki, (kt, qo, n, tc) in enumerate(klist):
                        steps.append(dict(bi=bi, qa=qa, qn=qn, po=po, pd=pd, hp=hp, kt=kt, qo=qo, n=n, tc=tc,
                                          first=ki == 0, last=ki == len(klist) - 1,
                                          endblk=(hp == 1 and ki == len(klist) - 1)))

            def QK(si):
                st = steps[si]
                hs = slice(st["hp"] * 64, st["hp"] * 64 + 64)
                sb_, eb, n = 2 + si % 2, si % 6, st["n"]
                q0 = st["qa"] + st["qo"]
                mm(ps[sb_][:, 0:n], KT[hs, st["kt"] * 128:(st["kt"] + 1) * 128], QT[hs, q0:q0 + n], True, True,
                   [("KT",), ("QT",)], [psk(sb_)])
                act(E[:, eb, 0:n], ps[sb_][:, 0:n], AF.Exp, [psk(sb_)], [("E", eb)])
                if st["tc"] is not None:
                    tt("dve", E[:, eb, 0:n], E[:, eb, 0:n],
                       Tb[:, st["hp"], st["tc"]:st["tc"] + n], ALU.mult, [("E", eb), ("Tb", st["hp"])], [("E", eb)])

            def PV(si):
                st = steps[si]
                hs = slice(st["hp"] * 64, st["hp"] * 64 + 64)
                eb, n, qo, po, pd = si % 6, st["n"], st["qo"], st["po"], st["pd"]
                mm(ps[po][hs, qo:qo + n], Vtok[:, st["kt"], hs], E[:, eb, 0:n], st["first"], st["last"],
                   [("Vtok",), ("E", eb)], [psk(po)])
                mm(ps[pd][hs, qo:qo + n], onesb[:, :], E[:, eb, 0:n], st["first"], st["last"],
                   [("onesb",), ("E", eb)], [psk(pd)])
                if st["endblk"]:
                    qa, qn = st["qa"], st["qn"]
                    P.op("dve", lambda e: e.reciprocal(out=rec[:, 0:qn], in_=ps[pd][:, 0:qn]), [psk(pd)],
                         [("rec",)])
                    tt("dve", uu[:, 0:qn], ps[po][:, 0:qn], rec[:, 0:qn], ALU.mult, [psk(po), ("rec",)], [("uu",)])
                    tt("dve", UT[:, p, qa:qa + qn], uu[:, 0:qn], SG[:, qa:qa + qn], ALU.mult, [("uu",), ("SG",)],
                       [("UT", tq, p) for tq in range(qa // 128, (qa + qn) // 128)])

            LAG = 3
            for si in range(len(steps) + LAG):
                if si < len(steps):
                    QK(si)
                if si - LAG >= 0:
                    PV(si - LAG)
        return wout_n

    def outproj(l, need_ctx, last, wout):
        xsrc = xin if l == 0 else xs[(l - 1) % 2]
        P.dma("sp", lnb[:, 0, :], ln_g[l].partition_broadcast(128), writes=[("lnb", 0)])
        P.dma("sp", lnb[:, 1, :], ln_b[l].partition_broadcast(128), writes=[("lnb", 1)])
        tl = list(range(NT)) if need_ctx else list(range(2, NT))
        for ti, t in enumerate(tl):
            xb = ti % 2
            cond = 1 if t < 2 else 0
            tk = slice(t * 128, (t + 1) * 128)
            x_ = xt[:, xb, :]
            P.dma("sp", x_, xsrc[tk, :], reads=[("xd", l - 1, t)], writes=[("xt", xb)], semkey=("xt", xb))
            for nh in range(2):
                pb = 2 * xb + nh
                for k in range(KC):
                    mm(ps[pb][:, :], UT[:, k, tk], wout[:, k, nh * 512:(nh + 1) * 512], k == 0, k == KC - 1,
                       [("UT", t), ("wout",)], [psk(pb)])
                tt("dve", t1[:, xb, nh * 512:(nh + 1) * 512], ps[pb][:, :], gtb[:, cond, nh * 512:(nh + 1) * 512],
                   ALU.mult, [psk(pb), ("gtb", cond)], [("t1", xb, nh)])
            stt(x_, x_, float(ALPHA), t1[:, xb, :], ALU.mult, ALU.add, [("xt", xb), ("t1", xb)], [("xt", xb)])
            for nh in range(2):
                P.op("dve", lambda e, nh=nh, x_=x_, xb=xb: e.bn_stats(out=bst[:, xb, nh, :],
                                                                     in_=x_[:, nh * 512:(nh + 1) * 512]),
                     [("xt", xb)], [("bst", xb, nh)])
            P.op("dve", lambda e, xb=xb: e.bn_aggr(out=mv[:, xb, :],
                                                   in_=bst[:, xb, :, :].rearrange("p a b -> p (a b)")),
                 [("bst", xb)], [("mv", xb)])
            act(smo[:, xb, 0:1], mv[:, xb, 1:2], AF.Ln, [("mv", xb)], [("smo", xb, 0)], bias=LN_EPS)
            act(smo[:, xb, 1:2], smo[:, xb, 0:1], AF.Exp, [("smo", xb, 0)], [("smo", xb, 1)], scale=-0.5)
            tsc("dve", x_, x_, mv[:, xb, 0:1], smo[:, xb, 1:2], ALU.subtract, ALU.mult,
                [("xt", xb), ("mv", xb), ("smo", xb, 1)], [("xt", xb)])
            tt("pool", x_, x_, lnb[:, 0, :], ALU.mult, [("xt", xb), ("lnb", 0)], [("xt", xb)])
            tt("pool", x_, x_, lnb[:, 1, :], ALU.add, [("xt", xb), ("lnb", 1)], [("xt", xb)])
            if last:
                if t >= 2:
                    P.dma("sp", out[(t - 2) * 128:(t - 1) * 128, :], x_, reads=[("xt", xb)], writes=[("outd", t)],
                          semkey=("xst", xb))
            else:
                P.dma("sp", xs[l % 2][tk, :], x_, reads=[("xt", xb)], writes=[("xd", l, t)], semkey=("xst", xb))
                emit_hT(t, x_, ("xt", xb), (l + 1) % 2)

    def main():
        stage("S0")
        ada_mod(0, 0)
        stage("S1")
        for t in range(NT):
            xb = t % 2
            P.dma("sp", xt[:, xb, :], xin[t * 128:(t + 1) * 128, :], writes=[("xt", xb)], semkey=("xt", xb))
            emit_hT(t, xt[:, xb, :], ("xt", xb), 0)
        stage("S2")
        for l in range(n_layers):
            need_ctx = l < DEPTH - 1
            last = l == n_layers - 1
            ada_gate(l, l % 2)
            stage("S3")
            if not last:
                ada_mod(l + 1, (l + 1) % 2)
            if l % 2 == 0:
                wo = gla_layer(l, l // 2)
            else:
                wo = na_layer(l, l // 2, need_ctx)
            stage("S8")
            outproj(l, need_ctx, last, wo)
            if not last:
                barrier()

    try:
        main()
    except _Stop:
        P.dma("sp", out[0:128, :], xt[:, 0, :], reads=[("xt", 0)], writes=[("outd", 0)], semkey=("xst", 0))
    P.finish()
    return nc, P


_CACHE = {}


def prep_inputs(H, x, c, ctx, c_ctx, ada_w, ada_b, ln_g, ln_b, w_out, gla_w_in, gla_dec_w1, gla_dec_w2,
                gla_dec_b, gla_norm_g, na_w_in, na_rpb):
    f = lambda a: np.ascontiguousarray(np.asarray(a, dtype=np.float32))
    B = x.shape[0]
    cst, rope = make_consts(H)
    natab = make_natab(f(na_rpb), H)
    adab = f(np.asarray(ada_b).reshape(DEPTH, 24, 128).transpose(2, 0, 1).reshape(128, DEPTH * 24))
    shared = dict(ada_w=f(ada_w), adab=adab, ln_g=f(ln_g), ln_b=f(ln_b), w_out=f(w_out), gla_w_in=f(gla_w_in),
                  gla_dec_w1=f(gla_dec_w1), gla_dec_w2=f(gla_dec_w2), gla_dec_b=f(gla_dec_b),
                  gla_norm_g=f(gla_norm_g), na_w_in=f(na_w_in), natab=natab, cst=cst, rope=rope)
    maps = []
    for b in range(B):
        m = dict(shared)
        m["xin"] = f(np.concatenate([np.asarray(ctx[b]), np.asarray(x[b])], 0))
        cc = np.stack([np.asarray(c[b]).reshape(KC, 128).T, np.asarray(c_ctx).reshape(KC, 128).T], -1)
        m["cc"] = f(cc)
        maps.append(m)
    return maps


def kernel(**inputs):
    H = 32
    if "nc" not in _CACHE:
        _CACHE["nc"] = build(H, DEPTH)[0]
    nc = _CACHE["nc"]
    maps = prep_inputs(H, **inputs)
    res = run_bass_kernel_spmd(nc, maps, core_ids=list(range(len(maps))))
    return np.stack([np.asarray(r["out"], dtype=np.float32) for r in res.results], 0)
```

```python
from contextlib import ExitStack
import os

import numpy as np
import concourse.bass as bass
import concourse.mybir as mybir
from concourse.bass_utils import run_bass_kernel_spmd

F32 = mybir.dt.float32
BF16 = mybir.dt.bfloat16
AF = mybir.ActivationFunctionType
ALU = mybir.AluOpType


class _Op:
    __slots__ = ("eng", "fn", "deps", "is_dma", "semkey", "sig", "val", "n")

    def __init__(self, eng, fn, is_dma, semkey):
        self.eng, self.fn, self.is_dma, self.semkey = eng, fn, is_dma, semkey
        self.deps, self.sig, self.val, self.n = (), False, 0, 0


class _Ent:
    __slots__ = ("w", "r")

    def __init__(self):
        self.w, self.r = None, {}


class Prog:
    ENG = ("pe", "act", "dve", "pool", "sp")

    def __init__(self, nc):
        self.nc = nc
        self.stack = ExitStack()
        self.streams = {e: [] for e in self.ENG}
        self.res = {}
        self.nops = 0
        self.dma_last = {}

    def sb(self, name, shape, dtype):
        return self.stack.enter_context(self.nc.sbuf_tensor(name, list(shape), dtype))

    def ps(self, name):
        return self.stack.enter_context(self.nc.psum_tensor(name, [128, 512], F32))

    @staticmethod
    def _ck(o):
        return o.semkey if o.is_dma else o.eng

    def _conf(self, key):
        b = self.res.get(key[0])
        if not b:
            return
        n = len(key)
        for k, ent in b.items():
            m = len(k)
            if (k[:n] == key) if m >= n else (key[:m] == k):
                yield k, ent

    def _ent(self, key):
        b = self.res.setdefault(key[0], {})
        e = b.get(key)
        if e is None:
            e = b[key] = _Ent()
        return e

    def _record(self, o, reads, writes):
        reads = [k[:2] if k[0] == "ps" else k for k in reads]
        writes = [k[:2] if k[0] == "ps" else k for k in writes]
        deps = {}

        def add(d):
            if d is None or d is o:
                return
            if o.eng == "pe" and d.eng == "pe" and not d.is_dma:
                return
            if d.is_dma:
                d = self.dma_last[d.semkey]
            ck = self._ck(d)
            p = deps.get(ck)
            if p is None or d.n > p.n:
                deps[ck] = d

        for k in reads:
            for _, ent in self._conf(k):
                add(ent.w)
        for k in writes:
            for _, ent in self._conf(k):
                add(ent.w)
                for r in ent.r.values():
                    add(r)
        for k in reads:
            self._ent(k).r[self._ck(o)] = o
        for k in writes:
            n = len(k)
            b = self.res.setdefault(k[0], {})
            for kk in [kk for kk in b if len(kk) > n and kk[:n] == k]:
                del b[kk]
            e = self._ent(k)
            e.w, e.r = o, {}
        o.deps = tuple(deps.values())
        self.nops += 1
        o.n = self.nops
        self.streams[o.eng].append(o)
        return o

    def op(self, eng, fn, reads=(), writes=()):
        return self._record(_Op(eng, fn, False, None), reads, writes)

    def dma(self, q, out, in_, reads=(), writes=(), semkey=None):
        if semkey is None:
            semkey = writes[0]
        fn = lambda e: e.dma_start(out=out, in_=in_)
        o = self._record(_Op(q, fn, True, ("dma",) + tuple(semkey)), reads, writes)
        self.dma_last[o.semkey] = o
        return o

    def finish(self):
        nc = self.nc
        for s in self.streams.values():
            for o in s:
                for d in o.deps:
                    d.sig = True
        semnames = {}
        for e in self.ENG:
            cnt = 0
            for o in self.streams[e]:
                if not o.is_dma and o.sig:
                    cnt += 1
                    o.val = cnt
            semnames[e] = None
        dcnt = {}
        for e in self.ENG:
            for o in self.streams[e]:
                if o.is_dma:
                    dcnt[o.semkey] = dcnt.get(o.semkey, 0) + 16
                    o.val = dcnt[o.semkey]
                    semnames[o.semkey] = None
        sems = {}
        for i, k in enumerate(semnames):
            sems[k] = self.stack.enter_context(nc.semaphore("s%d" % i))
        self.n_sems = len(sems)
        ck = self._ck

        def emit(name, eng):
            seen = {}
            for o in self.streams[name]:
                for d in o.deps:
                    k = ck(d)
                    if seen.get(k, 0) < d.val:
                        eng.wait_ge(sems[k], d.val)
                        seen[k] = d.val
                if o.fn is None:
                    continue
                ins = o.fn(eng)
                if o.is_dma:
                    ins.then_inc(sems[o.semkey], 16)
                elif o.sig:
                    ins.then_inc(sems[name], 1)
            if name == "sp":
                for k, v in dcnt.items():
                    if seen.get(k, 0) < v:
                        eng.wait_ge(sems[k], v)

        with nc.Block() as block:
            @block.tensor
            def _(e):
                emit("pe", e)

            @block.scalar
            def _(e):
                emit("act", e)

            @block.vector
            def _(e):
                emit("dve", e)

            @block.gpsimd
            def _(e):
                emit("pool", e)

            @block.sync
            def _(e):
                emit("sp", e)
        self.stack.close()


D = 1024
KC = 8
CTXN = 256
GRID_W = 64
NA_KH, NA_KW = 8, 16
GLA_DK, GLA_DV, GLA_H = 128, 256, 4
NA_DH, NA_H = 64, 16
DEPTH = 4
ALPHA = (2 * DEPTH) ** 0.25
LN_EPS = 1e-5
NORM_EPS = 1e-6
NEG = -30000.0


def na_plan(H):
    rs = lambda r: min(max(r - NA_KH // 2, 0), H - NA_KH)
    tiles = []
    for t in range(H // 2):
        rows = [r for r in range(H) if any(rs(r) <= 2 * t + a < rs(r) + NA_KH for a in (0, 1))]
        assert rows == list(range(rows[0], rows[-1] + 1))
        sig = [(r - 2 * t, rs(r) <= 2 * t < rs(r) + NA_KH, rs(r) <= 2 * t + 1 < rs(r) + NA_KH) for r in rows]
        tiles.append((rows[0], sig))
    table, offs = [], {}
    for t in sorted(range(len(tiles)), key=lambda t: -len(tiles[t][1])):
        sig = tiles[t][1]
        off = None
        for o in range(0, len(table) - len(sig) + 1):
            if table[o:o + len(sig)] == sig:
                off = o
                break
        if off is None:
            off = len(table)
            table.extend(sig)
        offs[t] = off
    return tiles, table, offs


def make_consts(H):
    LAT = H * GRID_W
    idx = np.arange(128)
    s, t = idx[:, None], idx[None, :]
    same = (s // 64) == (t // 64)
    cst = np.zeros((128, 1152), np.float32)
    cst[:, 0:128] = np.eye(128, dtype=np.float32)
    sc = np.float32(-1.0 / 16.0)
    cst[:, 128:256] = (same & (s <= t)) * sc
    cst[:, 256:384] = (same & (s > t)) * sc
    cst[:, 384:512] = (same & (s >= t)) * sc
    cst[:, 512:640] = (same & (s < t)) * sc
    cst[:, 640:768] = (same & (s <= t))
    cst[:, 768:896] = (same & (s > t))
    rot = np.zeros((128, 128), np.float32)
    for m in range(64):
        rot[m + 64, m] = -1.0
        rot[m, m + 64] = 1.0
    cst[:, 896:1024] = rot
    cst[:, 1024:1152] = 1.0
    pos = np.arange(LAT)
    row = (pos // GRID_W).astype(np.float32)
    col = (pos % GRID_W).astype(np.float32)
    inv = np.power(np.float32(10000.0), -np.arange(32, dtype=np.float32) / np.float32(32)).astype(np.float32)
    ang = np.concatenate([row[:, None] * inv, col[:, None] * inv], -1).astype(np.float32)
    cos, sin = np.cos(ang).astype(np.float32), np.sin(ang).astype(np.float32)
    rope = np.zeros((128, 2, LAT), np.float32)
    rope[:, 0, :] = np.concatenate([cos.T, cos.T], 0)
    rope[:, 1, :] = np.concatenate([sin.T, sin.T], 0)
    return cst, rope


def make_natab(rpb, H):
    _, table, _ = na_plan(H)
    NB = len(table)
    nl, nh = rpb.shape[0], rpb.shape[1]
    a = np.arange(2)[:, None, None]
    kc = np.arange(64)[None, :, None]
    c = np.arange(64)[None, None, :]
    cs = np.clip(c - NA_KW // 2, 0, GRID_W - NA_KW)
    colok = (kc >= cs) & (kc < cs + NA_KW)
    dc = np.clip(kc - c + NA_KW - 1, 0, 2 * NA_KW - 2)
    out = np.full((nl, nh, 128, NB * 64), NEG, np.float32)
    for b, (u, v0, v1) in enumerate(table):
        d = a - u + NA_KH - 1
        vrow = np.array([v0, v1])[:, None, None]
        ok = np.broadcast_to(vrow & colok & (d >= 0) & (d <= 2 * NA_KH - 2), (2, 64, 64))
        dd = np.broadcast_to(np.clip(d, 0, 2 * NA_KH - 2), (2, 64, 64))
        dcc = np.broadcast_to(dc, (2, 64, 64))
        vals = rpb[:, :, dd, dcc]
        blk = np.where(ok[None, None], vals, np.float32(NEG)).reshape(nl, nh, 128, 64)
        out[:, :, :, b * 64:(b + 1) * 64] = blk
    return out


class _Stop(Exception):
    pass


def build(H=32, n_layers=DEPTH, stop=None):
    LAT = H * GRID_W
    TOK = CTXN + LAT
    NT = TOK // 128
    blocks = [(0, CTXN)] + [(CTXN + 512 * i, 512) for i in range(LAT // 512)]
    tiles_na, table_na, offs_na = na_plan(H)
    NBC = len(table_na) * 64

    nc = bass.Bass("TRN2", target_bir_lowering=False)
    P = Prog(nc)
    din = lambda name, shape: nc.dram_tensor(name, list(shape), F32, kind="ExternalInput").ap()
    xin = din("xin", [TOK, D])
    cc_d = din("cc", [128, KC, 2])
    ada_w = din("ada_w", [DEPTH, D, 3 * D])
    adab_d = din("adab", [128, DEPTH * 24])
    ln_g = din("ln_g", [DEPTH, D])
    ln_b = din("ln_b", [DEPTH, D])
    w_out = din("w_out", [DEPTH, D, D])
    gla_w_in = din("gla_w_in", [2, D, 3 * D])
    gla_w1 = din("gla_dec_w1", [2, 2, D, 16])
    gla_w2 = din("gla_dec_w2", [2, 2, 16, 512])
    gla_b = din("gla_dec_b", [2, 2, 512])
    gla_ng = din("gla_norm_g", [2, GLA_DV])
    na_w_in = din("na_w_in", [2, D, 4 * D])
    natab = din("natab", [2, NA_H, 128, NBC])
    cst_d = din("cst", [128, 1152])
    rope_d = din("rope", [128, 2, LAT])
    out = nc.dram_tensor("out", [LAT, D], F32, kind="ExternalOutput").ap()
    xs = [nc.dram_tensor("xs%d" % i, [TOK, D], F32).ap() for i in range(2)]

    hT = P.sb("hT", [128, KC, TOK], BF16)
    UT = P.sb("UT", [128, KC, TOK], BF16)
    cst = P.sb("cstb", [128, 1152], F32)
    ident = cst[:, 0:128]
    A_le, A_gt, A_ge, A_lt = (cst[:, 128 + 128 * i:256 + 128 * i] for i in range(4))
    A_gtge = cst[:, 256:512]
    M_f, M_b = cst[:, 640:768], cst[:, 768:896]
    rot = cst[:, 896:1024]
    onesf = cst[:, 1024:1152]
    identb = P.sb("identb", [128, 128], BF16)
    onesb = P.sb("onesb", [128, 64], BF16)
    adaW = P.sb("adaW", [128, 2, KC, 128], F32)
    ccs = P.sb("ccs", [128, KC, 2], F32)
    cct = P.sb("cct", [128, KC, 2], F32)
    adab = P.sb("adabs", [128, DEPTH * 24], F32)
    mcol = P.sb("mcol", [128, 2, 24, 2], F32)
    sc1 = P.sb("sc1", [128, 2, KC, 2], F32)
    gdiag = P.sb("gdiag", [128, 2, 128], F32)
    gtb = P.sb("gtb", [128, 2, D], F32)
    lnb = P.sb("lnb", [128, 2, D], F32)
    xt = P.sb("xt", [128, 2, D], F32)
    t1 = P.sb("t1", [128, 2, D], F32)
    bst = P.sb("bst", [128, 2, 2, 6], F32)
    mv = P.sb("mv", [128, 2, 2], F32)
    sm = P.sb("sm", [128, 8], F32)
    smo = P.sb("smo", [128, 2, 4], F32)
    ARENA = 86 * 1024
    arena = P.sb("arena", [128, ARENA // 2], BF16)
    ps = [P.ps("ps%d" % i) for i in range(8)]
    psk = lambda i: ("ps", i)

    class Carver:
        def __init__(self):
            self.off = 0

        def __call__(self, dtype, *free):
            n = int(np.prod(free))
            nb = n * (4 if dtype == F32 else 2)
            nb = (nb + 63) // 64 * 64
            assert self.off + nb <= ARENA, (self.off, nb)
            a = arena[:, self.off // 2:(self.off + nb) // 2]
            self.off += nb
            if dtype == F32:
                a = a.bitcast(F32)
            a = a[:, 0:n]
            if len(free) == 2:
                a = a.rearrange("p (a b) -> p a b", a=free[0])
            elif len(free) == 3:
                a = a.rearrange("p (a b c) -> p a b c", a=free[0], b=free[1])
            return a

    def mm(out_, lhsT, rhs, start, stop, reads, writes):
        P.op("pe", lambda e: e.matmul(out_, lhsT=lhsT, rhs=rhs, start=start, stop=stop), reads, writes)

    def tr(out_, in_, idn, reads, writes):
        P.op("pe", lambda e: e.transpose(out_, in_, idn), reads, writes)

    def act(out_, in_, func, reads, writes, scale=1.0, bias=0.0, accum=None):
        if accum is None:
            P.op("act", lambda e: e.activation(out=out_, in_=in_, func=func, bias=bias, scale=scale), reads, writes)
        else:
            P.op("act", lambda e: e.activation(out=out_, in_=in_, func=func, bias=bias, scale=scale, accum_out=accum),
                 reads, writes)

    def tt(eng, out_, in0, in1, op, reads, writes):
        P.op(eng, lambda e: e.tensor_tensor(out=out_, in0=in0, in1=in1, op=op), reads, writes)

    def tsc(eng, out_, in0, s1, s2, op0, op1, reads, writes):
        if s2 is None:
            P.op(eng, lambda e: e.tensor_scalar(out=out_, in0=in0, scalar1=s1, scalar2=None, op0=op0), reads, writes)
        else:
            P.op(eng, lambda e: e.tensor_scalar(out=out_, in0=in0, scalar1=s1, scalar2=s2, op0=op0, op1=op1),
                 reads, writes)

    def stt(out_, in0, scalar, in1, op0, op1, reads, writes):
        P.op("dve", lambda e: e.scalar_tensor_tensor(out=out_, in0=in0, scalar=scalar, in1=in1, op0=op0, op1=op1),
             reads, writes)

    def cp(eng, out_, in_, reads, writes):
        if eng == "act":
            act(out_, in_, AF.Copy, reads, writes)
        else:
            P.op(eng, lambda e: e.tensor_copy(out=out_, in_=in_), reads, writes)

    def barrier():
        last = {}
        for e in P.ENG:
            for o in P.streams[e]:
                if o.fn is not None:
                    last[P._ck(o)] = o
        for e in P.ENG:
            b = _Op(e, None, False, None)
            b.deps = tuple(d for d in last.values() if not (d.eng == e and not d.is_dma))
            P.nops += 1
            b.n = P.nops
            P.streams[e].append(b)
        P.res = {}

    def stage(name):
        if stop == name:
            raise _Stop()

    tiles_of = lambda a, n: [("hT", t) for t in range(a // 128, (a + n) // 128)]

    P.dma("sp", cst[:, :], cst_d, writes=[("cst",)])
    P.dma("sp", ccs[:, :, :], cc_d, writes=[("ccs",)])
    P.dma("sp", adab[:, :], adab_d, writes=[("adab",)])
    cp("dve", identb[:, :], ident, [("cst",)], [("identb",)])
    P.op("pool", lambda e: e.memset(onesb[:, :], 1.0), writes=[("onesb",)])
    act(cct[:, :, :], ccs[:, :, :], AF.Exp, [("ccs",)], [("cct",)], scale=-1.0)
    tsc("dve", cct[:, :, :], cct[:, :, :], 1.0, None, ALU.add, None, [("cct",)], [("cct",)])
    P.op("dve", lambda e: e.reciprocal(out=cct[:, :, :], in_=cct[:, :, :]), [("cct",)], [("cct",)])
    tt("dve", ccs[:, :, :], ccs[:, :, :], cct[:, :, :], ALU.mult, [("ccs",), ("cct",)], [("ccs",)])

    coljobs = []
    for l_ in range(n_layers):
        if l_ == 0:
            coljobs += [(0, j, 0) for j in range(16)]
        coljobs += [(l_, j, l_ % 2) for j in range(16, 24)]
        if l_ + 1 < n_layers:
            coljobs += [(l_ + 1, j, (l_ + 1) % 2) for j in range(16)]
    col_next = [0]
    ada_bank = [7]
    bg = []

    def col_dma(g):
        if g < len(coljobs):
            l, j, par = coljobs[g]
            P.dma("sp", adaW[:, g % 2, :, :], ada_w[l, :, j * 128:(j + 1) * 128].rearrange("(k p) c -> p k c", p=128),
                  writes=[("adaW", g % 2)])

    def col_job():
        g = col_next[0]
        col_next[0] += 1
        l, j, par = coljobs[g]
        b = g % 2
        pa = ps[ada_bank[0]]
        for k in range(KC):
            mm(pa[:, 2 * j:2 * j + 2], adaW[:, b, k, :], ccs[:, k, :], k == 0, k == KC - 1,
               [("adaW", b), ("ccs",)], [psk(ada_bank[0])])
        tsc("dve", mcol[:, par, j, :], pa[:, 2 * j:2 * j + 2], adab[:, l * 24 + j:l * 24 + j + 1], None,
            ALU.add, None, [psk(ada_bank[0]), ("adab",)], [("mcol", par, j)])
        col_dma(g + 2)

    def sc1_job(par):
        tsc("dve", sc1[:, par, :, :], mcol[:, par, 8:16, :], 1.0, None, ALU.add, None,
            [("mcol", par)], [("sc1", par)])

    def gate_job(par, cond, half):
        pa = ps[ada_bank[0]]
        for jj in range(4):
            j = half * 4 + jj
            gb = jj % 2
            tsc("dve", gdiag[:, gb, :], ident, mcol[:, par, 16 + j, cond:cond + 1], None, ALU.mult, None,
                [("cst",), ("mcol", par, 16 + j)], [("gdiag", gb)])
            mm(pa[:, jj * 128:(jj + 1) * 128], onesf, gdiag[:, gb, :], True, True,
               [("cst",), ("gdiag", gb)], [psk(ada_bank[0])])
        cp("act", gtb[:, cond, half * 512:(half + 1) * 512], pa[:, :], [psk(ada_bank[0])], [("gtb", cond, half)])

    def queue_mod(l):
        for _ in range(16):
            bg.append(col_job)
        bg.append(lambda: sc1_job(l % 2))

    def queue_gate(l):
        for _ in range(8):
            bg.append(col_job)
        for cond in range(2):
            for half in range(2):
                bg.append(lambda cond=cond, half=half: gate_job(l % 2, cond, half))

    def bg_step(n=1):
        for _ in range(n):
            if bg:
                bg.pop(0)()

    def bg_flush():
        while bg:
            bg.pop(0)()

    def emit_hT(t, src, srckey, par):
        cond = 1 if t < 2 else 0
        for half in range(2):
            pb = 5 + half
            for kk in range(4):
                k = half * 4 + kk
                tr(ps[pb][:, kk * 128:(kk + 1) * 128], src[:, k * 128:(k + 1) * 128], ident,
                   [srckey, ("cst",)], [psk(pb)])
            for kk in range(4):
                k = half * 4 + kk
                o_ = hT[:, k, t * 128:(t + 1) * 128]
                i_ = ps[pb][:, kk * 128:(kk + 1) * 128]
                rd = [psk(pb), ("sc1", par), ("mcol", par)]
                tsc("dve", o_, i_, sc1[:, par, k, cond:cond + 1], mcol[:, par, k, cond:cond + 1],
                    ALU.mult, ALU.add, rd, [("hT", t, k)])

    def gla_layer(l, j):
        C = Carver()
        Wqk = C(BF16, KC, 256)
        Wv = C(BF16, KC, 256)
        qT = C(BF16, TOK)
        kT = C(BF16, TOK)
        ktok = C(BF16, NT, 128)
        vtok = C(BF16, NT, 256)
        obw = C(BF16, max(KC * D, NT * 512))
        ob = obw[:, 0:NT * 512].bitcast(F32).rearrange("p (a b) -> p a b", a=NT)
        wout_l = obw[:, 0:KC * D].rearrange("p (k c) -> p k c", k=KC)
        lowT = C(BF16, TOK)
        w1 = C(BF16, KC, 32)
        w2a = C(BF16, 512)
        normg = C(F32, 256)
        ropeb = C(F32, 1, 2, 512)
        qraw = C(F32, 2, 512)
        tb_ = C(F32, 1, 512)
        e1 = C(F32, 2, 128)
        sp_ = C(F32, 2, 128)
        eq = C(F32, 3, 128)
        ek = C(F32, 2, 128)
        edec = C(F32, 3, 2)
        et = C(F32, 2, 128)
        qd = C(BF16, 3, 128)
        kd = C(BF16, 2, 128)
        kend = C(BF16, 2, 2, 128)
        attT = C(BF16, 2, 128)
        Sa = C(F32, 2, 256)
        Sb = C(BF16, 2, 256)
        eg = C(F32, 256)
        g2 = C(F32, 3, 256)
        of = C(F32, 256)
        ub = C(BF16, 2, 256)

        wsrc = gla_w_in[j]
        for z in range(2):
            P.dma("pool", w1[:, :, z * 16:(z + 1) * 16], gla_w1[j, z].rearrange("(k p) r -> p k r", p=128),
                  writes=[("w1", z)], semkey=("w1",))
            P.dma("pool", w2a[z * 32:z * 32 + 16, :], gla_w2[j, z], writes=[("w2a", z, 0)], semkey=("w2a",))
            P.dma("pool", w2a[z * 32 + 16:z * 32 + 17, :], gla_b[j, z:z + 1, :], writes=[("w2a", z, 1)],
                  semkey=("w2a",))
        P.dma("sp", normg[:, :], gla_ng[j].partition_broadcast(128), writes=[("normg",)])
        P.op("pool", lambda e: e.memset(lowT[0:64, :], 1.0), writes=[("lowT",)])
        P.op("pool", lambda e: e.memset(kend[:, :, :, :], 0.0), writes=[("kend",)])
        for (a, n) in blocks:
            for z in range(2):
                for k in range(KC):
                    mm(ps[0][z * 32:z * 32 + 16, 0:n], w1[:, k, z * 16:(z + 1) * 16], hT[:, k, a:a + n], k == 0,
                       k == KC - 1, [("w1", z)] + tiles_of(a, n), [psk(0)])
                cp("act", lowT[z * 32:z * 32 + 16, a:a + n], ps[0][z * 32:z * 32 + 16, 0:n], [psk(0)],
                   [("lowT", z, a)])

        stage("S4")
        for h in range(GLA_H):
            P.dma("pool", Wqk[:, :, 0:128], wsrc[:, h * 128:(h + 1) * 128].rearrange("(k p) c -> p k c", p=128),
                  writes=[("Wqk", 0)])
            P.dma("pool", Wqk[:, :, 128:256],
                  wsrc[:, 512 + h * 128:512 + (h + 1) * 128].rearrange("(k p) c -> p k c", p=128),
                  writes=[("Wqk", 1)])
            P.dma("pool", Wv[:, :, :],
                  wsrc[:, 1024 + h * 256:1024 + (h + 1) * 256].rearrange("(k p) c -> p k c", p=128),
                  writes=[("Wv",)])
            for (a, n) in blocks:
                if a >= CTXN:
                    rb = 0
                    P.dma("sp", ropeb[:, rb, :, :], rope_d[:, :, a - CTXN:a - CTXN + 512], writes=[("ropeb", rb)])
                for w in range(2):
                    dst, dk_ = (qT, "qT") if w == 0 else (kT, "kT")
                    pb = w
                    for k in range(KC):
                        mm(ps[pb][:, 0:n], Wqk[:, k, w * 128:(w + 1) * 128], hT[:, k, a:a + n], k == 0, k == KC - 1,
                           [("Wqk", w)] + tiles_of(a, n), [psk(pb)])
                    scl = GLA_DK ** -0.5 if w == 0 else 1.0
                    if a < CTXN:
                        act(dst[:, a:a + n], ps[pb][:, 0:n], AF.Copy, [psk(pb)], [(dk_, a)], scale=scl)
                    else:
                        qr, tb2, pr = qraw[:, w, :], tb_[:, 0, :], 2 + w
                        act(qr, ps[pb][:, 0:n], AF.Copy, [psk(pb)], [("qraw", w)], scale=scl)
                        mm(ps[pr][:, :], rot, qr, True, True, [("cst",), ("qraw", w)], [psk(pr)])
                        tt("pool", qr, qr, ropeb[:, rb, 0, :], ALU.mult, [("qraw", w), ("ropeb", rb)], [("qraw", w)])
                        tt("dve", tb2, ps[pr][:, :], ropeb[:, rb, 1, :], ALU.mult, [psk(pr), ("ropeb", rb)], [("tb",)])
                        tt("dve", dst[:, a:a + n], qr, tb2, ALU.add, [("qraw", w), ("tb",)], [(dk_, a)])
            for t0 in range(0, NT, 4):
                nt = min(4, NT - t0)
                pv = ps[2][:, 0:256].bitcast(BF16)
                for i in range(nt):
                    tr(pv[:, i * 128:(i + 1) * 128], kT[:, (t0 + i) * 128:(t0 + i + 1) * 128], identb[:, :],
                       [("kT",), ("identb",)], [psk(2)])
                cp("act", ktok[:, t0:t0 + nt, :], pv[:, 0:nt * 128].rearrange("p (a b) -> p a b", a=nt), [psk(2)],
                   [("ktok", t0)])
            for t0 in range(0, NT, 2):
                pb = 3
                for i in range(2):
                    t = t0 + i
                    for k in range(KC):
                        mm(ps[pb][:, i * 256:(i + 1) * 256], hT[:, k, t * 128:(t + 1) * 128], Wv[:, k, :], k == 0,
                           k == KC - 1, [("Wv",), ("hT", t)], [psk(pb)])
                cp("act", vtok[:, t0:t0 + 2, :], ps[pb][:, :].rearrange("p (a b) -> p a b", a=2), [psk(pb)],
                   [("vtok", t0)])
            P.dma("pool", Wv[:, :, :],
                  wsrc[:, 2048 + h * 256:2048 + (h + 1) * 256].rearrange("(k p) c -> p k c", p=128),
                  writes=[("Wv",)])
            stage("S5")

            def run_pass(order, fwd):
                z = 0 if fwd else 1
                zr = slice(z * 32, z * 32 + 17)
                nO = len(order)
                corder = (0, 1) if fwd else (1, 0)
                P.op("pool", lambda e: e.memset(Sa[:, 0, :], 0.0), writes=[("Sa", 0)])
                P.op("pool", lambda e: e.memset(Sb[:, 0, :], 0.0), writes=[("Sb", 0)])
                tks = lambda p: slice(order[p] * 128, (order[p] + 1) * 128)
                rlg = ps[4][:, 0:128]
                rt = ps[5][:, 256:384]
                ra = ps[6][:, 0:128]
                rg = ps[7][:, 0:256]

                def P0pe(p):
                    b = p % 2
                    mm(rlg, lowT[zr, tks(p)], w2a[zr, h * 128:(h + 1) * 128], True, True, [("lowT",), ("w2a", z)],
                       [psk(4)])

                def P0act(p):
                    b = p % 2
                    act(e1[:, b, :], rlg, AF.Exp, [psk(4)], [("e1", b)], scale=-1.0)
                    act(sp_[:, b, :], e1[:, b, :], AF.Ln, [("e1", b)], [("sp", b)], bias=1.0)

                def P1pe(p):
                    b = p % 2
                    if fwd:
                        mm(ps[5][:, 0:128], sp_[:, b, :], A_le, True, True, [("sp", b), ("cst",)], [psk(5)])
                        mm(rt, A_gt, sp_[:, b, :], True, True, [("sp", b), ("cst",)], [psk(5)])
                    else:
                        mm(ps[5][:, 0:256], sp_[:, b, :], A_gtge, True, True, [("sp", b), ("cst",)], [psk(5)])
                        mm(rt, A_lt, sp_[:, b, :], True, True, [("sp", b), ("cst",)], [psk(5)])

                def P1act(p):
                    b, b3 = p % 2, p % 3
                    act(eq[:, b3, :], ps[5][:, 0:128], AF.Exp, [psk(5)], [("eq", b3)])
                    if fwd:
                        act(ek[:, b, :], ps[5][:, 0:128], AF.Exp, [psk(5)], [("ek", b)], scale=-1.0)
                    else:
                        act(ek[:, b, :], ps[5][:, 128:256], AF.Exp, [psk(5)], [("ek", b)], scale=-1.0)
                        act(edec[:, b3, :], ps[5][:, 128:256:64], AF.Exp, [psk(5)], [("edec", b3)])
                    act(et[:, b, :], rt, AF.Exp, [psk(5)], [("et", b)])

                def P1mul(p):
                    t, b, b3 = order[p], p % 2, p % 3
                    tt("pool", qd[:, b3, :], qT[:, tks(p)], eq[:, b3, :], ALU.mult, [("qT",), ("eq", b3)], [("qd", b3)])
                    for c in range(2):
                        cs = slice(c * 64, (c + 1) * 64)
                        tt("dve", kend[cs, b, c, :], ktok[cs, t, :], et[cs, b, :], ALU.mult, [("ktok",), ("et", b)],
                           [("kend", b, c)])
                    tt("pool", kd[:, b, :], kT[:, tks(p)], ek[:, b, :], ALU.mult, [("kT",), ("ek", b)], [("kd", b)])

                def P2pe(p):
                    t, b, b3 = order[p], p % 2, p % 3
                    mm(ra, kd[:, b, :], qd[:, b3, :], True, True, [("kd", b), ("qd", b3)], [psk(6)])
                    for c in range(2):
                        mm(ps[b][:, c * 256:(c + 1) * 256], kend[:, b, c, :], vtok[:, t, :], True, True,
                           [("kend", b, c), ("vtok",)], [psk(b)])

                def P2dve(p):
                    b = p % 2
                    tt("dve", attT[:, b, :], ra, M_f if fwd else M_b, ALU.mult, [psk(6), ("cst",)], [("attT", b)])

                def SWa(p):
                    t, b, b3 = order[p], p % 2, p % 3
                    pso = ps[2 + b]
                    mm(pso[:, 0:256], attT[:, b, :], vtok[:, t, :], True, False, [("attT", b), ("vtok",)],
                       [psk(2 + b)])
                    c = corder[0]
                    cs = slice(c * 64, (c + 1) * 64)
                    mm(pso[cs, 0:256], qd[:, b3, cs], Sb[:, 0, :], False, True, [("qd", b3), ("Sb", 0)], [psk(2 + b)])
                    cur = 0
                    for c in corder:
                        dsc = eq[:, b3, c * 64 + 63:c * 64 + 64] if fwd else edec[:, b3, c:c + 1]
                        nx = 1 - cur
                        stt(Sa[:, nx, :], Sa[:, cur, :], dsc, ps[b][:, c * 256:(c + 1) * 256], ALU.mult, ALU.add,
                            [("Sa", cur), ("eq", b3), ("edec", b3), psk(b)], [("Sa", nx)])
                        cp("act", Sb[:, nx, :], Sa[:, nx, :], [("Sa", nx)], [("Sb", nx)])
                        cur = nx

                def SWb(p):
                    b, b3 = p % 2, p % 3
                    c = corder[1]
                    cs = slice(c * 64, (c + 1) * 64)
                    mm(ps[2 + b][cs, 0:256], qd[:, b3, cs], Sb[:, 1, :], False, True, [("qd", b3), ("Sb", 1)],
                       [psk(2 + b)])

                def OBE(p):
                    t, b = order[p], p % 2
                    cp("act", ob[:, t, :], ps[2 + b][:, 0:256], [psk(2 + b)], [("ob", t)])

                def Gpe(p):
                    t = order[p]
                    for k in range(KC):
                        mm(rg, hT[:, k, tks(p)], Wv[:, k, :], k == 0, k == KC - 1, [("Wv",), ("hT", t)], [psk(7)])

                def Gact(p):
                    act(eg[:, :], rg, AF.Exp, [psk(7)], [("eg",)], scale=-1.0)

                def Gdve(p):
                    b3 = p % 3
                    tsc("dve", eg[:, :], eg[:, :], 1.0, None, ALU.add, None, [("eg",)], [("eg",)])
                    P.op("dve", lambda e: e.reciprocal(out=eg[:, :], in_=eg[:, :]), [("eg",)], [("eg",)])
                    tt("dve", g2[:, b3, :], rg, eg[:, :], ALU.mult, [psk(7), ("eg",)], [("g2", b3)])
                    tt("pool", g2[:, b3, :], g2[:, b3, :], normg[:, :], ALU.mult, [("g2", b3), ("normg",)],
                       [("g2", b3)])

                def FINa(p):
                    t, b = order[p], p % 2
                    tt("dve", of[:, :], ps[2 + b][:, 0:256], ob[:, t, :], ALU.add, [psk(2 + b), ("ob", t)], [("of",)])
                    act(eg[:, :], of[:, :], AF.Square, [("of",)], [("eg",), ("sm", 0)], accum=sm[:, 0:1])

                def FINb(p):
                    b, b3 = p % 2, p % 3
                    act(sm[:, 1:2], sm[:, 0:1], AF.Ln, [("sm", 0)], [("sm", 1)], scale=1.0 / GLA_DV, bias=NORM_EPS)
                    act(sm[:, 2:3], sm[:, 1:2], AF.Exp, [("sm", 1)], [("sm", 2)], scale=-0.5)
                    stt(ub[:, b, :], of[:, :], sm[:, 2:3], g2[:, b3, :], ALU.mult, ALU.mult,
                        [("of",), ("sm", 2), ("g2", b3)], [("ub", b)])

                def FIN2(p):
                    t, b = order[p], p % 2
                    pu = ps[7][:, 256:384].bitcast(BF16)
                    for i in range(2):
                        tr(pu[:, i * 128:(i + 1) * 128], ub[:, b, i * 128:(i + 1) * 128], identb[:, :],
                           [("ub", b), ("identb",)], [psk(7)])
                    cp("act", UT[:, 2 * h:2 * h + 2, tks(p)], pu.rearrange("p (a b) -> p a b", a=2), [psk(7)],
                       [("UT", t, 2 * h), ("UT", t, 2 * h + 1)])

                if fwd:
                    sched = [(SWa, 0), (P1pe, 2), (P2pe, 1), (P0pe, 3), (Gpe, 1), (P1act, 2), (P2dve, 1), (Gact, 1),
                             (P0act, 3), (P1mul, 2), (Gdve, 1), (SWb, 0), (FINa, -1), (FINb, -1), (FIN2, -2)]
                else:
                    sched = [(SWa, 0), (P1pe, 2), (P2pe, 1), (P0pe, 3), (P1act, 2), (P2dve, 1), (P0act, 3),
                             (P1mul, 2), (SWb, 0), (OBE, -1)]
                for i in range(-3, nO + 2):
                    for fn, off in sched:
                        if 0 <= i + off < nO:
                            fn(i + off)
                    if i % 4 == 0:
                        bg_step()

            run_pass([1, 0] + list(range(NT - 1, 1, -1)), False)
            stage("S6")
            run_pass(list(range(NT)), True)
            stage("S7")
        P.dma("pool", wout_l, w_out[l].rearrange("(k p) c -> p k c", p=128), writes=[("ob",), ("wout",)],
              semkey=("wout",))
        return wout_l

    def na_layer(l, j, need_ctx):
        C = Carver()
        Wp = C(BF16, KC, 512)
        QT = C(BF16, TOK)
        KT = C(BF16, TOK)
        VT = C(BF16, TOK)
        SG = C(F32, TOK)
        Vtok = C(BF16, NT, 128)
        tabf = C(F32, NBC)
        Tb = C(BF16, 2, NBC)
        E = C(BF16, 6, 512)
        rec = C(F32, 512)
        uu = C(F32, 512)
        egb = C(F32, 512)
        wout_n = C(BF16, KC, D)
        wsrc = na_w_in[j]
        P.dma("pool", wout_n, w_out[l].rearrange("(k p) c -> p k c", p=128), writes=[("wout",)],
              semkey=("wout",))

        for p in range(NA_H // 2):
            for i in range(4):
                P.dma("pool", Wp[:, :, i * 128:(i + 1) * 128],
                      wsrc[:, i * D + p * 128:i * D + (p + 1) * 128].rearrange("(k p) c -> p k c", p=128),
                      writes=[("Wp", i)])
            for (a, n) in blocks:
                for i in range(4):
                    pb = i
                    for k in range(KC):
                        mm(ps[pb][:, 0:n], Wp[:, k, i * 128:(i + 1) * 128], hT[:, k, a:a + n], k == 0, k == KC - 1,
                           [("Wp", i)] + tiles_of(a, n), [psk(pb)])
                    if i == 0:
                        act(QT[:, a:a + n], ps[pb][:, 0:n], AF.Copy, [psk(pb)], [("QT", a)], scale=NA_DH ** -0.5)
                    elif i == 1:
                        cp("dve", KT[:, a:a + n], ps[pb][:, 0:n], [psk(pb)], [("KT", a)])
                    elif i == 2:
                        cp("act", VT[:, a:a + n], ps[pb][:, 0:n], [psk(pb)], [("VT", a)])
                    else:
                        act(egb[:, 0:n], ps[pb][:, 0:n], AF.Exp, [psk(pb)], [("egb",)], scale=-1.0)
                        cp("act", SG[:, a:a + n], ps[pb][:, 0:n], [psk(pb)], [("SG", a)])
                        tsc("dve", egb[:, 0:n], egb[:, 0:n], 1.0, None, ALU.add, None, [("egb",)], [("egb",)])
                        P.op("dve", lambda e, n=n: e.reciprocal(out=egb[:, 0:n], in_=egb[:, 0:n]), [("egb",)],
                             [("egb",)])
                        tt("pool", SG[:, a:a + n], SG[:, a:a + n], egb[:, 0:n], ALU.mult, [("SG", a), ("egb",)],
                           [("SG", a)])
            if p == 0:
                stage("N1")
            for t0 in range(0, NT, 4):
                nt = min(4, NT - t0)
                pv = ps[2][:, 0:256].bitcast(BF16)
                for i in range(nt):
                    tr(pv[:, i * 128:(i + 1) * 128], VT[:, (t0 + i) * 128:(t0 + i + 1) * 128], identb[:, :],
                       [("VT",), ("identb",)], [psk(2)])
                cp("dve", Vtok[:, t0:t0 + nt, :], pv[:, 0:nt * 128].rearrange("p (a b) -> p a b", a=nt), [psk(2)],
                   [("Vtok", t0)])
            for hp in range(2):
                P.dma("sp", tabf[:, :], natab[j, 2 * p + hp], writes=[("tabf",)])
                act(Tb[:, hp, :], tabf[:, :], AF.Exp, [("tabf",)], [("Tb", hp)])
            if p == 0:
                stage("N2")
            qblocks = [(CTXN + 512 * i, 512, 8 * i) for i in range(LAT // 512)]
            if need_ctx:
                qblocks.append((0, CTXN, None))
            steps = []
            for bi, (qa, qn, r0) in enumerate(qblocks):
                po, pd = (4, 5) if bi % 2 == 0 else (6, 7)
                for hp in range(2):
                    klist = [(0, 0, qn, None), (1, 0, qn, None)]
                    if r0 is not None:
                        for t in range(H // 2):
                            lo, sig = tiles_na[t]
                            hi = lo + len(sig) - 1
                            ra, rb = max(lo, r0), min(hi, r0 + 7)
                            if ra > rb:
                                continue
                            klist.append((2 + t, (ra - r0) * 64, (rb - ra + 1) * 64, (offs_na[t] + ra - lo) * 64))
                    for ki, (kt, qo, n, tc) in enumerate(klist):
                        steps.append(dict(bi=bi, qa=qa, qn=qn, po=po, pd=pd, hp=hp, kt=kt, qo=qo, n=n, tc=tc,
                                          first=ki == 0, last=ki == len(klist) - 1,
                                          endblk=(hp == 1 and ki == len(klist) - 1)))

            def QK(si):
                st = steps[si]
                hs = slice(st["hp"] * 64, st["hp"] * 64 + 64)
                sb_, eb, n = 2 + si % 2, si % 6, st["n"]
                q0 = st["qa"] + st["qo"]
                mm(ps[sb_][:, 0:n], KT[hs, st["kt"] * 128:(st["kt"] + 1) * 128], QT[hs, q0:q0 + n], True, True,
                   [("KT",), ("QT",)], [psk(sb_)])
                act(E[:, eb, 0:n], ps[sb_][:, 0:n], AF.Exp, [psk(sb_)], [("E", eb)])
                if st["tc"] is not None:
                    tt("dve", E[:, eb, 0:n], E[:, eb, 0:n],
                       Tb[:, st["hp"], st["tc"]:st["tc"] + n], ALU.mult, [("E", eb), ("Tb", st["hp"])], [("E", eb)])

            def PV(si):
                st = steps[si]
                hs = slice(st["hp"] * 64, st["hp"] * 64 + 64)
                eb, n, qo, po, pd = si % 6, st["n"], st["qo"], st["po"], st["pd"]
                mm(ps[po][hs, qo:qo + n], Vtok[:, st["kt"], hs], E[:, eb, 0:n], st["first"], st["last"],
                   [("Vtok",), ("E", eb)], [psk(po)])
                mm(ps[pd][hs, qo:qo + n], onesb[:, :], E[:, eb, 0:n], st["first"], st["last"],
                   [("onesb",), ("E", eb)], [psk(pd)])
                if st["endblk"]:
                    qa, qn = st["qa"], st["qn"]
                    P.op("dve", lambda e: e.reciprocal(out=rec[:, 0:qn], in_=ps[pd][:, 0:qn]), [psk(pd)],
                         [("rec",)])
                    tt("dve", uu[:, 0:qn], ps[po][:, 0:qn], rec[:, 0:qn], ALU.mult, [psk(po), ("rec",)], [("uu",)])
                    tt("dve", UT[:, p, qa:qa + qn], uu[:, 0:qn], SG[:, qa:qa + qn], ALU.mult, [("uu",), ("SG",)],
                       [("UT", tq, p) for tq in range(qa // 128, (qa + qn) // 128)])

            LAG = 3
            for si in range(len(steps) + LAG):
                if si < len(steps):
                    QK(si)
                if si - LAG >= 0:
                    PV(si - LAG)
                if si % 16 == 8:
                    bg_step()
            if p == 0:
                stage("N3")
        return wout_n

    def outproj(l, need_ctx, last, wout):
        xsrc = xin if l == 0 else xs[(l - 1) % 2]
        P.dma("sp", lnb[:, 0, :], ln_g[l].partition_broadcast(128), writes=[("lnb", 0)])
        P.dma("sp", lnb[:, 1, :], ln_b[l].partition_broadcast(128), writes=[("lnb", 1)])
        tl = list(range(NT)) if need_ctx else list(range(2, NT))
        for ti, t in enumerate(tl):
            xb = ti % 2
            cond = 1 if t < 2 else 0
            tk = slice(t * 128, (t + 1) * 128)
            x_ = xt[:, xb, :]
            P.dma("sp", x_, xsrc[tk, :], reads=[("xd", l - 1, t)], writes=[("xt", xb)], semkey=("xt", xb))
            for nh in range(2):
                pb = 2 * xb + nh
                for k in range(KC):
                    mm(ps[pb][:, :], UT[:, k, tk], wout[:, k, nh * 512:(nh + 1) * 512], k == 0, k == KC - 1,
                       [("UT", t), ("wout",)], [psk(pb)])
                tt("dve", t1[:, xb, nh * 512:(nh + 1) * 512], ps[pb][:, :], gtb[:, cond, nh * 512:(nh + 1) * 512],
                   ALU.mult, [psk(pb), ("gtb", cond)], [("t1", xb, nh)])
            stt(x_, x_, float(ALPHA), t1[:, xb, :], ALU.mult, ALU.add, [("xt", xb), ("t1", xb)], [("xt", xb)])
            for nh in range(2):
                P.op("dve", lambda e, nh=nh, x_=x_, xb=xb: e.bn_stats(out=bst[:, xb, nh, :],
                                                                     in_=x_[:, nh * 512:(nh + 1) * 512]),
                     [("xt", xb)], [("bst", xb, nh)])
            P.op("dve", lambda e, xb=xb: e.bn_aggr(out=mv[:, xb, :],
                                                   in_=bst[:, xb, :, :].rearrange("p a b -> p (a b)")),
                 [("bst", xb)], [("mv", xb)])
            act(smo[:, xb, 0:1], mv[:, xb, 1:2], AF.Ln, [("mv", xb)], [("smo", xb, 0)], bias=LN_EPS)
            act(smo[:, xb, 1:2], smo[:, xb, 0:1], AF.Exp, [("smo", xb, 0)], [("smo", xb, 1)], scale=-0.5)
            tsc("dve", x_, x_, mv[:, xb, 0:1], smo[:, xb, 1:2], ALU.subtract, ALU.mult,
                [("xt", xb), ("mv", xb), ("smo", xb, 1)], [("xt", xb)])
            tt("pool", x_, x_, lnb[:, 0, :], ALU.mult, [("xt", xb), ("lnb", 0)], [("xt", xb)])
            tt("pool", x_, x_, lnb[:, 1, :], ALU.add, [("xt", xb), ("lnb", 1)], [("xt", xb)])
            if last:
                if t >= 2:
                    P.dma("sp", out[(t - 2) * 128:(t - 1) * 128, :], x_, reads=[("xt", xb)], writes=[("outd", t)],
                          semkey=("xst", xb))
            else:
                P.dma("sp", xs[l % 2][tk, :], x_, reads=[("xt", xb)], writes=[("xd", l, t)], semkey=("xst", xb))
                emit_hT(t, x_, ("xt", xb), (l + 1) % 2)

    def main():
        stage("S0")
        col_dma(0)
        col_dma(1)
        queue_mod(0)
        bg_flush()
        stage("S1")
        for t in range(NT):
            xb = t % 2
            P.dma("sp", xt[:, xb, :], xin[t * 128:(t + 1) * 128, :], writes=[("xt", xb)], semkey=("xt", xb))
            emit_hT(t, xt[:, xb, :], ("xt", xb), 0)
        stage("S2")
        for l in range(n_layers):
            need_ctx = l < DEPTH - 1
            last = l == n_layers - 1
            queue_gate(l)
            stage("S3")
            if not last:
                queue_mod(l + 1)
            ada_bank[0] = 6 if l % 2 == 0 else 0
            if l % 2 == 0:
                wo = gla_layer(l, l // 2)
            else:
                wo = na_layer(l, l // 2, need_ctx)
            bg_flush()
            stage("S8")
            outproj(l, need_ctx, last, wo)
            if not last:
                barrier()

    try:
        main()
    except _Stop:
        P.dma("sp", out[0:128, :], xt[:, 0, :], reads=[("xt", 0)], writes=[("outd", 0)], semkey=("xst", 0))
    P.finish()
    return nc, P


_CACHE = {}


def prep_inputs(H, x, c, ctx, c_ctx, ada_w, ada_b, ln_g, ln_b, w_out, gla_w_in, gla_dec_w1, gla_dec_w2,
                gla_dec_b, gla_norm_g, na_w_in, na_rpb):
    f = lambda a: np.ascontiguousarray(np.asarray(a, dtype=np.float32))
    B = x.shape[0]
    cst, rope = make_consts(H)
    natab = make_natab(f(na_rpb), H)
    adab = f(np.asarray(ada_b).reshape(DEPTH, 24, 128).transpose(2, 0, 1).reshape(128, DEPTH * 24))
    shared = dict(ada_w=f(ada_w), adab=adab, ln_g=f(ln_g), ln_b=f(ln_b), w_out=f(w_out), gla_w_in=f(gla_w_in),
                  gla_dec_w1=f(gla_dec_w1), gla_dec_w2=f(gla_dec_w2), gla_dec_b=f(gla_dec_b),
                  gla_norm_g=f(gla_norm_g), na_w_in=f(na_w_in), natab=natab, cst=cst, rope=rope)
    maps = []
    for b in range(B):
        m = dict(shared)
        m["xin"] = f(np.concatenate([np.asarray(ctx[b]), np.asarray(x[b])], 0))
        cc = np.stack([np.asarray(c[b]).reshape(KC, 128).T, np.asarray(c_ctx).reshape(KC, 128).T], -1)
        m["cc"] = f(cc)
        maps.append(m)
    return maps


def kernel(**inputs):
    H = 32
    if "nc" not in _CACHE:
        _CACHE["nc"] = build(H, DEPTH)[0]
    nc = _CACHE["nc"]
    maps = prep_inputs(H, **inputs)
    res = run_bass_kernel_spmd(nc, maps, core_ids=list(range(len(maps))))
    return np.stack([np.asarray(r["out"], dtype=np.float32) for r in res.results], 0)
```

```python
from contextlib import ExitStack
import os

import numpy as np
import concourse.bass as bass
import concourse.mybir as mybir
from concourse.bass_utils import run_bass_kernel_spmd

F32 = mybir.dt.float32
BF16 = mybir.dt.bfloat16
AF = mybir.ActivationFunctionType
ALU = mybir.AluOpType


class _Op:
    __slots__ = ("eng", "fn", "deps", "is_dma", "semkey", "sig", "val", "n")

    def __init__(self, eng, fn, is_dma, semkey):
        self.eng, self.fn, self.is_dma, self.semkey = eng, fn, is_dma, semkey
        self.deps, self.sig, self.val, self.n = (), False, 0, 0


class _Ent:
    __slots__ = ("w", "r")

    def __init__(self):
        self.w, self.r = None, {}


class Prog:
    ENG = ("pe", "act", "dve", "pool", "sp")

    def __init__(self, nc):
        self.nc = nc
        self.stack = ExitStack()
        self.streams = {e: [] for e in self.ENG}
        self.res = {}
        self.nops = 0
        self.dma_last = {}

    def sb(self, name, shape, dtype):
        return self.stack.enter_context(self.nc.sbuf_tensor(name, list(shape), dtype))

    def ps(self, name):
        return self.stack.enter_context(self.nc.psum_tensor(name, [128, 512], F32))

    @staticmethod
    def _ck(o):
        return o.semkey if o.is_dma else o.eng

    def _conf(self, key):
        b = self.res.get(key[0])
        if not b:
            return
        n = len(key)
        for k, ent in b.items():
            m = len(k)
            if (k[:n] == key) if m >= n else (key[:m] == k):
                yield k, ent

    def _ent(self, key):
        b = self.res.setdefault(key[0], {})
        e = b.get(key)
        if e is None:
            e = b[key] = _Ent()
        return e

    def _record(self, o, reads, writes):
        reads = [k[:2] if k[0] == "ps" else k for k in reads]
        writes = [k[:2] if k[0] == "ps" else k for k in writes]
        deps = {}

        def add(d):
            if d is None or d is o:
                return
            if o.eng == "pe" and d.eng == "pe" and not d.is_dma:
                return
            if d.is_dma:
                d = self.dma_last[d.semkey]
            ck = self._ck(d)
            p = deps.get(ck)
            if p is None or d.n > p.n:
                deps[ck] = d

        for k in reads:
            for _, ent in self._conf(k):
                add(ent.w)
        for k in writes:
            for _, ent in self._conf(k):
                add(ent.w)
                for r in ent.r.values():
                    add(r)
        for k in reads:
            self._ent(k).r[self._ck(o)] = o
        for k in writes:
            n = len(k)
            b = self.res.setdefault(k[0], {})
            for kk in [kk for kk in b if len(kk) > n and kk[:n] == k]:
                del b[kk]
            e = self._ent(k)
            e.w, e.r = o, {}
        o.deps = tuple(deps.values())
        self.nops += 1
        o.n = self.nops
        self.streams[o.eng].append(o)
        return o

    def op(self, eng, fn, reads=(), writes=()):
        return self._record(_Op(eng, fn, False, None), reads, writes)

    def dma(self, q, out, in_, reads=(), writes=(), semkey=None):
        if semkey is None:
            semkey = writes[0]
        fn = lambda e: e.dma_start(out=out, in_=in_)
        o = self._record(_Op(q, fn, True, ("dma",) + tuple(semkey)), reads, writes)
        self.dma_last[o.semkey] = o
        return o

    def finish(self):
        nc = self.nc
        for s in self.streams.values():
            for o in s:
                for d in o.deps:
                    d.sig = True
        semnames = {}
        for e in self.ENG:
            cnt = 0
            for o in self.streams[e]:
                if not o.is_dma and o.sig:
                    cnt += 1
                    o.val = cnt
            semnames[e] = None
        dcnt = {}
        for e in self.ENG:
            for o in self.streams[e]:
                if o.is_dma:
                    dcnt[o.semkey] = dcnt.get(o.semkey, 0) + 16
                    o.val = dcnt[o.semkey]
                    semnames[o.semkey] = None
        sems = {}
        for i, k in enumerate(semnames):
            sems[k] = self.stack.enter_context(nc.semaphore("s%d" % i))
        self.n_sems = len(sems)
        ck = self._ck

        def emit(name, eng):
            seen = {}
            for o in self.streams[name]:
                for d in o.deps:
                    k = ck(d)
                    if seen.get(k, 0) < d.val:
                        eng.wait_ge(sems[k], d.val)
                        seen[k] = d.val
                if o.fn is None:
                    continue
                ins = o.fn(eng)
                if o.is_dma:
                    ins.then_inc(sems[o.semkey], 16)
                elif o.sig:
                    ins.then_inc(sems[name], 1)
            if name == "sp":
                for k, v in dcnt.items():
                    if seen.get(k, 0) < v:
                        eng.wait_ge(sems[k], v)

        with nc.Block() as block:
            @block.tensor
            def _(e):
                emit("pe", e)

            @block.scalar
            def _(e):
                emit("act", e)

            @block.vector
            def _(e):
                emit("dve", e)

            @block.gpsimd
            def _(e):
                emit("pool", e)

            @block.sync
            def _(e):
                emit("sp", e)
        self.stack.close()


D = 1024
KC = 8
CTXN = 256
GRID_W = 64
NA_KH, NA_KW = 8, 16
GLA_DK, GLA_DV, GLA_H = 128, 256, 4
NA_DH, NA_H = 64, 16
DEPTH = 4
ALPHA = (2 * DEPTH) ** 0.25
LN_EPS = 1e-5
NORM_EPS = 1e-6
NEG = -30000.0


def na_plan(H):
    rs = lambda r: min(max(r - NA_KH // 2, 0), H - NA_KH)
    tiles = []
    for t in range(H // 2):
        rows = [r for r in range(H) if any(rs(r) <= 2 * t + a < rs(r) + NA_KH for a in (0, 1))]
        assert rows == list(range(rows[0], rows[-1] + 1))
        sig = [(r - 2 * t, rs(r) <= 2 * t < rs(r) + NA_KH, rs(r) <= 2 * t + 1 < rs(r) + NA_KH) for r in rows]
        tiles.append((rows[0], sig))
    table, offs = [], {}
    for t in sorted(range(len(tiles)), key=lambda t: -len(tiles[t][1])):
        sig = tiles[t][1]
        off = None
        for o in range(0, len(table) - len(sig) + 1):
            if table[o:o + len(sig)] == sig:
                off = o
                break
        if off is None:
            off = len(table)
            table.extend(sig)
        offs[t] = off
    return tiles, table, offs


def make_consts(H):
    LAT = H * GRID_W
    idx = np.arange(128)
    s, t = idx[:, None], idx[None, :]
    same = (s // 64) == (t // 64)
    cst = np.zeros((128, 1152), np.float32)
    cst[:, 0:128] = np.eye(128, dtype=np.float32)
    sc = np.float32(-1.0 / 16.0)
    cst[:, 128:256] = (same & (s <= t)) * sc
    cst[:, 256:384] = (same & (s > t)) * sc
    cst[:, 384:512] = (same & (s >= t)) * sc
    cst[:, 512:640] = (same & (s < t)) * sc
    cst[:, 640:768] = (same & (s <= t))
    cst[:, 768:896] = (same & (s > t))
    rot = np.zeros((128, 128), np.float32)
    for m in range(64):
        rot[m + 64, m] = -1.0
        rot[m, m + 64] = 1.0
    cst[:, 896:1024] = rot
    cst[:, 1024:1152] = 1.0
    pos = np.arange(LAT)
    row = (pos // GRID_W).astype(np.float32)
    col = (pos % GRID_W).astype(np.float32)
    inv = np.power(np.float32(10000.0), -np.arange(32, dtype=np.float32) / np.float32(32)).astype(np.float32)
    ang = np.concatenate([row[:, None] * inv, col[:, None] * inv], -1).astype(np.float32)
    cos, sin = np.cos(ang).astype(np.float32), np.sin(ang).astype(np.float32)
    rope = np.zeros((128, 2, LAT), np.float32)
    rope[:, 0, :] = np.concatenate([cos.T, cos.T], 0)
    rope[:, 1, :] = np.concatenate([sin.T, sin.T], 0)
    return cst, rope


def make_natab(rpb, H):
    _, table, _ = na_plan(H)
    NB = len(table)
    nl, nh = rpb.shape[0], rpb.shape[1]
    a = np.arange(2)[:, None, None]
    kc = np.arange(64)[None, :, None]
    c = np.arange(64)[None, None, :]
    cs = np.clip(c - NA_KW // 2, 0, GRID_W - NA_KW)
    colok = (kc >= cs) & (kc < cs + NA_KW)
    dc = np.clip(kc - c + NA_KW - 1, 0, 2 * NA_KW - 2)
    out = np.full((nl, nh, 128, NB * 64), NEG, np.float32)
    for b, (u, v0, v1) in enumerate(table):
        d = a - u + NA_KH - 1
        vrow = np.array([v0, v1])[:, None, None]
        ok = np.broadcast_to(vrow & colok & (d >= 0) & (d <= 2 * NA_KH - 2), (2, 64, 64))
        dd = np.broadcast_to(np.clip(d, 0, 2 * NA_KH - 2), (2, 64, 64))
        dcc = np.broadcast_to(dc, (2, 64, 64))
        vals = rpb[:, :, dd, dcc]
        blk = np.where(ok[None, None], vals, np.float32(NEG)).reshape(nl, nh, 128, 64)
        out[:, :, :, b * 64:(b + 1) * 64] = blk
    return out


class _Stop(Exception):
    pass


def build(H=32, n_layers=DEPTH, stop=None):
    LAT = H * GRID_W
    TOK = CTXN + LAT
    NT = TOK // 128
    blocks = [(0, CTXN)] + [(CTXN + 512 * i, 512) for i in range(LAT // 512)]
    tiles_na, table_na, offs_na = na_plan(H)
    NBC = len(table_na) * 64

    nc = bass.Bass("TRN2", target_bir_lowering=False)
    P = Prog(nc)
    din = lambda name, shape: nc.dram_tensor(name, list(shape), F32, kind="ExternalInput").ap()
    xin = din("xin", [TOK, D])
    cc_d = din("cc", [128, KC, 2])
    ada_w = din("ada_w", [DEPTH, D, 3 * D])
    adab_d = din("adab", [128, DEPTH * 24])
    ln_g = din("ln_g", [DEPTH, D])
    ln_b = din("ln_b", [DEPTH, D])
    w_out = din("w_out", [DEPTH, D, D])
    gla_w_in = din("gla_w_in", [2, D, 3 * D])
    gla_w1 = din("gla_dec_w1", [2, 2, D, 16])
    gla_w2 = din("gla_dec_w2", [2, 2, 16, 512])
    gla_b = din("gla_dec_b", [2, 2, 512])
    gla_ng = din("gla_norm_g", [2, GLA_DV])
    na_w_in = din("na_w_in", [2, D, 4 * D])
    natab = din("natab", [2, NA_H, 128, NBC])
    cst_d = din("cst", [128, 1152])
    rope_d = din("rope", [128, 2, LAT])
    out = nc.dram_tensor("out", [LAT, D], F32, kind="ExternalOutput").ap()
    xs = [nc.dram_tensor("xs%d" % i, [TOK, D], F32).ap() for i in range(2)]

    hT = P.sb("hT", [128, KC, TOK], BF16)
    UT = P.sb("UT", [128, KC, TOK], BF16)
    cst = P.sb("cstb", [128, 1152], F32)
    ident = cst[:, 0:128]
    A_le, A_gt, A_ge, A_lt = (cst[:, 128 + 128 * i:256 + 128 * i] for i in range(4))
    A_gtge = cst[:, 256:512]
    M_f, M_b = cst[:, 640:768], cst[:, 768:896]
    rot = cst[:, 896:1024]
    onesf = cst[:, 1024:1152]
    identb = P.sb("identb", [128, 128], BF16)
    onesb = P.sb("onesb", [128, 64], BF16)
    adaW = P.sb("adaW", [128, 2, KC, 128], F32)
    ccs = P.sb("ccs", [128, KC, 2], F32)
    cct = P.sb("cct", [128, KC, 2], F32)
    adab = P.sb("adabs", [128, DEPTH * 24], F32)
    mcol = P.sb("mcol", [128, 2, 24, 2], F32)
    sc1 = P.sb("sc1", [128, 2, KC, 2], F32)
    gdiag = P.sb("gdiag", [128, 2, 128], F32)
    gtb = P.sb("gtb", [128, 2, D], F32)
    lnb = P.sb("lnb", [128, 2, D], F32)
    xt = P.sb("xt", [128, 2, D], F32)
    t1 = P.sb("t1", [128, 2, D], F32)
    bst = P.sb("bst", [128, 2, 2, 6], F32)
    mv = P.sb("mv", [128, 2, 2], F32)
    sm = P.sb("sm", [128, 8], F32)
    smo = P.sb("smo", [128, 2, 4], F32)
    ARENA = 86 * 1024
    arena = P.sb("arena", [128, ARENA // 2], BF16)
    ps = [P.ps("ps%d" % i) for i in range(8)]
    psk = lambda i: ("ps", i)

    class Carver:
        def __init__(self):
            self.off = 0

        def __call__(self, dtype, *free):
            n = int(np.prod(free))
            nb = n * (4 if dtype == F32 else 2)
            nb = (nb + 63) // 64 * 64
            assert self.off + nb <= ARENA, (self.off, nb)
            a = arena[:, self.off // 2:(self.off + nb) // 2]
            self.off += nb
            if dtype == F32:
                a = a.bitcast(F32)
            a = a[:, 0:n]
            if len(free) == 2:
                a = a.rearrange("p (a b) -> p a b", a=free[0])
            elif len(free) == 3:
                a = a.rearrange("p (a b c) -> p a b c", a=free[0], b=free[1])
            return a

    def mm(out_, lhsT, rhs, start, stop, reads, writes):
        P.op("pe", lambda e: e.matmul(out_, lhsT=lhsT, rhs=rhs, start=start, stop=stop), reads, writes)

    def tr(out_, in_, idn, reads, writes):
        P.op("pe", lambda e: e.transpose(out_, in_, idn), reads, writes)

    def act(out_, in_, func, reads, writes, scale=1.0, bias=0.0, accum=None):
        if accum is None:
            P.op("act", lambda e: e.activation(out=out_, in_=in_, func=func, bias=bias, scale=scale), reads, writes)
        else:
            P.op("act", lambda e: e.activation(out=out_, in_=in_, func=func, bias=bias, scale=scale, accum_out=accum),
                 reads, writes)

    def tt(eng, out_, in0, in1, op, reads, writes):
        P.op(eng, lambda e: e.tensor_tensor(out=out_, in0=in0, in1=in1, op=op), reads, writes)

    def tsc(eng, out_, in0, s1, s2, op0, op1, reads, writes):
        if s2 is None:
            P.op(eng, lambda e: e.tensor_scalar(out=out_, in0=in0, scalar1=s1, scalar2=None, op0=op0), reads, writes)
        else:
            P.op(eng, lambda e: e.tensor_scalar(out=out_, in0=in0, scalar1=s1, scalar2=s2, op0=op0, op1=op1),
                 reads, writes)

    def stt(out_, in0, scalar, in1, op0, op1, reads, writes):
        P.op("dve", lambda e: e.scalar_tensor_tensor(out=out_, in0=in0, scalar=scalar, in1=in1, op0=op0, op1=op1),
             reads, writes)

    def cp(eng, out_, in_, reads, writes):
        if eng == "act":
            act(out_, in_, AF.Copy, reads, writes)
        else:
            P.op(eng, lambda e: e.tensor_copy(out=out_, in_=in_), reads, writes)

    def barrier():
        last = {}
        for e in P.ENG:
            for o in P.streams[e]:
                if o.fn is not None:
                    last[P._ck(o)] = o
        for e in P.ENG:
            b = _Op(e, None, False, None)
            b.deps = tuple(d for d in last.values() if not (d.eng == e and not d.is_dma))
            P.nops += 1
            b.n = P.nops
            P.streams[e].append(b)
        P.res = {}

    def stage(name):
        if stop == name:
            raise _Stop()

    tiles_of = lambda a, n: [("hT", t) for t in range(a // 128, (a + n) // 128)]

    P.dma("sp", cst[:, :], cst_d, writes=[("cst",)])
    P.dma("sp", ccs[:, :, :], cc_d, writes=[("ccs",)])
    P.dma("sp", adab[:, :], adab_d, writes=[("adab",)])
    cp("dve", identb[:, :], ident, [("cst",)], [("identb",)])
    P.op("pool", lambda e: e.memset(onesb[:, :], 1.0), writes=[("onesb",)])
    act(cct[:, :, :], ccs[:, :, :], AF.Exp, [("ccs",)], [("cct",)], scale=-1.0)
    tsc("dve", cct[:, :, :], cct[:, :, :], 1.0, None, ALU.add, None, [("cct",)], [("cct",)])
    P.op("dve", lambda e: e.reciprocal(out=cct[:, :, :], in_=cct[:, :, :]), [("cct",)], [("cct",)])
    tt("dve", ccs[:, :, :], ccs[:, :, :], cct[:, :, :], ALU.mult, [("ccs",), ("cct",)], [("ccs",)])

    coljobs = []
    for l_ in range(n_layers):
        if l_ == 0:
            coljobs += [(0, j, 0) for j in range(16)]
        coljobs += [(l_, j, l_ % 2) for j in range(16, 24)]
        if l_ + 1 < n_layers:
            coljobs += [(l_ + 1, j, (l_ + 1) % 2) for j in range(16)]
    col_next = [0]
    ada_bank = [7]
    bg = []

    def col_dma(g):
        if g < len(coljobs):
            l, j, par = coljobs[g]
            P.dma("sp", adaW[:, g % 2, :, :], ada_w[l, :, j * 128:(j + 1) * 128].rearrange("(k p) c -> p k c", p=128),
                  writes=[("adaW", g % 2)])

    def col_job():
        g = col_next[0]
        col_next[0] += 1
        l, j, par = coljobs[g]
        b = g % 2
        pa = ps[ada_bank[0]]
        for k in range(KC):
            mm(pa[:, 2 * j:2 * j + 2], adaW[:, b, k, :], ccs[:, k, :], k == 0, k == KC - 1,
               [("adaW", b), ("ccs",)], [psk(ada_bank[0])])
        tsc("dve", mcol[:, par, j, :], pa[:, 2 * j:2 * j + 2], adab[:, l * 24 + j:l * 24 + j + 1], None,
            ALU.add, None, [psk(ada_bank[0]), ("adab",)], [("mcol", par, j)])
        col_dma(g + 2)

    def sc1_job(par):
        tsc("dve", sc1[:, par, :, :], mcol[:, par, 8:16, :], 1.0, None, ALU.add, None,
            [("mcol", par)], [("sc1", par)])

    def gate_job(par, cond, half):
        pa = ps[ada_bank[0]]
        for jj in range(4):
            j = half * 4 + jj
            gb = jj % 2
            tsc("dve", gdiag[:, gb, :], ident, mcol[:, par, 16 + j, cond:cond + 1], None, ALU.mult, None,
                [("cst",), ("mcol", par, 16 + j)], [("gdiag", gb)])
            mm(pa[:, jj * 128:(jj + 1) * 128], onesf, gdiag[:, gb, :], True, True,
               [("cst",), ("gdiag", gb)], [psk(ada_bank[0])])
        cp("act", gtb[:, cond, half * 512:(half + 1) * 512], pa[:, :], [psk(ada_bank[0])], [("gtb", cond, half)])

    def queue_mod(l):
        for _ in range(16):
            bg.append(col_job)
        bg.append(lambda: sc1_job(l % 2))

    def queue_gate(l):
        for _ in range(8):
            bg.append(col_job)
        for cond in range(2):
            for half in range(2):
                bg.append(lambda cond=cond, half=half: gate_job(l % 2, cond, half))

    def bg_step(n=1):
        for _ in range(n):
            if bg:
                bg.pop(0)()

    def bg_flush():
        while bg:
            bg.pop(0)()

    def emit_hT(t, src, srckey, par):
        cond = 1 if t < 2 else 0
        for half in range(2):
            pb = 5 + half
            for kk in range(4):
                k = half * 4 + kk
                tr(ps[pb][:, kk * 128:(kk + 1) * 128], src[:, k * 128:(k + 1) * 128], ident,
                   [srckey, ("cst",)], [psk(pb)])
            for kk in range(4):
                k = half * 4 + kk
                o_ = hT[:, k, t * 128:(t + 1) * 128]
                i_ = ps[pb][:, kk * 128:(kk + 1) * 128]
                rd = [psk(pb), ("sc1", par), ("mcol", par)]
                tsc("dve", o_, i_, sc1[:, par, k, cond:cond + 1], mcol[:, par, k, cond:cond + 1],
                    ALU.mult, ALU.add, rd, [("hT", t, k)])

    def gla_layer(l, j):
        C = Carver()
        Wqk = C(BF16, KC, 256)
        Wv = C(BF16, KC, 256)
        qT = C(BF16, TOK)
        kT = C(BF16, TOK)
        ktok = C(BF16, NT, 128)
        vtok = C(BF16, NT, 256)
        obw = C(BF16, max(KC * D, NT * 512))
        ob = obw[:, 0:NT * 512].bitcast(F32).rearrange("p (a b) -> p a b", a=NT)
        wout_l = obw[:, 0:KC * D].rearrange("p (k c) -> p k c", k=KC)
        lowT = C(BF16, TOK)
        w1 = C(BF16, KC, 32)
        w2a = C(BF16, 2, 512)
        normg = C(F32, 256)
        ropeb = C(F32, 1, 2, 512)
        qraw = C(F32, 2, 512)
        tb_ = C(F32, 1, 512)
        e1 = C(F32, 2, 128)
        sp_ = C(F32, 2, 128)
        eq = C(F32, 3, 128)
        ek = C(F32, 2, 128)
        edec = C(F32, 3, 2)
        et = C(F32, 2, 128)
        qd = C(BF16, 3, 128)
        kd = C(BF16, 2, 128)
        kend = C(BF16, 2, 2, 128)
        attT = C(BF16, 2, 128)
        Sa = C(F32, 2, 256)
        Sb = C(BF16, 2, 256)
        eg = C(F32, 256)
        g2 = C(F32, 3, 256)
        of = C(F32, 256)
        ub = C(BF16, 2, 256)

        wsrc = gla_w_in[j]
        P.op("pool", lambda e: e.memset(w2a[:, :, :], 0.0), writes=[("w2a",)])
        for z in range(2):
            P.dma("pool", w1[:, :, z * 16:(z + 1) * 16], gla_w1[j, z].rearrange("(k p) r -> p k r", p=128),
                  writes=[("w1", z)], semkey=("w1",))
            P.dma("pool", w2a[z * 32:z * 32 + 16, z, :], gla_w2[j, z], writes=[("w2a", z, 0)], semkey=("w2a",))
            P.dma("pool", w2a[z * 32 + 16:z * 32 + 17, z, :], gla_b[j, z:z + 1, :], writes=[("w2a", z, 1)],
                  semkey=("w2a",))
        P.dma("sp", normg[:, :], gla_ng[j].partition_broadcast(128), writes=[("normg",)])
        P.op("pool", lambda e: e.memset(lowT[:, :], 1.0), writes=[("lowT",)])
        P.op("pool", lambda e: e.memset(kend[:, :, :, :], 0.0), writes=[("kend",)])
        for (a, n) in blocks:
            for z in range(2):
                for k in range(KC):
                    mm(ps[0][z * 32:z * 32 + 16, 0:n], w1[:, k, z * 16:(z + 1) * 16], hT[:, k, a:a + n], k == 0,
                       k == KC - 1, [("w1", z)] + tiles_of(a, n), [psk(0)])
                cp("act", lowT[z * 32:z * 32 + 16, a:a + n], ps[0][z * 32:z * 32 + 16, 0:n], [psk(0)],
                   [("lowT", z, a)])

        stage("S4")
        for h in range(GLA_H):
            P.dma("pool", Wqk[:, :, 0:128], wsrc[:, h * 128:(h + 1) * 128].rearrange("(k p) c -> p k c", p=128),
                  writes=[("Wqk", 0)])
            P.dma("pool", Wqk[:, :, 128:256],
                  wsrc[:, 512 + h * 128:512 + (h + 1) * 128].rearrange("(k p) c -> p k c", p=128),
                  writes=[("Wqk", 1)])
            P.dma("pool", Wv[:, :, :],
                  wsrc[:, 1024 + h * 256:1024 + (h + 1) * 256].rearrange("(k p) c -> p k c", p=128),
                  writes=[("Wv",)])
            for (a, n) in blocks:
                if a >= CTXN:
                    rb = 0
                    P.dma("sp", ropeb[:, rb, :, :], rope_d[:, :, a - CTXN:a - CTXN + 512], writes=[("ropeb", rb)])
                for w in range(2):
                    dst, dk_ = (qT, "qT") if w == 0 else (kT, "kT")
                    pb = w
                    for k in range(KC):
                        mm(ps[pb][:, 0:n], Wqk[:, k, w * 128:(w + 1) * 128], hT[:, k, a:a + n], k == 0, k == KC - 1,
                           [("Wqk", w)] + tiles_of(a, n), [psk(pb)])
                    scl = GLA_DK ** -0.5 if w == 0 else 1.0
                    if a < CTXN:
                        act(dst[:, a:a + n], ps[pb][:, 0:n], AF.Copy, [psk(pb)], [(dk_, a)], scale=scl)
                    else:
                        qr, tb2, pr = qraw[:, w, :], tb_[:, 0, :], 2 + w
                        act(qr, ps[pb][:, 0:n], AF.Copy, [psk(pb)], [("qraw", w)], scale=scl)
                        mm(ps[pr][:, :], rot, qr, True, True, [("cst",), ("qraw", w)], [psk(pr)])
                        tt("pool", qr, qr, ropeb[:, rb, 0, :], ALU.mult, [("qraw", w), ("ropeb", rb)], [("qraw", w)])
                        tt("dve", tb2, ps[pr][:, :], ropeb[:, rb, 1, :], ALU.mult, [psk(pr), ("ropeb", rb)], [("tb",)])
                        tt("dve", dst[:, a:a + n], qr, tb2, ALU.add, [("qraw", w), ("tb",)], [(dk_, a)])
            for t0 in range(0, NT, 4):
                nt = min(4, NT - t0)
                pv = ps[2][:, 0:256].bitcast(BF16)
                for i in range(nt):
                    tr(pv[:, i * 128:(i + 1) * 128], kT[:, (t0 + i) * 128:(t0 + i + 1) * 128], identb[:, :],
                       [("kT",), ("identb",)], [psk(2)])
                cp("act", ktok[:, t0:t0 + nt, :], pv[:, 0:nt * 128].rearrange("p (a b) -> p a b", a=nt), [psk(2)],
                   [("ktok", t0)])
            for t0 in range(0, NT, 2):
                pb = 3
                for i in range(2):
                    t = t0 + i
                    for k in range(KC):
                        mm(ps[pb][:, i * 256:(i + 1) * 256], hT[:, k, t * 128:(t + 1) * 128], Wv[:, k, :], k == 0,
                           k == KC - 1, [("Wv",), ("hT", t)], [psk(pb)])
                cp("act", vtok[:, t0:t0 + 2, :], ps[pb][:, :].rearrange("p (a b) -> p a b", a=2), [psk(pb)],
                   [("vtok", t0)])
            P.dma("pool", Wv[:, :, :],
                  wsrc[:, 2048 + h * 256:2048 + (h + 1) * 256].rearrange("(k p) c -> p k c", p=128),
                  writes=[("Wv",)])
            stage("S5")

            def run_pass(order, fwd):
                z = 0 if fwd else 1
                zr = slice(z * 32, z * 32 + 17)
                nO = len(order)
                corder = (0, 1) if fwd else (1, 0)
                P.op("pool", lambda e: e.memset(Sa[:, 0, :], 0.0), writes=[("Sa", 0)])
                P.op("pool", lambda e: e.memset(Sb[:, 0, :], 0.0), writes=[("Sb", 0)])
                tks = lambda p: slice(order[p] * 128, (order[p] + 1) * 128)
                rlg = ps[4][:, 0:128]
                rt = ps[5][:, 256:384]
                ra = ps[6][:, 0:128]
                rg = ps[7][:, 0:256]

                def P0pe(p):
                    b = p % 2
                    mm(rlg, lowT[:, tks(p)], w2a[:, z, h * 128:(h + 1) * 128], True, True, [("lowT",), ("w2a", z)],
                       [psk(4)])

                def P0act(p):
                    b = p % 2
                    act(e1[:, b, :], rlg, AF.Exp, [psk(4)], [("e1", b)], scale=-1.0)
                    act(sp_[:, b, :], e1[:, b, :], AF.Ln, [("e1", b)], [("sp", b)], bias=1.0)

                def P1pe(p):
                    b = p % 2
                    if fwd:
                        mm(ps[5][:, 0:128], sp_[:, b, :], A_le, True, True, [("sp", b), ("cst",)], [psk(5)])
                        mm(rt, A_gt, sp_[:, b, :], True, True, [("sp", b), ("cst",)], [psk(5)])
                    else:
                        mm(ps[5][:, 0:256], sp_[:, b, :], A_gtge, True, True, [("sp", b), ("cst",)], [psk(5)])
                        mm(rt, A_lt, sp_[:, b, :], True, True, [("sp", b), ("cst",)], [psk(5)])

                def P1act(p):
                    b, b3 = p % 2, p % 3
                    act(eq[:, b3, :], ps[5][:, 0:128], AF.Exp, [psk(5)], [("eq", b3)])
                    if fwd:
                        act(ek[:, b, :], ps[5][:, 0:128], AF.Exp, [psk(5)], [("ek", b)], scale=-1.0)
                    else:
                        act(ek[:, b, :], ps[5][:, 128:256], AF.Exp, [psk(5)], [("ek", b)], scale=-1.0)
                        act(edec[:, b3, :], ps[5][:, 128:256:64], AF.Exp, [psk(5)], [("edec", b3)])
                    act(et[:, b, :], rt, AF.Exp, [psk(5)], [("et", b)])

                def P1mul(p):
                    t, b, b3 = order[p], p % 2, p % 3
                    tt("pool", qd[:, b3, :], qT[:, tks(p)], eq[:, b3, :], ALU.mult, [("qT",), ("eq", b3)], [("qd", b3)])
                    for c in range(2):
                        cs = slice(c * 64, (c + 1) * 64)
                        tt("dve", kend[cs, b, c, :], ktok[cs, t, :], et[cs, b, :], ALU.mult, [("ktok",), ("et", b)],
                           [("kend", b, c)])
                    tt("pool", kd[:, b, :], kT[:, tks(p)], ek[:, b, :], ALU.mult, [("kT",), ("ek", b)], [("kd", b)])

                def P2pe(p):
                    t, b, b3 = order[p], p % 2, p % 3
                    mm(ra, kd[:, b, :], qd[:, b3, :], True, True, [("kd", b), ("qd", b3)], [psk(6)])
                    for c in range(2):
                        mm(ps[b][:, c * 256:(c + 1) * 256], kend[:, b, c, :], vtok[:, t, :], True, True,
                           [("kend", b, c), ("vtok",)], [psk(b)])

                def P2dve(p):
                    b = p % 2
                    tt("dve", attT[:, b, :], ra, M_f if fwd else M_b, ALU.mult, [psk(6), ("cst",)], [("attT", b)])

                def SWa(p):
                    t, b, b3 = order[p], p % 2, p % 3
                    pso = ps[2 + b]
                    mm(pso[:, 0:256], attT[:, b, :], vtok[:, t, :], True, False, [("attT", b), ("vtok",)],
                       [psk(2 + b)])
                    c = corder[0]
                    cs = slice(c * 64, (c + 1) * 64)
                    mm(pso[cs, 0:256], qd[:, b3, cs], Sb[:, 0, :], False, True, [("qd", b3), ("Sb", 0)], [psk(2 + b)])
                    cur = 0
                    for c in corder:
                        dsc = eq[:, b3, c * 64 + 63:c * 64 + 64] if fwd else edec[:, b3, c:c + 1]
                        nx = 1 - cur
                        stt(Sa[:, nx, :], Sa[:, cur, :], dsc, ps[b][:, c * 256:(c + 1) * 256], ALU.mult, ALU.add,
                            [("Sa", cur), ("eq", b3), ("edec", b3), psk(b)], [("Sa", nx)])
                        cp("act", Sb[:, nx, :], Sa[:, nx, :], [("Sa", nx)], [("Sb", nx)])
                        cur = nx

                def SWb(p):
                    b, b3 = p % 2, p % 3
                    c = corder[1]
                    cs = slice(c * 64, (c + 1) * 64)
                    mm(ps[2 + b][cs, 0:256], qd[:, b3, cs], Sb[:, 1, :], False, True, [("qd", b3), ("Sb", 1)],
                       [psk(2 + b)])

                def OBE(p):
                    t, b = order[p], p % 2
                    cp("act", ob[:, t, :], ps[2 + b][:, 0:256], [psk(2 + b)], [("ob", t)])

                def Gpe(p):
                    t = order[p]
                    for k in range(KC):
                        mm(rg, hT[:, k, tks(p)], Wv[:, k, :], k == 0, k == KC - 1, [("Wv",), ("hT", t)], [psk(7)])

                def Gact(p):
                    act(eg[:, :], rg, AF.Exp, [psk(7)], [("eg",)], scale=-1.0)

                def Gdve(p):
                    b3 = p % 3
                    tsc("dve", eg[:, :], eg[:, :], 1.0, None, ALU.add, None, [("eg",)], [("eg",)])
                    P.op("dve", lambda e: e.reciprocal(out=eg[:, :], in_=eg[:, :]), [("eg",)], [("eg",)])
                    tt("dve", g2[:, b3, :], rg, eg[:, :], ALU.mult, [psk(7), ("eg",)], [("g2", b3)])
                    tt("pool", g2[:, b3, :], g2[:, b3, :], normg[:, :], ALU.mult, [("g2", b3), ("normg",)],
                       [("g2", b3)])

                def FINa(p):
                    t, b = order[p], p % 2
                    tt("dve", of[:, :], ps[2 + b][:, 0:256], ob[:, t, :], ALU.add, [psk(2 + b), ("ob", t)], [("of",)])
                    act(eg[:, :], of[:, :], AF.Square, [("of",)], [("eg",), ("sm", 0)], accum=sm[:, 0:1])

                def FINb(p):
                    b, b3 = p % 2, p % 3
                    act(sm[:, 1:2], sm[:, 0:1], AF.Ln, [("sm", 0)], [("sm", 1)], scale=1.0 / GLA_DV, bias=NORM_EPS)
                    act(sm[:, 2:3], sm[:, 1:2], AF.Exp, [("sm", 1)], [("sm", 2)], scale=-0.5)
                    stt(ub[:, b, :], of[:, :], sm[:, 2:3], g2[:, b3, :], ALU.mult, ALU.mult,
                        [("of",), ("sm", 2), ("g2", b3)], [("ub", b)])

                def FIN2(p):
                    t, b = order[p], p % 2
                    pu = ps[7][:, 256:384].bitcast(BF16)
                    for i in range(2):
                        tr(pu[:, i * 128:(i + 1) * 128], ub[:, b, i * 128:(i + 1) * 128], identb[:, :],
                           [("ub", b), ("identb",)], [psk(7)])
                    cp("act", UT[:, 2 * h:2 * h + 2, tks(p)], pu.rearrange("p (a b) -> p a b", a=2), [psk(7)],
                       [("UT", t, 2 * h), ("UT", t, 2 * h + 1)])

                if fwd:
                    sched = [(SWa, 0), (P1pe, 2), (P2pe, 1), (P0pe, 3), (Gpe, 1), (P1act, 2), (P2dve, 1), (Gact, 1),
                             (P0act, 3), (P1mul, 2), (Gdve, 1), (SWb, 0), (FINa, -1), (FINb, -1), (FIN2, -2)]
                else:
                    sched = [(SWa, 0), (P1pe, 2), (P2pe, 1), (P0pe, 3), (P1act, 2), (P2dve, 1), (P0act, 3),
                             (P1mul, 2), (SWb, 0), (OBE, -1)]
                for i in range(-3, nO + 2):
                    for fn, off in sched:
                        if 0 <= i + off < nO:
                            fn(i + off)
                    if i % 4 == 0:
                        bg_step()

            run_pass([1, 0] + list(range(NT - 1, 1, -1)), False)
            stage("S6")
            run_pass(list(range(NT)), True)
            stage("S7")
        P.dma("pool", wout_l, w_out[l].rearrange("(k p) c -> p k c", p=128), writes=[("ob",), ("wout",)],
              semkey=("wout",))
        return wout_l

    def na_layer(l, j, need_ctx):
        C = Carver()
        Wp = C(BF16, KC, 512)
        QT = C(BF16, 2, TOK)
        KT = C(BF16, TOK)
        VT = C(BF16, TOK)
        SG = C(F32, TOK)
        Vtok = C(BF16, NT, 128)
        tabf = C(F32, NBC)
        Tb = C(BF16, 2, NBC)
        E = C(BF16, 4, 512)
        rec = C(F32, 512)
        uu = C(F32, 512)
        egb = rec
        onesw = C(BF16, 128)
        wout_n = C(BF16, KC, D)
        wsrc = na_w_in[j]
        P.dma("pool", wout_n, w_out[l].rearrange("(k p) c -> p k c", p=128), writes=[("wout",)],
              semkey=("wout",))
        P.op("pool", lambda e: e.memset(QT[:, :, :], 0.0), writes=[("QT",)])
        P.op("pool", lambda e: e.memset(onesw[:, :], 1.0), writes=[("onesw",)])

        for p in range(NA_H // 2):
            for i in range(4):
                P.dma("pool", Wp[:, :, i * 128:(i + 1) * 128],
                      wsrc[:, i * D + p * 128:i * D + (p + 1) * 128].rearrange("(k p) c -> p k c", p=128),
                      writes=[("Wp", i)])
            for (a, n) in blocks:
                for i in range(4):
                    pb = i
                    for k in range(KC):
                        mm(ps[pb][:, 0:n], Wp[:, k, i * 128:(i + 1) * 128], hT[:, k, a:a + n], k == 0, k == KC - 1,
                           [("Wp", i)] + tiles_of(a, n), [psk(pb)])
                    if i == 0:
                        for hp in range(2):
                            hs = slice(hp * 64, hp * 64 + 64)
                            act(QT[hs, hp, a:a + n], ps[pb][hs, 0:n], AF.Copy, [psk(pb)], [("QT", hp, a)],
                                scale=NA_DH ** -0.5)
                    elif i == 1:
                        cp("dve", KT[:, a:a + n], ps[pb][:, 0:n], [psk(pb)], [("KT", a)])
                    elif i == 2:
                        cp("act", VT[:, a:a + n], ps[pb][:, 0:n], [psk(pb)], [("VT", a)])
                    else:
                        act(egb[:, 0:n], ps[pb][:, 0:n], AF.Exp, [psk(pb)], [("rec",)], scale=-1.0)
                        cp("act", SG[:, a:a + n], ps[pb][:, 0:n], [psk(pb)], [("SG", a)])
                        tsc("dve", egb[:, 0:n], egb[:, 0:n], 1.0, None, ALU.add, None, [("rec",)], [("rec",)])
                        P.op("dve", lambda e, n=n: e.reciprocal(out=egb[:, 0:n], in_=egb[:, 0:n]), [("rec",)],
                             [("rec",)])
                        tt("pool", SG[:, a:a + n], SG[:, a:a + n], egb[:, 0:n], ALU.mult, [("SG", a), ("rec",)],
                           [("SG", a)])
            if p == 0:
                stage("N1")
            for t0 in range(0, NT, 4):
                nt = min(4, NT - t0)
                pv = ps[2][:, 0:256].bitcast(BF16)
                for i in range(nt):
                    tr(pv[:, i * 128:(i + 1) * 128], VT[:, (t0 + i) * 128:(t0 + i + 1) * 128], identb[:, :],
                       [("VT",), ("identb",)], [psk(2)])
                cp("dve", Vtok[:, t0:t0 + nt, :], pv[:, 0:nt * 128].rearrange("p (a b) -> p a b", a=nt), [psk(2)],
                   [("Vtok", t0)])
            for hp in range(2):
                P.dma("sp", tabf[:, :], natab[j, 2 * p + hp], writes=[("tabf",)])
                act(Tb[:, hp, :], tabf[:, :], AF.Exp, [("tabf",)], [("Tb", hp)])
            if p == 0:
                stage("N2")
            qblocks = [(CTXN + 512 * i, 512, 8 * i) for i in range(LAT // 512)]
            if need_ctx:
                qblocks.append((0, CTXN, None))
            steps = []
            for bi, (qa, qn, r0) in enumerate(qblocks):
                for hp in range(2):
                    po, pd = (4, 5) if hp == 0 else (6, 7)
                    klist = [(0, 0, qn, None), (1, 0, qn, None)]
                    if r0 is not None:
                        for t in range(H // 2):
                            lo, sig = tiles_na[t]
                            hi = lo + len(sig) - 1
                            ra, rb = max(lo, r0), min(hi, r0 + 7)
                            if ra > rb:
                                continue
                            klist.append((2 + t, (ra - r0) * 64, (rb - ra + 1) * 64, (offs_na[t] + ra - lo) * 64))
                    for ki, (kt, qo, n, tc) in enumerate(klist):
                        steps.append(dict(bi=bi, qa=qa, qn=qn, po=po, pd=pd, hp=hp, kt=kt, qo=qo, n=n, tc=tc,
                                          first=ki == 0, last=ki == len(klist) - 1,
                                          endblk=(ki == len(klist) - 1)))

            def QK(si):
                st = steps[si]
                sb_, eb, n = 2 + si % 2, si % 4, st["n"]
                q0 = st["qa"] + st["qo"]
                mm(ps[sb_][:, 0:n], KT[:, st["kt"] * 128:(st["kt"] + 1) * 128], QT[:, st["hp"], q0:q0 + n], True, True,
                   [("KT",), ("QT",)], [psk(sb_)])
                act(E[:, eb, 0:n], ps[sb_][:, 0:n], AF.Exp, [psk(sb_)], [("E", eb)])
                if st["tc"] is not None:
                    tt("dve", E[:, eb, 0:n], E[:, eb, 0:n],
                       Tb[:, st["hp"], st["tc"]:st["tc"] + n], ALU.mult, [("E", eb), ("Tb", st["hp"])], [("E", eb)])

            def PV(si):
                st = steps[si]
                hp = st["hp"]
                hs = slice(hp * 64, hp * 64 + 64)
                eb, n, qo, po, pd = si % 4, st["n"], st["qo"], st["po"], st["pd"]
                mm(ps[po][:, qo:qo + n], Vtok[:, st["kt"], :], E[:, eb, 0:n], st["first"], st["last"],
                   [("Vtok",), ("E", eb)], [psk(po)])
                mm(ps[pd][:, qo:qo + n], onesw[:, :], E[:, eb, 0:n], st["first"], st["last"],
                   [("onesw",), ("E", eb)], [psk(pd)])
                if st["endblk"]:
                    qa, qn = st["qa"], st["qn"]
                    P.op("dve", lambda e: e.reciprocal(out=rec[hs, 0:qn], in_=ps[pd][hs, 0:qn]), [psk(pd)],
                         [("rec", hp)])
                    tt("dve", uu[hs, 0:qn], ps[po][hs, 0:qn], rec[hs, 0:qn], ALU.mult, [psk(po), ("rec", hp)],
                       [("uu", hp)])
                    tt("dve", UT[hs, p, qa:qa + qn], uu[hs, 0:qn], SG[hs, qa:qa + qn], ALU.mult,
                       [("uu", hp), ("SG",)], [("UT", tq, p, hp) for tq in range(qa // 128, (qa + qn) // 128)])

            LAG = 3
            for si in range(len(steps) + LAG):
                if si < len(steps):
                    QK(si)
                if si - LAG >= 0:
                    PV(si - LAG)
                if si % 16 == 8:
                    bg_step()
            if p == 0:
                stage("N3")
        return wout_n

    def outproj(l, need_ctx, last, wout):
        xsrc = xin if l == 0 else xs[(l - 1) % 2]
        P.dma("sp", lnb[:, 0, :], ln_g[l].partition_broadcast(128), writes=[("lnb", 0)])
        P.dma("sp", lnb[:, 1, :], ln_b[l].partition_broadcast(128), writes=[("lnb", 1)])
        tl = list(range(NT)) if need_ctx else list(range(2, NT))
        def stA(ti):
            t = tl[ti]
            xb = ti % 2
            cond = 1 if t < 2 else 0
            tk = slice(t * 128, (t + 1) * 128)
            x_ = xt[:, xb, :]
            P.dma("sp", x_, xsrc[tk, :], reads=[("xd", l - 1, t)], writes=[("xt", xb)], semkey=("xt", xb))
            for nh in range(2):
                pb = 2 * xb + nh
                for k in range(KC):
                    mm(ps[pb][:, :], UT[:, k, tk], wout[:, k, nh * 512:(nh + 1) * 512], k == 0, k == KC - 1,
                       [("UT", t), ("wout",)], [psk(pb)])
                tt("dve", t1[:, xb, nh * 512:(nh + 1) * 512], ps[pb][:, :], gtb[:, cond, nh * 512:(nh + 1) * 512],
                   ALU.mult, [psk(pb), ("gtb", cond)], [("t1", xb, nh)])
            stt(x_, x_, float(ALPHA), t1[:, xb, :], ALU.mult, ALU.add, [("xt", xb), ("t1", xb)], [("xt", xb)])
            for nh in range(2):
                P.op("dve", lambda e, nh=nh, x_=x_, xb=xb: e.bn_stats(out=bst[:, xb, nh, :],
                                                                     in_=x_[:, nh * 512:(nh + 1) * 512]),
                     [("xt", xb)], [("bst", xb, nh)])
            P.op("dve", lambda e, xb=xb: e.bn_aggr(out=mv[:, xb, :],
                                                   in_=bst[:, xb, :, :].rearrange("p a b -> p (a b)")),
                 [("bst", xb)], [("mv", xb)])
            act(smo[:, xb, 0:1], mv[:, xb, 1:2], AF.Ln, [("mv", xb)], [("smo", xb, 0)], bias=LN_EPS)
            act(smo[:, xb, 1:2], smo[:, xb, 0:1], AF.Exp, [("smo", xb, 0)], [("smo", xb, 1)], scale=-0.5)
            tsc("dve", x_, x_, mv[:, xb, 0:1], smo[:, xb, 1:2], ALU.subtract, ALU.mult,
                [("xt", xb), ("mv", xb), ("smo", xb, 1)], [("xt", xb)])

        def stB(ti):
            t = tl[ti]
            xb = ti % 2
            cond = 1 if t < 2 else 0
            tk = slice(t * 128, (t + 1) * 128)
            x_ = xt[:, xb, :]
            tt("pool", x_, x_, lnb[:, 0, :], ALU.mult, [("xt", xb), ("lnb", 0)], [("xt", xb)])
            tt("pool", x_, x_, lnb[:, 1, :], ALU.add, [("xt", xb), ("lnb", 1)], [("xt", xb)])
            if last:
                if t >= 2:
                    P.dma("sp", out[(t - 2) * 128:(t - 1) * 128, :], x_, reads=[("xt", xb)], writes=[("outd", t)],
                          semkey=("xst", xb))
            else:
                P.dma("sp", xs[l % 2][tk, :], x_, reads=[("xt", xb)], writes=[("xd", l, t)], semkey=("xst", xb))
                emit_hT(t, x_, ("xt", xb), (l + 1) % 2)


        stA(0)
        for ti in range(len(tl)):
            if ti + 1 < len(tl):
                stA(ti + 1)
            stB(ti)

    def main():
        stage("S0")
        col_dma(0)
        col_dma(1)
        queue_mod(0)
        bg_flush()
        stage("S1")
        for t in range(NT):
            xb = t % 2
            P.dma("sp", xt[:, xb, :], xin[t * 128:(t + 1) * 128, :], writes=[("xt", xb)], semkey=("xt", xb))
            emit_hT(t, xt[:, xb, :], ("xt", xb), 0)
        stage("S2")
        for l in range(n_layers):
            need_ctx = l < DEPTH - 1
            last = l == n_layers - 1
            queue_gate(l)
            stage("S3")
            if not last:
                queue_mod(l + 1)
            ada_bank[0] = 6 if l % 2 == 0 else 0
            if l % 2 == 0:
                wo = gla_layer(l, l // 2)
            else:
                wo = na_layer(l, l // 2, need_ctx)
            bg_flush()
            stage("S8")
            outproj(l, need_ctx, last, wo)
            if not last:
                barrier()

    try:
        main()
    except _Stop:
        P.dma("sp", out[0:128, :], xt[:, 0, :], reads=[("xt", 0)], writes=[("outd", 0)], semkey=("xst", 0))
    P.finish()
    return nc, P


_CACHE = {}


def prep_inputs(H, x, c, ctx, c_ctx, ada_w, ada_b, ln_g, ln_b, w_out, gla_w_in, gla_dec_w1, gla_dec_w2,
                gla_dec_b, gla_norm_g, na_w_in, na_rpb):
    f = lambda a: np.ascontiguousarray(np.asarray(a, dtype=np.float32))
    B = x.shape[0]
    cst, rope = make_consts(H)
    natab = make_natab(f(na_rpb), H)
    adab = f(np.asarray(ada_b).reshape(DEPTH, 24, 128).transpose(2, 0, 1).reshape(128, DEPTH * 24))
    shared = dict(ada_w=f(ada_w), adab=adab, ln_g=f(ln_g), ln_b=f(ln_b), w_out=f(w_out), gla_w_in=f(gla_w_in),
                  gla_dec_w1=f(gla_dec_w1), gla_dec_w2=f(gla_dec_w2), gla_dec_b=f(gla_dec_b),
                  gla_norm_g=f(gla_norm_g), na_w_in=f(na_w_in), natab=natab, cst=cst, rope=rope)
    maps = []
    for b in range(B):
        m = dict(shared)
        m["xin"] = f(np.concatenate([np.asarray(ctx[b]), np.asarray(x[b])], 0))
        cc = np.stack([np.asarray(c[b]).reshape(KC, 128).T, np.asarray(c_ctx).reshape(KC, 128).T], -1)
        m["cc"] = f(cc)
        maps.append(m)
    return maps


def kernel(**inputs):
    H = 32
    if "nc" not in _CACHE:
        _CACHE["nc"] = build(H, DEPTH)[0]
    nc = _CACHE["nc"]
    maps = prep_inputs(H, **inputs)
    res = run_bass_kernel_spmd(nc, maps, core_ids=list(range(len(maps))))
    return np.stack([np.asarray(r["out"], dtype=np.float32) for r in res.results], 0)
```
